# Optimizing a Trainium2 kernel written in Bass

```python
import math
import jax
import jax.numpy as jnp
from jax import lax
import numpy as np

D_MODEL = 1024
BATCH = 2
SEQ = 8192
DEPTH = 4
DEC_BATCH = 128
DEC_SEQ = 1
PAST_LEN = 8192
PAGE_SIZE = 128

M_HEADDIM = 64
M_HEADS = D_MODEL // M_HEADDIM
M_INNER = M_HEADS * M_HEADDIM
M_GROUPS = 4
M_STATE = 128
CONV_W = 4
CONV_DIM = M_INNER + 2 * M_GROUPS * M_STATE
CHUNK = 128
DT_MIN = 0.001
DT_MAX = 0.1
S5_WIDTH = D_MODEL
S5_GSIZE = 16
S5_GROUPS = S5_WIDTH // S5_GSIZE
S5_STATE = 64
HEAD_DIM = 64
A_HEADS = D_MODEL // HEAD_DIM
KV_HEADS = 4
Q_PER_KV = A_HEADS // KV_HEADS
ROT_DIM = HEAD_DIM // 4
ROPE_THETA = 500000.0
WINDOW = 128
D_FF = 4 * D_MODEL
N_BRANCH = 3
EPS = 1e-6
WIN_BUF = min(WINDOW, PAST_LEN)

OFF_Z = 0
OFF_XBC = OFF_Z + M_INNER
OFF_DT = OFF_XBC + CONV_DIM
OFF_U = OFF_DT + M_HEADS
OFF_Q = OFF_U + S5_WIDTH
OFF_K = OFF_Q + A_HEADS * HEAD_DIM
OFF_V = OFF_K + KV_HEADS * HEAD_DIM
OFF_G = OFF_V + KV_HEADS * HEAD_DIM
IN_COLS = OFF_G + N_BRANCH * D_MODEL

kernel_name = 'hybrid_ssd_s5_swa_decoder_step'


def _rmsnorm(x, w):
    xf = x.astype(jnp.float32)
    y = xf * lax.rsqrt(jnp.mean(xf * xf, axis=-1, keepdims=True) + EPS)
    return (y * w.astype(jnp.float32)).astype(x.dtype)


def _rope(x, pos):
    half = ROT_DIM // 2
    inv_freq = jnp.exp(-(2.0 * jnp.arange(half, dtype=jnp.float32) / ROT_DIM) * math.log(ROPE_THETA))
    ang = pos.astype(jnp.float32)[:, None] * inv_freq[None, :]
    cos = jnp.cos(ang)[None, :, None, :]
    sin = jnp.sin(ang)[None, :, None, :]
    xf = x.astype(jnp.float32)
    x1 = xf[..., :half]
    x2 = xf[..., half:ROT_DIM]
    out = jnp.concatenate([x1 * cos - x2 * sin, x2 * cos + x1 * sin, xf[..., ROT_DIM:]], axis=-1)
    return out.astype(x.dtype)


def _causal_conv(xbc, buf, conv_w, conv_b):
    full = jnp.concatenate([buf.astype(xbc.dtype), xbc], axis=1)
    out = lax.conv_general_dilated(full, conv_w[:, None, :].astype(xbc.dtype), window_strides=(1,),
                                   padding='VALID', dimension_numbers=('NWC', 'WIO', 'NWC'),
                                   feature_group_count=CONV_DIM)
    return jax.nn.silu(out + conv_b.astype(xbc.dtype)), full[:, -(CONV_W - 1):]


def _ssd(x, dt, a_log, bm, cm, d_skip, h0):
    bsz, L = x.shape[0], x.shape[1]
    q = CHUNK if L % CHUNK == 0 else L
    nc = L // q
    hpg = M_HEADS // M_GROUPS
    A = -jnp.exp(a_log).reshape(M_GROUPS, hpg)
    xc = x.reshape(bsz, nc, q, M_GROUPS, hpg, M_HEADDIM)
    dtc = dt.reshape(bsz, nc, q, M_GROUPS, hpg)
    bc = bm.reshape(bsz, nc, q, M_GROUPS, M_STATE)
    cc = cm.reshape(bsz, nc, q, M_GROUPS, M_STATE)
    acum = jnp.cumsum(dtc * A, axis=2)
    acum_h = jnp.moveaxis(acum, 2, -1)
    seg = acum_h[..., :, None] - acum_h[..., None, :]
    causal = jnp.tril(jnp.ones((q, q), dtype=bool))
    decay = jnp.exp(jnp.where(causal, seg, -jnp.inf))
    cb = jnp.einsum('bctgn,bcsgn->bcgts', cc, bc)
    xdt = xc * dtc[..., None]
    y_diag = jnp.einsum('bcghts,bcsghp->bctghp', cb[:, :, :, None] * decay, xdt)
    decay_end = jnp.exp(acum[:, :, -1:] - acum)
    states = jnp.einsum('bcsgn,bcsgh,bcsghp->bcghpn', bc, decay_end * dtc, xc)
    chunk_decay = jnp.exp(acum[:, :, -1])

    def step(h, inp):
        st, dec = inp
        return h * dec[..., None, None] + st, h

    h0g = h0.reshape(bsz, M_GROUPS, hpg, M_HEADDIM, M_STATE)
    h_last, h_prev = lax.scan(step, h0g, (jnp.moveaxis(states, 1, 0), jnp.moveaxis(chunk_decay, 1, 0)))
    h_prev = jnp.moveaxis(h_prev, 0, 1)
    y_off = jnp.einsum('bctgn,bcghpn,bctgh->bctghp', cc, h_prev, jnp.exp(acum))
    y = y_diag + y_off + xc * d_skip.reshape(M_GROUPS, hpg)[:, :, None]
    return y.reshape(bsz, L, M_HEADS, M_HEADDIM), h_last.reshape(bsz, M_HEADS, M_HEADDIM, M_STATE)


def _mamba2(z, xbc, dt_raw, conv_buf, h0, conv_w, conv_b, dt_bias, a_log, m_d, m_norm_w):
    f32 = jnp.float32
    bsz, L = z.shape[0], z.shape[1]
    xbc_c, conv_new = _causal_conv(xbc, conv_buf, conv_w, conv_b)
    xbc_c = xbc_c.astype(f32)
    gn = M_GROUPS * M_STATE
    xs = xbc_c[..., :M_INNER].reshape(bsz, L, M_HEADS, M_HEADDIM)
    bm = xbc_c[..., M_INNER:M_INNER + gn].reshape(bsz, L, M_GROUPS, M_STATE)
    cm = xbc_c[..., M_INNER + gn:].reshape(bsz, L, M_GROUPS, M_STATE)
    dt = jax.nn.softplus(dt_raw.astype(f32) + dt_bias.astype(f32))
    y, h_last = _ssd(xs, dt, a_log.astype(f32), bm, cm, m_d.astype(f32), h0.astype(f32))
    y = y.reshape(bsz, L, M_INNER) * jax.nn.silu(z.astype(f32))
    yg = y.reshape(bsz, L, M_GROUPS, M_INNER // M_GROUPS)
    yg = yg * lax.rsqrt(jnp.mean(yg * yg, axis=-1, keepdims=True) + EPS)
    y = yg.reshape(bsz, L, M_INNER) * m_norm_w.astype(f32)
    return y.astype(z.dtype), conv_new, h_last.astype(h0.dtype)


def _s5(u, h0_re, h0_im, lam_re, lam_im, log_step, b_re, b_im, c_re, c_im, d_skip):
    f32 = jnp.float32
    bsz, L = u.shape[0], u.shape[1]
    uf = u.astype(f32).reshape(bsz, L, S5_GROUPS, S5_GSIZE)
    step = jnp.exp(log_step.astype(f32))[:, None]
    lr = lam_re.astype(f32)
    li = lam_im.astype(f32)
    mag = jnp.exp(lr * step)
    ab_re = mag * jnp.cos(li * step)
    ab_im = mag * jnp.sin(li * step)
    den = lr * lr + li * li
    nr = ab_re - 1.0
    ni = ab_im
    f_re = (nr * lr + ni * li) / den
    f_im = (ni * lr - nr * li) / den
    br = b_re.astype(f32)
    bi = b_im.astype(f32)
    bb_re = f_re[..., None] * br - f_im[..., None] * bi
    bb_im = f_re[..., None] * bi + f_im[..., None] * br
    bu_re = jnp.einsum('blgi,gni->blgn', uf, bb_re)
    bu_im = jnp.einsum('blgi,gni->blgn', uf, bb_im)
    a_re = jnp.broadcast_to(ab_re[None, None], (1, L, S5_GROUPS, S5_STATE))
    a_im = jnp.broadcast_to(ab_im[None, None], (1, L, S5_GROUPS, S5_STATE))

    def combine(left, right):
        ar1, ai1, br1, bi1 = left
        ar2, ai2, br2, bi2 = right
        return (ar2 * ar1 - ai2 * ai1, ar2 * ai1 + ai2 * ar1,
                ar2 * br1 - ai2 * bi1 + br2, ar2 * bi1 + ai2 * br1 + bi2)

    p_re, p_im, s_re, s_im = lax.associative_scan(combine, (a_re, a_im, bu_re, bu_im), axis=1)
    hr0 = h0_re.astype(f32)[:, None]
    hi0 = h0_im.astype(f32)[:, None]
    h_re = p_re * hr0 - p_im * hi0 + s_re
    h_im = p_re * hi0 + p_im * hr0 + s_im
    y = (jnp.einsum('gon,blgn->blgo', c_re.astype(f32), h_re)
         - jnp.einsum('gon,blgn->blgo', c_im.astype(f32), h_im))
    y = y + d_skip.astype(f32).reshape(S5_GROUPS, S5_GSIZE) * uf
    y = jax.nn.gelu(y).reshape(bsz, L, S5_WIDTH)
    return y.astype(u.dtype), h_re[:, -1].astype(h0_re.dtype), h_im[:, -1].astype(h0_im.dtype)


def _sink_attend(q, k, v, q_pos, k_pos, sinks):
    s = jnp.einsum('bntkgd,bnskd->bnkgts', q.astype(jnp.float32), k.astype(jnp.float32)) * (HEAD_DIM ** -0.5)
    dpos = q_pos[:, :, None] - k_pos[:, None, :]
    valid = (dpos >= 0) & (dpos <= WINDOW) & (k_pos[:, None, :] >= 0)
    s = jnp.where(valid[None, :, None, None], s, -jnp.inf)
    sink = sinks.astype(jnp.float32).reshape(KV_HEADS, Q_PER_KV)[None, None, :, :, None, None]
    m = jnp.maximum(jnp.max(s, axis=-1, keepdims=True), sink)
    p = jnp.exp(s - m)
    denom = jnp.sum(p, axis=-1, keepdims=True) + jnp.exp(sink - m)
    return jnp.einsum('bnkgts,bnskd->bntkgd', p / denom, v.astype(jnp.float32))


def _attn_prompt(q, k, v, sinks):
    bsz, L = q.shape[0], q.shape[1]
    nb = L // WINDOW
    qb = q.reshape(bsz, nb, WINDOW, KV_HEADS, Q_PER_KV, HEAD_DIM)
    kb = k.reshape(bsz, nb, WINDOW, KV_HEADS, HEAD_DIM)
    vb = v.reshape(bsz, nb, WINDOW, KV_HEADS, HEAD_DIM)
    pad = jnp.zeros_like(kb[:, :1])
    k_ctx = jnp.concatenate([jnp.concatenate([pad, kb[:, :-1]], axis=1), kb], axis=2)
    v_ctx = jnp.concatenate([jnp.concatenate([pad, vb[:, :-1]], axis=1), vb], axis=2)
    q_pos = jnp.arange(L, dtype=jnp.int32).reshape(nb, WINDOW)
    k_pos = (jnp.arange(nb, dtype=jnp.int32)[:, None] - 1) * WINDOW + jnp.arange(2 * WINDOW, dtype=jnp.int32)[None, :]
    o = _sink_attend(qb, k_ctx, v_ctx, q_pos, k_pos, sinks)
    return o.reshape(bsz, L, A_HEADS * HEAD_DIM).astype(q.dtype)


def _attn_sample(q, k, v, k_buf, v_buf, sinks, pos0):
    bsz, T = q.shape[0], q.shape[1]
    k_all = jnp.concatenate([k_buf.astype(k.dtype), k], axis=1)
    v_all = jnp.concatenate([v_buf.astype(v.dtype), v], axis=1)
    qb = q.reshape(bsz, 1, T, KV_HEADS, Q_PER_KV, HEAD_DIM)
    q_pos = (pos0 + jnp.arange(T, dtype=jnp.int32))[None]
    k_pos = (pos0 - WIN_BUF + jnp.arange(WIN_BUF + T, dtype=jnp.int32))[None]
    o = _sink_attend(qb, k_all[:, None], v_all[:, None], q_pos, k_pos, sinks)
    return (o.reshape(bsz, T, A_HEADS * HEAD_DIM).astype(q.dtype),
            k_all[:, -WIN_BUF:], v_all[:, -WIN_BUF:])


def _layer(x, pos0, conv_buf, ssm_h0, s5_h0r, s5_h0i, k_buf, v_buf,
           norm1_w, w_in, conv_w, conv_b, dt_bias, a_log, m_d, m_norm_w, m_proj,
           s5_lam_re, s5_lam_im, s5_log_step, s5_b_re, s5_b_im, s5_c_re, s5_c_im, s5_d, s5_glu_w,
           attn_sinks, attn_o, w_out, norm2_w, mlp_up, mlp_down):
    bsz, L, _ = x.shape
    dty = x.dtype
    h = _rmsnorm(x, norm1_w)
    proj = h @ w_in.astype(dty)
    z = proj[..., OFF_Z:OFF_XBC]
    xbc = proj[..., OFF_XBC:OFF_DT]
    dt_raw = proj[..., OFF_DT:OFF_U]
    u = proj[..., OFF_U:OFF_Q]
    q_raw = proj[..., OFF_Q:OFF_K]
    k_raw = proj[..., OFF_K:OFF_V]
    v_raw = proj[..., OFF_V:OFF_G]
    g_pre = proj[..., OFF_G:]

    y_m, conv_new, ssm_new = _mamba2(z, xbc, dt_raw, conv_buf, ssm_h0, conv_w, conv_b,
                                     dt_bias, a_log, m_d, m_norm_w)
    y_m = y_m @ m_proj.astype(dty)

    y_s, s5r_new, s5i_new = _s5(u, s5_h0r, s5_h0i, s5_lam_re, s5_lam_im, s5_log_step,
                                s5_b_re, s5_b_im, s5_c_re, s5_c_im, s5_d)
    glu = y_s @ s5_glu_w.astype(dty)
    y_s = glu[..., :D_MODEL] * jax.nn.sigmoid(glu[..., D_MODEL:])

    pos = pos0 + jnp.arange(L, dtype=jnp.int32)
    q = _rope(q_raw.reshape(bsz, L, A_HEADS, HEAD_DIM), pos)
    k = _rope(k_raw.reshape(bsz, L, KV_HEADS, HEAD_DIM), pos)
    v = v_raw.reshape(bsz, L, KV_HEADS, HEAD_DIM)
    if k_buf is None:
        o = _attn_prompt(q, k, v, attn_sinks)
        k_new = k[:, -WIN_BUF:]
        v_new = v[:, -WIN_BUF:]
    else:
        o, k_new, v_new = _attn_sample(q, k, v, k_buf, v_buf, attn_sinks, pos0)
    y_a = o @ attn_o.astype(dty)

    gates = jax.nn.sigmoid(g_pre.astype(jnp.float32)).astype(dty).reshape(bsz, L, N_BRANCH, D_MODEL)
    merged = gates[..., 0, :] * y_m + gates[..., 1, :] * y_s + gates[..., 2, :] * y_a
    x = x + merged @ w_out.astype(dty)

    h2 = _rmsnorm(x, norm2_w)
    x = x + jnp.square(jax.nn.relu(h2 @ mlp_up.astype(dty))) @ mlp_down.astype(dty)
    return x, conv_new, ssm_new, s5r_new, s5i_new, k_new, v_new


def setup_inputs(seed: int = 0) -> dict:
    key = jax.random.key(seed)
    ks = jax.random.split(key, 40)
    f32 = jnp.float32

    def nrm(k, shape, scale):
        return jax.random.normal(k, shape, f32) * scale

    x_prompt = nrm(ks[0], (BATCH, SEQ, D_MODEL), 1.0)
    x_sample = nrm(ks[1], (DEC_BATCH, DEC_SEQ, D_MODEL), 1.0)
    state_ssm = nrm(ks[2], (DEPTH, DEC_BATCH, M_HEADS, M_HEADDIM, M_STATE), 0.5)
    state_conv = nrm(ks[3], (DEPTH, DEC_BATCH, CONV_W - 1, CONV_DIM), 1.0)
    state_s5_re = nrm(ks[4], (DEPTH, DEC_BATCH, S5_GROUPS, S5_STATE), 0.1)
    state_s5_im = nrm(ks[5], (DEPTH, DEC_BATCH, S5_GROUPS, S5_STATE), 0.1)
    cache_k = nrm(ks[6], (DEPTH, DEC_BATCH, WIN_BUF, KV_HEADS, HEAD_DIM), 1.0)
    cache_v = nrm(ks[7], (DEPTH, DEC_BATCH, WIN_BUF, KV_HEADS, HEAD_DIM), 1.0)

    norm1_w = 1.0 + nrm(ks[8], (DEPTH, D_MODEL), 0.01)
    w_in = nrm(ks[9], (DEPTH, D_MODEL, IN_COLS), D_MODEL ** -0.5)
    conv_w = nrm(ks[10], (DEPTH, CONV_W, CONV_DIM), CONV_W ** -0.5)
    conv_b = nrm(ks[11], (DEPTH, CONV_DIM), 0.01)
    dt0 = jnp.exp(jax.random.uniform(ks[12], (DEPTH, M_HEADS), f32, math.log(DT_MIN), math.log(DT_MAX)))
    dt_bias = dt0 + jnp.log(-jnp.expm1(-dt0))
    a_log = jnp.log(jax.random.uniform(ks[13], (DEPTH, M_HEADS), f32, 1.0, 16.0))
    m_d = 1.0 + nrm(ks[14], (DEPTH, M_HEADS), 0.01)
    m_norm_w = 1.0 + nrm(ks[15], (DEPTH, M_INNER), 0.01)
    m_proj = nrm(ks[16], (DEPTH, M_INNER, D_MODEL), M_INNER ** -0.5)

    s5_lam_re = -0.5 + nrm(ks[17], (DEPTH, S5_GROUPS, S5_STATE), 0.01)
    s5_lam_im = math.pi * jnp.arange(S5_STATE, dtype=f32) + nrm(ks[18], (DEPTH, S5_GROUPS, S5_STATE), 0.01)
    s5_log_step = jax.random.uniform(ks[19], (DEPTH, S5_GROUPS), f32, math.log(DT_MIN), math.log(DT_MAX))
    s5_b_re = nrm(ks[20], (DEPTH, S5_GROUPS, S5_STATE, S5_GSIZE), (2 * S5_GSIZE) ** -0.5)
    s5_b_im = nrm(ks[21], (DEPTH, S5_GROUPS, S5_STATE, S5_GSIZE), (2 * S5_GSIZE) ** -0.5)
    s5_c_re = nrm(ks[22], (DEPTH, S5_GROUPS, S5_GSIZE, S5_STATE), S5_STATE ** -0.5)
    s5_c_im = nrm(ks[23], (DEPTH, S5_GROUPS, S5_GSIZE, S5_STATE), S5_STATE ** -0.5)
    s5_d = nrm(ks[24], (DEPTH, S5_WIDTH), 1.0)
    s5_glu_w = nrm(ks[25], (DEPTH, S5_WIDTH, 2 * D_MODEL), S5_WIDTH ** -0.5)

    attn_sinks = nrm(ks[26], (DEPTH, A_HEADS), 1.0)
    attn_o = nrm(ks[27], (DEPTH, A_HEADS * HEAD_DIM, D_MODEL), (A_HEADS * HEAD_DIM) ** -0.5)
    w_out = nrm(ks[28], (DEPTH, D_MODEL, D_MODEL), D_MODEL ** -0.5)
    norm2_w = 1.0 + nrm(ks[29], (DEPTH, D_MODEL), 0.01)
    mlp_up = nrm(ks[30], (DEPTH, D_MODEL, D_FF), D_MODEL ** -0.5)
    mlp_down = nrm(ks[31], (DEPTH, D_FF, D_MODEL), D_FF ** -0.5)
    final_norm_w = 1.0 + nrm(ks[32], (D_MODEL,), 0.01)

    return {
        'x_prompt': x_prompt, 'x_sample': x_sample,
        'state_ssm': state_ssm, 'state_conv': state_conv,
        'state_s5_re': state_s5_re, 'state_s5_im': state_s5_im,
        'cache_k': cache_k, 'cache_v': cache_v,
        'norm1_w': norm1_w, 'w_in': w_in, 'conv_w': conv_w, 'conv_b': conv_b,
        'dt_bias': dt_bias, 'a_log': a_log, 'm_d': m_d, 'm_norm_w': m_norm_w, 'm_proj': m_proj,
        's5_lam_re': s5_lam_re, 's5_lam_im': s5_lam_im, 's5_log_step': s5_log_step,
        's5_b_re': s5_b_re, 's5_b_im': s5_b_im, 's5_c_re': s5_c_re, 's5_c_im': s5_c_im,
        's5_d': s5_d, 's5_glu_w': s5_glu_w,
        'attn_sinks': attn_sinks, 'attn_o': attn_o, 'w_out': w_out,
        'norm2_w': norm2_w, 'mlp_up': mlp_up, 'mlp_down': mlp_down,
        'final_norm_w': final_norm_w,
    }


def reference(x_prompt, x_sample, state_ssm, state_conv, state_s5_re, state_s5_im, cache_k, cache_v,
              norm1_w, w_in, conv_w, conv_b, dt_bias, a_log, m_d, m_norm_w, m_proj,
              s5_lam_re, s5_lam_im, s5_log_step, s5_b_re, s5_b_im, s5_c_re, s5_c_im, s5_d, s5_glu_w,
              attn_sinks, attn_o, w_out, norm2_w, mlp_up, mlp_down, final_norm_w):
    xp = x_prompt
    xs = x_sample
    bp = xp.shape[0]
    dty = xp.dtype
    conv0 = jnp.zeros((bp, CONV_W - 1, CONV_DIM), dty)
    ssm0 = jnp.zeros((bp, M_HEADS, M_HEADDIM, M_STATE), dty)
    s50 = jnp.zeros((bp, S5_GROUPS, S5_STATE), dty)
    conv_p, conv_s, ssm_p, ssm_s = [], [], [], []
    s5re_p, s5re_s, s5im_p, s5im_s = [], [], [], []
    k_p, k_s, v_p, v_s = [], [], [], []
    for l in range(DEPTH):
        lp = (norm1_w[l], w_in[l], conv_w[l], conv_b[l], dt_bias[l], a_log[l], m_d[l], m_norm_w[l], m_proj[l],
              s5_lam_re[l], s5_lam_im[l], s5_log_step[l], s5_b_re[l], s5_b_im[l], s5_c_re[l], s5_c_im[l],
              s5_d[l], s5_glu_w[l], attn_sinks[l], attn_o[l], w_out[l], norm2_w[l], mlp_up[l], mlp_down[l])
        xp, c_n, h_n, r_n, i_n, kk, vv = _layer(xp, 0, conv0, ssm0, s50, s50, None, None, *lp)
        conv_p.append(c_n)
        ssm_p.append(h_n)
        s5re_p.append(r_n)
        s5im_p.append(i_n)
        k_p.append(kk)
        v_p.append(vv)
        xs, c_n, h_n, r_n, i_n, kk, vv = _layer(xs, PAST_LEN, state_conv[l], state_ssm[l], state_s5_re[l],
                                                state_s5_im[l], cache_k[l], cache_v[l], *lp)
        conv_s.append(c_n)
        ssm_s.append(h_n)
        s5re_s.append(r_n)
        s5im_s.append(i_n)
        k_s.append(kk)
        v_s.append(vv)
    y_prompt = _rmsnorm(xp, final_norm_w)
    y_sample = _rmsnorm(xs, final_norm_w)
    return (y_prompt, y_sample,
            jnp.stack(ssm_p), jnp.stack(ssm_s), jnp.stack(conv_p), jnp.stack(conv_s),
            jnp.stack(s5re_p), jnp.stack(s5re_s), jnp.stack(s5im_p), jnp.stack(s5im_s),
            jnp.stack(k_p), jnp.stack(k_s), jnp.stack(v_p), jnp.stack(v_s))
```

```python
import math
from contextlib import ExitStack
import numpy as np
import concourse.bass as bass
import concourse.mybir as mybir
from concourse.bass_utils import run_bass_kernel_spmd

F32 = mybir.dt.float32
BF16 = mybir.dt.bfloat16
AF = mybir.ActivationFunctionType
OP = mybir.AluOpType
AX = mybir.AxisListType

D = 1024
KT = 8
MH = 16
HP = 64
NG = 4
NS = 128
CONV_DIM = 2048
S5G = 64
S5N = 64
AH = 16
KVH = 4
HD = 64
DFF = 4096
IN_COLS = 8720
OFF_Z, OFF_XBC, OFF_DT, OFF_U, OFF_Q, OFF_K, OFF_G = 0, 1024, 3072, 3088, 4112, 5136, 5648
EPS = 1e-6
PAST_LEN = 8192
NEGBIG = -30000.0
TS5 = 32


class Cfg:
    def __init__(self, SEQ=8192, DEC_BATCH=128, DEPTH=4, NCORES=8, NCH=2):
        self.SEQ, self.DEC_BATCH, self.DEPTH, self.NCORES, self.NCH = SEQ, DEC_BATCH, DEPTH, NCORES, NCH
        self.NPC = SEQ // 128
        self.SPC = DEC_BATCH // NCORES
        assert self.NPC % NCH == 0 and self.SPC % NCH == 0
        self.NTOK = SEQ + self.SPC


class Tracker:
    def __init__(self, nc):
        self.nc = nc
        self.eng = {'pe': nc.tensor, 'dve': nc.vector, 'act': nc.scalar, 'pool': nc.gpsimd, 'sp': nc.sync}
        self.sem = {k: nc.alloc_semaphore('s_' + k) for k in self.eng}
        self.cnt = {k: 0 for k in self.sem}
        self.waited = {k: {} for k in self.eng}
        self.lastw = {}
        self.readers = {}
        self.ninst = 0
        self.bank = {}

    def _semkey(self, key):
        if key not in self.sem:
            self.sem[key] = self.nc.alloc_semaphore('s_' + key)
            self.cnt[key] = 0
        return self.sem[key]

    def _wait(self, e, key, val):
        if self.waited[e].get(key, 0) >= val:
            return
        self.eng[e].wait_ge(self.sem[key], val)
        self.waited[e][key] = val
        self.ninst += 1

    def deps(self, e, reads, writes):
        need = {}
        for r in reads:
            lw = self.lastw.get(r)
            if lw is not None:
                need[lw[0]] = max(need.get(lw[0], 0), lw[1])
        for w in writes:
            lw = self.lastw.get(w)
            if lw is not None:
                need[lw[0]] = max(need.get(lw[0], 0), lw[1])
            for k, v in self.readers.get(w, {}).items():
                need[k] = max(need.get(k, 0), v)
        if e != 'sp':
            for r in list(reads) + list(writes):
                if isinstance(r, str) and r.startswith('ps'):
                    bank = int(r[2]) if r[2].isdigit() else 7
                    for k, v in self.bank.setdefault(bank, {}).items():
                        if k != e:
                            need[k] = max(need.get(k, 0), v)
        for key, val in need.items():
            if key == e and e in ('pe', 'sp'):
                continue
            self._wait(e, key, val)

    def op(self, e, fn, reads=(), writes=()):
        self.deps(e, reads, writes)
        inst = fn(self.eng[e])
        self.cnt[e] += 1
        inst.then_inc(self.sem[e], 1)
        v = self.cnt[e]
        self.ninst += 1
        for r in list(reads) + list(writes):
            if isinstance(r, str) and r.startswith('ps'):
                bank = int(r[2]) if r[2].isdigit() else 7
                self.bank.setdefault(bank, {})[e] = v
        for r in reads:
            self.readers.setdefault(r, {})[e] = v
        for w in writes:
            self.lastw[w] = (e, v)
            self.readers[w] = {}

    def dma(self, tag, out, in_, reads=(), writes=(), q='sp'):
        key = 'd_' + tag
        self._semkey(key)
        self.deps(q, reads, writes)
        inst = self.eng[q].dma_start(out=out, in_=in_)
        self.cnt[key] += 16
        inst.then_inc(self.sem[key], 16)
        v = self.cnt[key]
        self.ninst += 1
        for r in reads:
            self.readers.setdefault(r, {})[key] = v
        for w in writes:
            self.lastw[w] = (key, v)
            self.readers[w] = {}

    def barrier(self):
        engs = ['pe', 'dve', 'act', 'pool']
        for k in list(self.sem):
            if (k.startswith('d_') or k in engs) and self.cnt[k] > 0:
                self._wait('sp', k, self.cnt[k])
        inst = self.eng['sp'].nop()
        self.cnt['sp'] += 1
        inst.then_inc(self.sem['sp'], 1)
        self.ninst += 1
        for e in engs:
            self._wait(e, 'sp', self.cnt['sp'])

    def finish(self):
        for k in self.sem:
            if self.cnt[k] > 0 and k != 'sp':
                self._wait('sp', k, self.cnt[k])


def weight_blocks():
    blks = []

    def add(name, src, r0, c0, w, kind='k8'):
        blks.append(dict(name=name, src=src, r0=r0, c0=c0, w=w, kind=kind))
    add('z0', 'w_in', 0, OFF_Z, 512); add('z1', 'w_in', 0, OFF_Z + 512, 512)
    for i in range(4):
        add('xbc%d' % i, 'w_in', 0, OFF_XBC + 512 * i, 512)
    add('dt', 'w_in', 0, OFF_DT, 16)
    add('mp0', 'm_proj', 0, 0, 512); add('mp1', 'm_proj', 0, 512, 512)
    add('g0_0', 'w_in', 0, OFF_G, 512); add('g0_1', 'w_in', 0, OFF_G + 512, 512)
    add('u0', 'w_in', 0, OFF_U, 512); add('u1', 'w_in', 0, OFF_U + 512, 512)
    add('gv0', 's5_glu_w', 0, 0, 512); add('gg0', 's5_glu_w', 0, 1024, 512)
    add('gv1', 's5_glu_w', 0, 512, 512); add('gg1', 's5_glu_w', 0, 1536, 512)
    add('g1_0', 'w_in', 0, OFF_G + 1024, 512); add('g1_1', 'w_in', 0, OFF_G + 1536, 512)
    add('q0', 'w_in', 0, OFF_Q, 512); add('q1', 'w_in', 0, OFF_Q + 512, 512)
    add('kv', 'w_in', 0, OFF_K, 512)
    add('ao0', 'attn_o', 0, 0, 512); add('ao1', 'attn_o', 0, 512, 512)
    add('g2_0', 'w_in', 0, OFF_G + 2048, 512); add('g2_1', 'w_in', 0, OFF_G + 2560, 512)
    add('wo0', 'w_out', 0, 0, 512); add('wo1', 'w_out', 0, 512, 512)
    for i in range(8):
        add('up%d' % i, 'mlp_up', 0, 512 * i, 512)
    for i in range(8):
        add('dn%d' % i, 'mlp_down', 0, 128 * i, 128, kind='k32')
    return blks


WBLKS = weight_blocks()
NBLK = len(WBLKS)

PB_WR, PB_WI, PB_CS, PB_SN = 0, 2048, 4096, 6144
PB_BW = 8192
PB_CW = PB_BW + 8192
PB_SZ = PB_CW + 2048
PF_MULT = 0
PF_ROT = PF_MULT + 64 * 33
PF_SZ = PF_ROT + 6 * 64


def build(cfg):
    L, NCH, NPC, SPC, NTOK = cfg.DEPTH, cfg.NCH, cfg.NPC, cfg.SPC, cfg.NTOK
    N = NCH * 128
    nc = bass.Bass("TRN2", target_bir_lowering=False)
    T = Tracker(nc)

    def din(name, shape, dt=F32):
        return nc.dram_tensor(name, list(shape), dt, kind="ExternalInput").ap()

    def dout(name, shape):
        return nc.dram_tensor(name, list(shape), F32, kind="ExternalOutput").ap()

    xin = din("xin", [NTOK, D])
    Wd = dict(w_in=din("w_in", [L, D, IN_COLS]), m_proj=din("m_proj", [L, D, D]),
              s5_glu_w=din("s5_glu_w", [L, D, 2 * D]), attn_o=din("attn_o", [L, D, D]),
              w_out=din("w_out", [L, D, D]), mlp_up=din("mlp_up", [L, D, DFF]),
              mlp_down=din("mlp_down", [L, DFF, D]))
    colsd = din("cols", [L, 128, 32])
    convd = din("convp", [L, 128, 16, 5])
    rowsd = din("rows", [L, 4, 16])
    lamd = din("lam", [L, 128, 2, 64])
    lstepd = din("lstep", [L, 64])
    bwd = din("bw", [L, 128, 8 * 4 * 2 * 128])
    cwd = din("cw", [L, 128, 64 * 2 * 16])
    fnwd = din("fnw", [128, 8])
    st_ssm = din("st_ssm", [L, SPC, D, NS])
    st_conv = din("st_conv", [L, SPC, 3, CONV_DIM])
    st_s5 = din("st_s5", [L, SPC, 2, S5G, S5N])
    st_k = din("st_k", [L, SPC, 128, 256])
    st_v = din("st_v", [L, SPC, 128, 256])
    cF = din("cF", [128, 8, 128])
    cM = din("cM", [128, 2, 256])
    cR = din("cR", [128, NPC + 1, 16])
    cS = din("cS", [128, 4])

    o_y = dout("o_y", [NTOK, D])
    o_ssm_p = dout("o_ssm_p", [L, D, NS]); o_ssm_s = dout("o_ssm_s", [L, SPC, D, NS])
    o_conv_p = dout("o_conv_p", [L, 3, CONV_DIM]); o_conv_s = dout("o_conv_s", [L, SPC, 3, CONV_DIM])
    o_s5_p = dout("o_s5_p", [L, 2, S5G, S5N]); o_s5_s = dout("o_s5_s", [L, SPC, 2, S5G, S5N])
    o_k_p = dout("o_k_p", [L, 128, 256]); o_k_s = dout("o_k_s", [L, SPC, 128, 256])
    o_v_p = dout("o_v_p", [L, 128, 256]); o_v_s = dout("o_v_s", [L, SPC, 128, 256])

    wbf = nc.dram_tensor("wbf", [L, NBLK, 128, 4096], BF16, kind="Internal").ap()
    packB = nc.dram_tensor("packB", [L, 128, PB_SZ], BF16, kind="Internal").ap()
    packF = nc.dram_tensor("packF", [L, 128, PF_SZ], F32, kind="Internal").ap()

    _uq = [0]

    def uq(name):
        _uq[0] += 1
        return '%s_%d' % (name, _uq[0])

    def sb(name, shape, dt=F32):
        return nc.alloc_sbuf_tensor(name, list(shape), dt)

    DBG = getattr(cfg, 'DEBUG', False)
    STOP = getattr(cfg, 'STOP', 99)
    if DBG:
        o_dbg = dout('o_dbg', [(cfg.NPC + cfg.SPC) // NCH, L, 4, 128, KT * N])

    def dbg_dump(sc, l, k, tile):
        if DBG:
            T.dma('dbg', o_dbg[sc, l, k], tile[:].rearrange('p a b -> p (a b)'), reads=['ybr', 'xT'], writes=['o_dbg'])

    PS = [nc.alloc_psum_tensor("ps%d" % i, [128, 512], F32) for i in range(7)]
    PSB = nc.alloc_psum_tensor("psb", [128, 1024], BF16)
    PSN = ['ps%d' % i for i in range(7)]
    PS2B = PS[2][:].bitcast(BF16)

    cFt = sb("cFt", [128, 8, 128]); cMt = sb("cMt", [128, 2, 256]); cRt = sb("cRt", [128, NPC + 1, 16])
    cSt = sb("cSt", [128, 4]); fnw = sb("fnw_t", [128, 8])
    identB = sb("identB", [128, 128], BF16)
    onesB = sb("onesB", [128, 128], BF16)
    identF = cFt[:, 0, :]; triF = cFt[:, 1, :]; negF = cFt[:, 2, :]; onesF = cFt[:, 3, :]; permF = cFt[:, 4, :]
    colsT = sb("colsT", [128, L, 32]); convT = sb("convT", [128, L, 16, 5]); rowsT = sb("rowsT", [128, L, 4, 16])
    Abc = sb("Abc", [128, L, 16])
    Sst = sb("Sst", [128, L, D])
    histT = sb("histT", [128, L, 16, 3])
    s5c = sb("s5c", [128, L, 64])
    kcat = sb("kcat", [128, L, 4, 256], BF16)
    vcat = sb("vcat", [128, L, 2, 256], BF16)
    wr = [sb("wr%d" % i, [128, 4096], BF16) for i in range(4)]
    pkB = sb("pkB", [128, PB_SZ], BF16)
    pkF = sb("pkF", [128, PF_SZ])

    T.dma('c_cF', cFt[:], cF, writes=['cF'])
    T.dma('c_cM', cMt[:], cM, writes=['cM'])
    T.dma('c_cR', cRt[:], cR, writes=['cR'])
    T.dma('c_cS', cSt[:], cS, writes=['cS'])
    T.dma('c_fnw', fnw[:], fnwd, writes=['fnw'])
    for l in range(L):
        T.dma('c_cols', colsT[:, l, :], colsd[l], writes=['cols'])
        T.dma('c_conv', convT[:, l, :, :], convd[l], writes=['conv'])
        T.dma('c_rows', rowsT[:, l, :, :].rearrange("p a b -> p (a b)"),
              rowsd[l].rearrange("a b -> (a b)").partition_broadcast(128), writes=['rows'])
    T.op('dve', lambda e: e.tensor_copy(identB[:], identF), reads=['cF'], writes=['identB'])
    T.op('dve', lambda e: e.tensor_copy(onesB[:], onesF), reads=['cF'], writes=['onesB'])
    T.op('act', lambda e: e.activation(out=Abc[:], in_=rowsT[:, :, 1, :], func=AF.Exp), reads=['rows'], writes=['Abc'])
    T.op('dve', lambda e: e.tensor_scalar(out=Abc[:], in0=Abc[:], scalar1=-1.0, scalar2=None, op0=OP.mult),
         reads=['Abc'], writes=['Abc'])

    with ExitStack() as es:
        lamT = es.enter_context(nc.sbuf_tensor(uq("lamT"), [128, 2, 64], F32))
        stp = es.enter_context(nc.sbuf_tensor(uq("stp"), [128, 64], F32))
        CS = es.enter_context(nc.sbuf_tensor(uq("CS"), [128, 64, 33], F32))
        SN = es.enter_context(nc.sbuf_tensor(uq("SN"), [128, 64, 33], F32))
        tA = es.enter_context(nc.sbuf_tensor(uq("tA"), [128, 64, 32], F32))
        tB = es.enter_context(nc.sbuf_tensor(uq("tB"), [128, 64, 32], F32))
        sm = es.enter_context(nc.sbuf_tensor(uq("sm"), [128, 12, 64], F32))
        stg = es.enter_context(nc.sbuf_tensor(uq("stg"), [128, 4096], F32))
        for l in range(L):
            T.dma('pl_lam', lamT[:], lamd[l], writes=['lamT'])
            T.dma('pl_stp', stp[:], lstepd[l].partition_broadcast(128), writes=['stp'])
            lr = lamT[:, 0, :]; li = lamT[:, 1, :]
            th = sm[:, 0, :]; mag = sm[:, 1, :]; r = sm[:, 2, :]; m_ = sm[:, 3, :]
            c1 = sm[:, 4, :]; s1 = sm[:, 5, :]; fre = sm[:, 6, :]; fim = sm[:, 7, :]
            t0 = sm[:, 8, :]; t1 = sm[:, 9, :]; t2 = sm[:, 10, :]; t3 = sm[:, 11, :]
            V = lambda f, rd, wr_: T.op('dve', f, reads=rd, writes=wr_)
            A_ = lambda f, rd, wr_: T.op('act', f, reads=rd, writes=wr_)
            A_(lambda e: e.activation(out=stp[:], in_=stp[:], func=AF.Exp), ['stp'], ['stp'])
            V(lambda e: e.tensor_tensor(out=th, in0=li, in1=stp[:], op=OP.mult), ['lamT', 'stp'], ['sm'])
            V(lambda e: e.tensor_tensor(out=t0, in0=lr, in1=stp[:], op=OP.mult), ['lamT', 'stp'], ['sm'])
            A_(lambda e: e.activation(out=mag, in_=t0, func=AF.Exp), ['sm'], ['sm'])
            V(lambda e: e.tensor_copy(r, th), ['sm'], ['sm'])
            for _ in range(4):
                V(lambda e: e.tensor_scalar(out=m_, in0=r, scalar1=math.pi, scalar2=-2.0 * math.pi, op0=OP.is_gt, op1=OP.mult), ['sm'], ['sm'])
                V(lambda e: e.tensor_tensor(out=r, in0=r, in1=m_, op=OP.add), ['sm'], ['sm'])
            A_(lambda e: e.activation(out=s1, in_=r, func=AF.Sin), ['sm'], ['sm'])
            V(lambda e: e.tensor_scalar(out=t0, in0=r, scalar1=-1.0, scalar2=None, op0=OP.mult), ['sm'], ['sm'])
            V(lambda e: e.tensor_tensor(out=t0, in0=t0, in1=r, op=OP.max), ['sm'], ['sm'])
            V(lambda e: e.tensor_scalar(out=t0, in0=t0, scalar1=-1.0, scalar2=math.pi / 2, op0=OP.mult, op1=OP.add), ['sm'], ['sm'])
            A_(lambda e: e.activation(out=c1, in_=t0, func=AF.Sin), ['sm'], ['sm'])
            V(lambda e: e.memset(CS[:, :, 0:1], 1.0), [], ['CS'])
            V(lambda e: e.memset(SN[:, :, 0:1], 0.0), [], ['SN'])
            V(lambda e: e.tensor_copy(CS[:, :, 1], c1), ['sm'], ['CS'])
            V(lambda e: e.tensor_copy(SN[:, :, 1], s1), ['sm'], ['SN'])
            m = 1
            while m < 32:
                cm = CS[:, :, m:m + 1].to_broadcast([128, 64, m]); smm = SN[:, :, m:m + 1].to_broadcast([128, 64, m])
                a = tA[:, :, 0:m]; b = tB[:, :, 0:m]
                V(lambda e: e.tensor_tensor(out=a, in0=CS[:, :, 1:m + 1], in1=cm, op=OP.mult), ['CS'], ['tA'])
                V(lambda e: e.tensor_tensor(out=b, in0=SN[:, :, 1:m + 1], in1=smm, op=OP.mult), ['SN'], ['tB'])
                V(lambda e: e.tensor_tensor(out=CS[:, :, m + 1:2 * m + 1], in0=a, in1=b, op=OP.subtract), ['tA', 'tB'], ['CS'])
                V(lambda e: e.tensor_tensor(out=a, in0=SN[:, :, 1:m + 1], in1=cm, op=OP.mult), ['SN', 'CS'], ['tA'])
                V(lambda e: e.tensor_tensor(out=b, in0=CS[:, :, 1:m + 1], in1=smm, op=OP.mult), ['SN', 'CS'], ['tB'])
                V(lambda e: e.tensor_tensor(out=SN[:, :, m + 1:2 * m + 1], in0=a, in1=b, op=OP.add), ['tA', 'tB'], ['SN'])
                m *= 2
            V(lambda e: e.tensor_tensor(out=t0, in0=mag, in1=c1, op=OP.mult), ['sm'], ['sm'])
            V(lambda e: e.tensor_tensor(out=t1, in0=mag, in1=s1, op=OP.mult), ['sm'], ['sm'])
            V(lambda e: e.tensor_scalar(out=t0, in0=t0, scalar1=-1.0, scalar2=None, op0=OP.add), ['sm'], ['sm'])
            V(lambda e: e.tensor_tensor(out=t2, in0=lr, in1=lr, op=OP.mult), ['lamT', 'sm'], ['sm'])
            V(lambda e: e.tensor_tensor(out=t3, in0=li, in1=li, op=OP.mult), ['lamT', 'sm'], ['sm'])
            V(lambda e: e.tensor_tensor(out=t2, in0=t2, in1=t3, op=OP.add), ['sm'], ['sm'])
            V(lambda e: e.reciprocal(out=t2, in_=t2), ['sm'], ['sm'])
            V(lambda e: e.tensor_tensor(out=fre, in0=t0, in1=lr, op=OP.mult), ['sm', 'lamT'], ['sm'])
            V(lambda e: e.tensor_tensor(out=t3, in0=t1, in1=li, op=OP.mult), ['sm', 'lamT'], ['sm'])
            V(lambda e: e.tensor_tensor(out=fre, in0=fre, in1=t3, op=OP.add), ['sm'], ['sm'])
            V(lambda e: e.tensor_tensor(out=fre, in0=fre, in1=t2, op=OP.mult), ['sm'], ['sm'])
            V(lambda e: e.tensor_tensor(out=fim, in0=t1, in1=lr, op=OP.mult), ['sm', 'lamT'], ['sm'])
            V(lambda e: e.tensor_tensor(out=t3, in0=t0, in1=li, op=OP.mult), ['sm', 'lamT'], ['sm'])
            V(lambda e: e.tensor_tensor(out=fim, in0=fim, in1=t3, op=OP.subtract), ['sm'], ['sm'])
            V(lambda e: e.tensor_tensor(out=fim, in0=fim, in1=t2, op=OP.mult), ['sm'], ['sm'])
            frb = sm[:, 6, :].unsqueeze(2).to_broadcast([128, 64, 32]); fib = sm[:, 7, :].unsqueeze(2).to_broadcast([128, 64, 32])
            pk3 = lambda off: pkB[:, off:off + 2048].rearrange("p (g t) -> p g t", t=32)
            V(lambda e: e.tensor_tensor(out=tA[:], in0=CS[:, :, 0:32], in1=frb, op=OP.mult), ['CS', 'sm'], ['tA'])
            V(lambda e: e.tensor_tensor(out=tB[:], in0=SN[:, :, 0:32], in1=fib, op=OP.mult), ['SN', 'sm'], ['tB'])
            V(lambda e: e.tensor_tensor(out=pk3(PB_WR), in0=tA[:], in1=tB[:], op=OP.add), ['tA', 'tB'], ['pkB'])
            V(lambda e: e.tensor_tensor(out=tA[:], in0=CS[:, :, 0:32], in1=fib, op=OP.mult), ['CS', 'sm', 'pkB'], ['tA'])
            V(lambda e: e.tensor_tensor(out=tB[:], in0=SN[:, :, 0:32], in1=frb, op=OP.mult), ['SN', 'sm', 'pkB'], ['tB'])
            V(lambda e: e.tensor_tensor(out=tA[:], in0=tA[:], in1=tB[:], op=OP.subtract), ['tA', 'tB'], ['tA'])
            V(lambda e: e.tensor_scalar(out=pk3(PB_WI), in0=tA[:], scalar1=cSt[:, 1:2], scalar2=None, op0=OP.mult), ['tA', 'cS'], ['pkB'])
            V(lambda e: e.tensor_scalar(out=pk3(PB_CS), in0=CS[:, :, 0:32], scalar1=cSt[:, 2:3], scalar2=None, op0=OP.mult), ['CS', 'cS'], ['pkB'])
            V(lambda e: e.tensor_scalar(out=pk3(PB_SN), in0=SN[:, :, 0:32], scalar1=-1.0, scalar2=None, op0=OP.mult), ['SN'], ['pkB'])
            for hb in range(2):
                T.dma('pl_bw', stg[:], bwd[l, :, hb * 4096:(hb + 1) * 4096], writes=['stg'])
                V(lambda e: e.tensor_copy(pkB[:, PB_BW + hb * 4096:PB_BW + (hb + 1) * 4096], stg[:]), ['stg'], ['pkB'])
            T.dma('pl_bw', stg[:, 0:2048], cwd[l], reads=[], writes=['stg'])
            V(lambda e: e.tensor_copy(pkB[:, PB_CW:PB_CW + 2048], stg[:, 0:2048]), ['stg'], ['pkB'])
            mlt = pkF[:, PF_MULT:PF_MULT + 64 * 33].rearrange("p (g t) -> p g t", t=33)
            V(lambda e: e.memset(mlt[:, :, 0:1], 0.0), [], ['pkF'])
            V(lambda e: e.tensor_copy(mlt[:, :, 1:33], sm[:, 1, :].unsqueeze(2).to_broadcast([128, 64, 32])), ['sm'], ['pkF'])
            rot = pkF[:, PF_ROT:PF_ROT + 384].rearrange("p (a c g) -> p a c g", a=3, c=2)
            for ai, tt in enumerate((1, 31, 32)):
                V(lambda e: e.tensor_copy(rot[:, ai, 0, :], CS[:, :, tt]), ['CS'], ['pkF'])
                V(lambda e: e.tensor_copy(rot[:, ai, 1, :], SN[:, :, tt]), ['SN'], ['pkF'])
            T.dma('pl_stB', packB[l], pkB[:], reads=['pkB'], writes=['packB%d' % l])
            T.dma('pl_stF', packF[l], pkF[:], reads=['pkF'], writes=['packF%d' % l])
        T.barrier()

    with ExitStack() as es:
        wst0 = es.enter_context(nc.sbuf_tensor(uq("wst0"), [128, 4096], F32))
        wst1 = es.enter_context(nc.sbuf_tensor(uq("wst1"), [128, 4096], F32))
        wst = [wst0, wst1]
        i = 0
        for l in range(L):
            for bi, blk in enumerate(WBLKS):
                s = i % 2
                src = Wd[blk['src']]
                if blk['kind'] == 'k8':
                    w = blk['w']
                    sap = src[l, 0:D, blk['c0']:blk['c0'] + w].rearrange("(kt p) c -> p kt c", p=128)
                    tap = wst[s][:, 0:8 * w].rearrange("p (kt c) -> p kt c", c=w)
                    n_el = 8 * w
                else:
                    sap = src[l, 0:DFF, blk['c0']:blk['c0'] + 128].rearrange("(kt p) c -> p kt c", p=128)
                    tap = wst[s][:, :].rearrange("p (kt c) -> p kt c", c=128)
                    n_el = 4096
                T.dma('wst%d' % s, tap, sap, writes=['wst%d' % s])
                ce = ('dve', 'act', 'pool')[i % 3]
                if ce == 'act':
                    T.op('act', lambda e: e.copy(out=wr[s][:, 0:n_el], in_=wst[s][:, 0:n_el]), reads=['wst%d' % s], writes=['wr%d' % s])
                else:
                    T.op(ce, lambda e: e.tensor_copy(wr[s][:, 0:n_el], wst[s][:, 0:n_el]), reads=['wst%d' % s], writes=['wr%d' % s])
                T.dma('wcs%d' % s, wbf[l, bi, :, 0:n_el], wr[s][:, 0:n_el], reads=['wr%d' % s], writes=['wbf%d_%d' % (l, bi)])
                i += 1
        T.barrier()

    if STOP <= 0:
        T.finish()
        return nc, T
    xT = sb("xT", [128, KT, N])
    hT = sb("hT", [128, KT, N], BF16)
    mrg = sb("mrg", [128, KT, N])
    mbf = sb("mbf", [128, KT, N], BF16)
    ybr = sb("ybr", [128, KT, N])
    brf = sb("brf", [128, KT, N], BF16)
    rs = sb("rs", [128, N])
    gs = [sb("gs%d" % i, [128, N]) for i in range(2)]
    seq = []
    n_sc = (NPC + SPC) // NCH
    for sc in range(n_sc):
        for l in range(L):
            for bi in range(NBLK):
                seq.append((l, bi))
    wstate = dict(issued=0, cur=-1)

    def w_issue(upto):
        while wstate['issued'] <= min(upto, len(seq) - 1):
            j = wstate['issued']
            l, bi = seq[j]
            s = j % 4
            blk = WBLKS[bi]
            n_el = 8 * blk['w'] if blk['kind'] == 'k8' else 4096
            T.dma('wr%d' % s, wr[s][:, 0:n_el], wbf[l, bi, :, 0:n_el], reads=['wbf%d_%d' % (l, bi)], writes=['wr%d' % s])
            wstate['issued'] += 1

    def w_next(name, ahead=3):
        wstate['cur'] += 1
        j = wstate['cur']
        l, bi = seq[j]
        assert WBLKS[bi]['name'] == name, (WBLKS[bi]['name'], name)
        w_issue(j + ahead)
        s = j % 4
        blk = WBLKS[bi]
        if blk['kind'] == 'k8':
            return wr[s][:, 0:8 * blk['w']].rearrange("p (kt c) -> p kt c", c=blk['w']), 'wr%d' % s
        return wr[s][:, :].rearrange("p (kt c) -> p kt c", c=128), 'wr%d' % s

    pmi = [0]

    def pm_next():
        pmi[0] ^= 1
        return PS[pmi[0]], PSN[pmi[0]]

    pend = []
    bankrot = [0]

    def flush_pend():
        while pend:
            f = pend.pop(0)
            f()

    def proj_fm(wname, act, actres, nm, cb, lag=0, banks=None):
        wt, wres = w_next(wname)
        for m in range(nm):
            if banks is None:
                ps, psn = pm_next()
            else:
                bk = banks[bankrot[0] % len(banks)]
                bankrot[0] += 1
                ps, psn = PS[bk], PSN[bk]
            for kt in range(KT):
                T.op('pe', lambda e: e.matmul(ps[:, 0:N], lhsT=wt[:, kt, m * 128:(m + 1) * 128], rhs=act[:, kt, :],
                                              start=(kt == 0), stop=(kt == KT - 1)), reads=[wres, actres], writes=[psn])
            if lag:
                pend.append(lambda m=m, ps=ps, psn=psn: cb(m, ps, psn))
                while len(pend) > lag:
                    pend.pop(0)()
            else:
                cb(m, ps, psn)

    def proj_tm(wname, act, actres, w, cb):
        wt, wres = w_next(wname)
        for c in range(NCH):
            ps, psn = pm_next()
            for kt in range(KT):
                T.op('pe', lambda e: e.matmul(ps[:, 0:w], lhsT=act[:, kt, c * 128:(c + 1) * 128], rhs=wt[:, kt, 0:w],
                                              start=(kt == 0), stop=(kt == KT - 1)), reads=[wres, actres], writes=[psn])
            cb(c, ps, psn)

    def rmsnorm(wcol, out_t, outres):
        T.op('act', lambda e: e.activation(out=mbf[:], in_=xT[:], func=AF.Square), reads=['xT'], writes=['mbf'])
        ps, psn = pm_next()
        for kt in range(KT):
            T.op('pe', lambda e: e.matmul(ps[:, 0:N], lhsT=onesB[:], rhs=mbf[:, kt, :], start=(kt == 0), stop=(kt == KT - 1)),
                 reads=['onesB', 'mbf'], writes=[psn])
        T.op('act', lambda e: e.activation(out=rs[:], in_=ps[:, 0:N], func=AF.Sqrt, bias=EPS, scale=1.0 / D), reads=[psn], writes=['rs'])
        T.op('dve', lambda e: e.reciprocal(out=rs[:], in_=rs[:]), reads=['rs'], writes=['rs'])
        for kt in range(KT):
            T.op('dve', lambda e: e.scalar_tensor_tensor(out=out_t[:, kt, :], in0=xT[:, kt, :], scalar=wcol(kt), in1=rs[:],
                                                         op0=OP.mult, op1=OP.mult), reads=['xT', 'rs', 'cols', 'fnw'], writes=[outres])

    def gate_stage(l, bidx):
        for half in range(2):
            def cb(m, ps, psn, half=half):
                mb = half * 4 + m
                g = gs[mb % 2]; gn = 'gs%d' % (mb % 2)
                T.op('act', lambda e: e.activation(out=g[:], in_=ps[:, 0:N], func=AF.Sigmoid), reads=[psn], writes=[gn])
                if bidx == 0:
                    T.op('dve', lambda e: e.tensor_tensor(out=mrg[:, mb, :], in0=g[:], in1=ybr[:, mb, :], op=OP.mult),
                         reads=[gn, 'ybr'], writes=['mrg'])
                else:
                    T.op('dve', lambda e: e.tensor_tensor(out=g[:], in0=g[:], in1=ybr[:, mb, :], op=OP.mult), reads=[gn, 'ybr'], writes=[gn])
                    if bidx == 1:
                        T.op('dve', lambda e: e.tensor_tensor(out=mrg[:, mb, :], in0=mrg[:, mb, :], in1=g[:], op=OP.add),
                             reads=[gn, 'mrg'], writes=['mrg'])
                    else:
                        T.op('dve', lambda e: e.tensor_tensor(out=mbf[:, mb, :], in0=mrg[:, mb, :], in1=g[:], op=OP.add),
                             reads=[gn, 'mrg'], writes=['mbf'])
            proj_fm('g%d_%d' % (bidx, half), hT, 'hT', 4, cb)

    def branch_out(names, src, srcres):
        for half, nm in enumerate(names):
            def cb(m, ps, psn, half=half):
                T.op('act', lambda e: e.copy(out=ybr[:, half * 4 + m, :], in_=ps[:, 0:N]), reads=[psn], writes=['ybr'])
            proj_fm(nm, src, srcres, 4, cb)

    def V(f, rd=(), wr_=()):
        T.op('dve', f, reads=rd, writes=wr_)

    def A(f, rd=(), wr_=()):
        T.op('act', f, reads=rd, writes=wr_)

    def GP(f, rd=(), wr_=()):
        T.op('pool', f, reads=rd, writes=wr_)

    def PE(f, rd=(), wr_=()):
        T.op('pe', f, reads=rd, writes=wr_)

    rtmp = sb("rtmp", [128, 2, 64])

    def rot(dst, dstres, w_ap, wres, ai, ROT):
        PE(lambda e: e.matmul(PS[6][:, 256:320], lhsT=permF, rhs=w_ap, start=True, stop=True), [wres, 'cF'], ['ps6r'])
        V(lambda e: e.tensor_tensor(out=rtmp[:, 0, :], in0=w_ap, in1=ROT[:, ai, 0, :], op=OP.mult), [wres, 'pkF'], ['rtmp0'])
        V(lambda e: e.tensor_tensor(out=rtmp[:, 1, :], in0=PS[6][:, 256:320], in1=ROT[:, ai, 1, :], op=OP.mult), ['ps6r', 'pkF'], ['rtmp1'])
        V(lambda e: e.tensor_tensor(out=dst, in0=rtmp[:, 0, :], in1=rtmp[:, 1, :], op=OP.add), ['rtmp0', 'rtmp1'], [dstres])

    last_prompt_sc = NPC // NCH - 1

    for sc in range(n_sc):
        chunks = [sc * NCH + c for c in range(NCH)]
        fake = chunks[0] >= NPC
        cs_ = lambda c: slice(c * 128, (c + 1) * 128)
        with ExitStack() as es:
            xtm = es.enter_context(nc.sbuf_tensor(uq("xtm"), [128, D], F32))
            for c, gc in enumerate(chunks):
                if fake:
                    b = gc - NPC
                    V(lambda e: e.memset(xtm[:], 0.0), [], ['xtm'])
                    T.dma('xtm', xtm[0:1, :], xin[cfg.SEQ + b:cfg.SEQ + b + 1, :], writes=['xtm'])
                else:
                    T.dma('xtm', xtm[:], xin[gc * 128:(gc + 1) * 128, :], writes=['xtm'])
                for half in range(2):
                    ps, psn = PS[2 + half], PSN[2 + half]
                    for j in range(4):
                        jj = half * 4 + j
                        PE(lambda e: e.transpose(ps[:, j * 128:(j + 1) * 128], xtm[:, jj * 128:(jj + 1) * 128], identF), ['xtm', 'cF'], [psn])
                    A(lambda e: e.copy(out=xT[:, half * 4:half * 4 + 4, cs_(c)], in_=ps[:].rearrange("p (j t) -> p j t", t=128)), [psn], ['xT'])
            T.barrier()

        for l in range(L):
            T.dma('pkB', pkB[:], packB[l], reads=['packB%d' % l], writes=['pkB'])
            T.dma('pkF', pkF[:], packF[l], reads=['packF%d' % l], writes=['pkF'])
            tab = lambda off: pkB[:, off:off + 2048].rearrange("p (g t) -> p g t", t=32)
            WRt, WIt, CSt_, SNt = tab(PB_WR), tab(PB_WI), tab(PB_CS), tab(PB_SN)
            Bw = pkB[:, PB_BW:PB_BW + 8192].rearrange("p (kt e o n) -> p kt e o n", kt=8, e=4, o=2)
            Cw = pkB[:, PB_CW:PB_CW + 2048].rearrange("p (g o c) -> p g o c", g=64, o=2)
            MULT = pkF[:, PF_MULT:PF_MULT + 64 * 33]
            ROT = pkF[:, PF_ROT:PF_ROT + 384].rearrange("p (a c g) -> p a c g", a=3, c=2)
            n1 = lambda kt: colsT[:, l, kt:kt + 1]
            n2 = lambda kt: colsT[:, l, 8 + kt:9 + kt]
            S_l = Sst[:, l, :]
            Sres = 'S%d' % l

            rmsnorm(n1, hT, 'hT')
            if STOP <= 1:
                T.finish()
                return nc, T

            with ExitStack() as es:
                zs = es.enter_context(nc.sbuf_tensor(uq("zs"), [128, NCH, D], F32))
                xc = es.enter_context(nc.sbuf_tensor(uq("xc"), [128, 16, N], BF16))
                rawb = es.enter_context(nc.sbuf_tensor(uq("rawb"), [128, 2, NCH, 132], BF16))
                dg = es.enter_context(nc.sbuf_tensor(uq("dg"), [128, 2, 4, 128], BF16))
                dtw = es.enter_context(nc.sbuf_tensor(uq("dtw"), [128, NCH, 8, 16], F32))
                xdt = es.enter_context(nc.sbuf_tensor(uq("xdt"), [128, 16, 64], BF16))
                xw = es.enter_context(nc.sbuf_tensor(uq("xw"), [128, 16, 64], BF16))
                xDd = es.enter_context(nc.sbuf_tensor(uq("xDd"), [128, 16, 64], F32))
                btm = es.enter_context(nc.sbuf_tensor(uq("btm"), [128, 512], BF16))
                LT = es.enter_context(nc.sbuf_tensor(uq("LT"), [128, 2, 128], BF16))
                MT = es.enter_context(nc.sbuf_tensor(uq("MT"), [128, 2, 128], BF16))
                yt = es.enter_context(nc.sbuf_tensor(uq("yt"), [128, D], F32))
                Sbf = es.enter_context(nc.sbuf_tensor(uq("Sbf"), [128, D], BF16))
                ynb = es.enter_context(nc.sbuf_tensor(uq("ynb"), [128, D], BF16))
                ssq = es.enter_context(nc.sbuf_tensor(uq("ssq"), [128, 8], F32))
                ctl = es.enter_context(nc.sbuf_tensor(uq("ctl"), [128, NCH, 8, 16], F32))
                hst = es.enter_context(nc.sbuf_tensor(uq("hst"), [128, NCH, 16, 3], F32))
                cio = es.enter_context(nc.sbuf_tensor(uq("cio"), [48, 128], F32))
                sio = es.enter_context(nc.sbuf_tensor(uq("sio"), [128, KT, 128], F32))
                for half in range(2):
                    def cbz(c, ps, psn, half=half):
                        A(lambda e: e.activation(out=zs[:, c, half * 512:(half + 1) * 512], in_=ps[:, 0:512], func=AF.Silu), [psn], ['zs'])
                    proj_tm('z%d' % half, hT, 'hT', 512, cbz)
                if fake:
                    for c, gc in enumerate(chunks):
                        b = gc - NPC
                        T.dma('cio', cio[:], st_conv[l, b].rearrange("k (mb ch) -> (k mb) ch", ch=128), writes=['cio'])
                        PE(lambda e: e.transpose(PS[6][:, 0:48], cio[:], identF[0:48, 0:48]), ['cio', 'cF'], ['ps6'])
                        V(lambda e: e.tensor_copy(hst[:, c, :, :], PS[6][:, 0:48].rearrange("p (k mb) -> p mb k", k=3)), ['ps6'], ['hst'])
                else:
                    if sc == 0:
                        V(lambda e: e.memset(histT[:, l, :, :], 0.0), [], ['hist%d' % l])
                        GP(lambda e: e.memset(S_l, 0.0), [], [Sres])
                    V(lambda e: e.tensor_copy(hst[:, 0, :, :], histT[:, l, :, :]), ['hist%d' % l], ['hst'])
                V(lambda e: e.memset(ctl[:], 0.0), [], ['ctl'])
                for blk in range(4):
                    def cbx(m, ps, psn, blk=blk):
                        mb = blk * 4 + m
                        rb = mb % 2
                        rw = rawb[:, rb, :, :]
                        rwn = 'raw%d' % rb
                        dgn = 'dg%d' % rb
                        psv = ps[:, 0:N].rearrange("p (c t) -> p c t", t=128)
                        A(lambda e: e.copy(out=rw[:, :, 3:131], in_=psv), [psn], [rwn])
                        for k in range(4):
                            V(lambda e: e.tensor_scalar(out=dg[:, rb, k, :], in0=identB[:], scalar1=convT[:, l, mb, k:k + 1], scalar2=None, op0=OP.mult), ['identB', 'conv'], [dgn])
                        for c in range(NCH):
                            if fake or c == 0:
                                V(lambda e: e.tensor_copy(rw[:, c, 0:3], hst[:, c, mb, :]), ['hst', rwn], [rwn])
                            else:
                                V(lambda e: e.tensor_copy(rw[:, c, 0:3], rw[:, c - 1, 128:131]), [rwn], [rwn])
                            cps, cpn = PS[2 + (c % 2)], PSN[2 + (c % 2)]
                            for k in range(4):
                                PE(lambda e: e.matmul(cps[:, 0:128], lhsT=dg[:, rb, k, :], rhs=rw[:, c, k:k + 128], start=(k == 0), stop=(k == 3)), [dgn, rwn], [cpn])
                            A(lambda e: e.activation(out=xc[:, mb, cs_(c)], in_=cps[:, 0:128], func=AF.Silu, bias=convT[:, l, mb, 4:5]), [cpn, 'conv'], ['xc'])
                            if fake:
                                V(lambda e: e.tensor_copy(ctl[:, c, 0:2, mb], hst[:, c, mb, 1:3]), ['hst'], ['ctl'])
                                V(lambda e: e.tensor_copy(ctl[:, c, 2:3, mb], psv[:, c, 0:1]), [psn], ['ctl'])
                        if not fake:
                            V(lambda e: e.tensor_copy(histT[:, l, mb, :], psv[:, NCH - 1, 125:128]), [psn], ['hist%d' % l])
                            if sc == last_prompt_sc:
                                V(lambda e: e.tensor_copy(ctl[:, 0, 0:3, mb], psv[:, NCH - 1, 125:128]), [psn], ['ctl'])
                    proj_fm('xbc%d' % blk, hT, 'hT', 4, cbx, lag=2, banks=[0, 1, 4, 5])
                flush_pend()
                cout = []
                if fake:
                    cout = [(c, o_conv_s[l, chunks[c] - NPC]) for c in range(NCH)]
                elif sc == last_prompt_sc:
                    cout = [(0, o_conv_p[l])]
                for c, dst in cout:
                    PE(lambda e: e.transpose(PS[6][:, 0:128], ctl[:, c, :, :].rearrange("p k mb -> p (k mb)"), identF), ['ctl', 'cF'], ['ps6'])
                    A(lambda e: e.copy(out=cio[:], in_=PS[6][0:48, 0:128]), ['ps6'], ['cio'])
                    T.dma('cio_o', dst.rearrange("k (mb ch) -> (k mb) ch", ch=128), cio[:], reads=['cio'], writes=['o_conv'])
                def cbdt(c, ps, psn):
                    dq = dtw[:, c, :, :]
                    V(lambda e: e.tensor_tensor(out=dq[:, 0, :], in0=ps[:, 0:16], in1=rowsT[:, l, 0, :], op=OP.add), [psn, 'rows'], ['dtw'])
                    V(lambda e: e.tensor_scalar(out=dq[:, 7, :], in0=dq[:, 0, :], scalar1=-1.0, scalar2=None, op0=OP.mult), ['dtw'], ['dtw'])
                    V(lambda e: e.tensor_tensor(out=dq[:, 7, :], in0=dq[:, 7, :], in1=dq[:, 0, :], op=OP.max), ['dtw'], ['dtw'])
                    A(lambda e: e.activation(out=dq[:, 7, :], in_=dq[:, 7, :], func=AF.Exp, scale=-1.0), ['dtw'], ['dtw'])
                    A(lambda e: e.activation(out=dq[:, 7, :], in_=dq[:, 7, :], func=AF.Ln, bias=1.0), ['dtw'], ['dtw'])
                    V(lambda e: e.scalar_tensor_tensor(out=dq[:, 0, :], in0=dq[:, 0, :], scalar=0.0, in1=dq[:, 7, :], op0=OP.max, op1=OP.add), ['dtw'], ['dtw'])
                    if fake:
                        V(lambda e: e.tensor_scalar(out=dq[:, 0, :], in0=dq[:, 0, :], scalar1=cSt[:, 0:1], scalar2=None, op0=OP.mult), ['dtw', 'cS'], ['dtw'])
                proj_tm('dt', hT, 'hT', 16, cbdt)

                for c, gc in enumerate(chunks):
                    dq = dtw[:, c, :, :]
                    dt_, dA, acum, de, cd, ea, nac, tmp = [dq[:, i, :] for i in range(8)]
                    bc64 = lambda ap: ap.unsqueeze(2).to_broadcast([128, 16, 64])
                    if fake:
                        b = gc - NPC
                        T.dma('sio', sio[:], st_ssm[l, b].rearrange("(j q) n -> q j n", q=128), writes=['sio'])
                        for half in range(2):
                            for j in range(4):
                                PE(lambda e: e.transpose(PS[4 + half][:, j * 128:(j + 1) * 128], sio[:, half * 4 + j, :], identF), ['sio', 'cF'], [PSN[4 + half]])
                            A(lambda e: e.copy(out=Sst[:, l, half * 512:(half + 1) * 512], in_=PS[4 + half][:]), [PSN[4 + half]], [Sres])
                    V(lambda e: e.tensor_tensor(out=dA, in0=dt_, in1=Abc[:, l, :], op=OP.mult), ['dtw', 'Abc'], ['dtw'])
                    PE(lambda e: e.matmul(PS[2][:, 0:16], lhsT=triF, rhs=dA, start=True, stop=True), ['dtw', 'cF'], ['ps2a'])
                    PE(lambda e: e.matmul(PS[2][:, 16:32], lhsT=onesF, rhs=dA, start=True, stop=True), ['dtw', 'cF'], ['ps2a'])
                    A(lambda e: e.copy(out=acum, in_=PS[2][:, 0:16]), ['ps2a'], ['dtw'])
                    V(lambda e: e.tensor_tensor(out=tmp, in0=PS[2][:, 16:32], in1=acum, op=OP.subtract), ['ps2a', 'dtw'], ['dtw'])
                    A(lambda e: e.activation(out=de, in_=tmp, func=AF.Exp), ['dtw'], ['dtw'])
                    A(lambda e: e.activation(out=cd, in_=PS[2][:, 16:32], func=AF.Exp), ['ps2a'], ['dtw'])
                    A(lambda e: e.activation(out=ea, in_=acum, func=AF.Exp), ['dtw'], ['dtw'])
                    V(lambda e: e.tensor_scalar(out=nac, in0=acum, scalar1=-1.0, scalar2=None, op0=OP.mult), ['dtw'], ['dtw'])
                    for j in range(8):
                        PE(lambda e: e.transpose(PSB[:, j * 128:(j + 1) * 128], xc[:, j, cs_(c)], identB[:]), ['xc', 'identB'], ['psb_lo', 'psb_hi'])
                    pv = PSB[:].rearrange("p (h d) -> p h d", d=64)
                    V(lambda e: e.tensor_tensor(out=xdt[:], in0=pv, in1=bc64(dt_), op=OP.mult), ['psb_lo', 'psb_hi', 'dtw'], ['xdt'])
                    V(lambda e: e.tensor_tensor(out=xDd[:], in0=pv, in1=bc64(rowsT[:, l, 2, :]), op=OP.mult), ['psb_lo', 'psb_hi', 'rows'], ['xDd'])
                    V(lambda e: e.tensor_tensor(out=xw[:], in0=xdt[:], in1=bc64(de), op=OP.mult), ['xdt', 'dtw'], ['xw'])
                    for g in range(4):
                        PE(lambda e: e.transpose(PSB[:, g * 128:(g + 1) * 128], xc[:, 8 + g, cs_(c)], identB[:]), ['xc', 'identB'], ['psb_lo', 'psb_hi'])
                    A(lambda e: e.copy(out=btm[:], in_=PSB[:, 0:512]), ['psb_lo', 'psb_hi'], ['btm'])
                    for g in range(4):
                        PE(lambda e: e.matmul(PS[3][:, g * 128:(g + 1) * 128], lhsT=xc[:, 8 + g, cs_(c)], rhs=xc[:, 12 + g, cs_(c)], start=True, stop=True),
                           ['xc'], ['ps3'])
                    A(lambda e: e.copy(out=Sbf[:], in_=S_l), [Sres], ['Sbf'])
                    for g in range(4):
                        PE(lambda e: e.matmul(PS[g // 2][:, (g % 2) * 256:(g % 2) * 256 + 256], lhsT=xc[:, 12 + g, cs_(c)],
                                              rhs=Sbf[:, g * 256:(g + 1) * 256], start=True, stop=True), ['xc', 'Sbf'], [PSN[g // 2]])
                    for half in range(2):
                        yv = yt[:, half * 512:(half + 1) * 512].rearrange("p (h d) -> p h d", d=64)
                        V(lambda e: e.tensor_tensor(out=yv, in0=PS[half][:].rearrange("p (h d) -> p h d", d=64),
                                                    in1=ea[:, half * 8:half * 8 + 8].unsqueeze(2).to_broadcast([128, 8, 64]), op=OP.mult),
                          [PSN[half], 'dtw'], ['yt'])
                    def ssd_A(h):
                        pb = h % 2
                        LTp = PS[pb][:, 0:128]
                        ln = PSN[pb]
                        PE(lambda e: e.matmul(LTp, lhsT=dA[:, h:h + 1].to_broadcast([128, 128]), rhs=triF, start=True, stop=False), ['dtw', 'cF'], [ln])
                        PE(lambda e: e.matmul(LTp, lhsT=identF, rhs=negF, start=False, stop=True), ['cF'], [ln])
                        A(lambda e: e.activation(out=LT[:, pb, :], in_=LTp, func=AF.Exp, bias=nac[:, h:h + 1]), [ln, 'dtw'], ['LT%d' % pb])

                    def ssd_B(h):
                        g = h // 4
                        pb = h % 2
                        V(lambda e: e.tensor_tensor(out=MT[:, pb, :], in0=PS[3][:, g * 128:(g + 1) * 128], in1=LT[:, pb, :], op=OP.mult),
                          ['ps3', 'LT%d' % pb], ['MT%d' % pb])
                        PE(lambda e: e.matmul(PS[4 + h // 8][:, (h % 8) * 64:(h % 8) * 64 + 64], lhsT=MT[:, pb, :], rhs=xdt[:, h, :], start=True, stop=True),
                           ['MT%d' % pb, 'xdt'], [PSN[4 + h // 8]])

                    for i in range(16 + 1):
                        if i < 16:
                            ssd_A(i)
                        if i >= 1:
                            ssd_B(i - 1)
                    for half in range(2):
                        V(lambda e: e.tensor_tensor(out=yt[:, half * 512:(half + 1) * 512], in0=PS[4 + half][:], in1=yt[:, half * 512:(half + 1) * 512], op=OP.add),
                          [PSN[4 + half], 'yt'], ['yt'])
                    V(lambda e: e.tensor_tensor(out=yt[:], in0=yt[:], in1=xDd[:].rearrange("p h d -> p (h d)"), op=OP.add), ['yt', 'xDd'], ['yt'])
                    for g in range(4):
                        PE(lambda e: e.matmul(PS[4 + g // 2][:, (g % 2) * 256:(g % 2) * 256 + 256], lhsT=btm[:, g * 128:(g + 1) * 128],
                                              rhs=xw[:, 4 * g:4 * g + 4, :].rearrange("p h d -> p (h d)"), start=True, stop=True), ['btm', 'xw'], [PSN[4 + g // 2]])
                    S3 = S_l.rearrange("p (h d) -> p h d", d=64)
                    V(lambda e: e.tensor_tensor(out=S3, in0=S3, in1=bc64(cd), op=OP.mult), [Sres, 'dtw'], [Sres])
                    for half in range(2):
                        V(lambda e: e.tensor_tensor(out=Sst[:, l, half * 512:(half + 1) * 512], in0=PS[4 + half][:], in1=Sst[:, l, half * 512:(half + 1) * 512], op=OP.add),
                          [PSN[4 + half], Sres], [Sres])
                    sdst = None
                    if fake:
                        sdst = o_ssm_s[l, gc - NPC]
                    elif sc == last_prompt_sc and c == NCH - 1:
                        sdst = o_ssm_p[l]
                    if sdst is not None:
                        for half in range(2):
                            for j in range(4):
                                jj = half * 4 + j
                                PE(lambda e: e.transpose(PS[half][:, j * 128:(j + 1) * 128], Sst[:, l, jj * 128:(jj + 1) * 128], identF), [Sres, 'cF'], [PSN[half]])
                            A(lambda e: e.copy(out=sio[:, half * 4:half * 4 + 4, :], in_=PS[half][:].rearrange("p (j n) -> p j n", n=128)), [PSN[half]], ['sio'])
                        T.dma('sio_o', sdst.rearrange("(j q) n -> q j n", q=128), sio[:], reads=['sio'], writes=['o_ssm'])
                    V(lambda e: e.tensor_tensor(out=yt[:], in0=yt[:], in1=zs[:, c, :], op=OP.mult), ['yt', 'zs'], ['yt'])
                    for g in range(4):
                        A(lambda e: e.activation(out=ynb[:, g * 256:(g + 1) * 256], in_=yt[:, g * 256:(g + 1) * 256], func=AF.Square, accum_out=ssq[:, g:g + 1]),
                          ['yt'], ['ynb', 'ssq'])
                    A(lambda e: e.activation(out=ssq[:, 4:8], in_=ssq[:, 0:4], func=AF.Sqrt, bias=EPS, scale=1.0 / 256.0), ['ssq'], ['ssq'])
                    V(lambda e: e.reciprocal(out=ssq[:, 4:8], in_=ssq[:, 4:8]), ['ssq'], ['ssq'])
                    V(lambda e: e.tensor_tensor(out=ynb[:].rearrange("p (g d) -> p g d", d=256), in0=yt[:].rearrange("p (g d) -> p g d", d=256),
                                                in1=ssq[:, 4:8].unsqueeze(2).to_broadcast([128, 4, 256]), op=OP.mult), ['yt', 'ssq', 'ynb'], ['ynb'])
                    for j in range(8):
                        PE(lambda e: e.transpose(PSB[:, j * 128:(j + 1) * 128], ynb[:, j * 128:(j + 1) * 128], identB[:]), ['ynb', 'identB'], ['psb_lo', 'psb_hi'])
                    V(lambda e: e.tensor_tensor(out=brf[:, :, cs_(c)], in0=PSB[:].rearrange("p (j t) -> p j t", t=128),
                                                in1=colsT[:, l, 16:24].unsqueeze(2).to_broadcast([128, 8, 128]), op=OP.mult), ['psb_lo', 'psb_hi', 'cols'], ['brf'])
                branch_out(['mp0', 'mp1'], brf, 'brf')
                dbg_dump(sc, l, 0, ybr)
                gate_stage(l, 0)
                T.barrier()

            if STOP <= 2:
                T.finish()
                return nc, T
            with ExitStack() as es:
                ufm = es.enter_context(nc.sbuf_tensor(uq("ufm"), [128, KT, N], BF16))
                Zt = es.enter_context(nc.sbuf_tensor(uq("Zt"), [128, 64, 33], F32))
                Gt = es.enter_context(nc.sbuf_tensor(uq("Gt"), [128, 64, 33], F32))
                tz = es.enter_context(nc.sbuf_tensor(uq("tz"), [128, 2, 2, 16, 32], BF16))
                Pp = es.enter_context(nc.sbuf_tensor(uq("Pp"), [128, 64, 32], BF16))
                Qp = es.enter_context(nc.sbuf_tensor(uq("Qp"), [128, 64, 32], BF16))
                ysb = es.enter_context(nc.sbuf_tensor(uq("ysb"), [32, D], F32))
                pre = es.enter_context(nc.sbuf_tensor(uq("pre"), [128, 2, 8, 32], F32))
                s5io = es.enter_context(nc.sbuf_tensor(uq("s5io"), [64, 128], F32))
                hcol2 = es.enter_context(nc.sbuf_tensor(uq("hcol"), [128, 128], F32))
                hcol = hcol2[:, 0:64]
                V(lambda e: e.memset(hcol2[:], 0.0), [], ['hcol'])
                for half in range(2):
                    def cbu(m, ps, psn, half=half):
                        A(lambda e: e.copy(out=ufm[:, half * 4 + m, :], in_=ps[:, 0:N]), [psn], ['ufm'])
                    proj_fm('u%d' % half, hT, 'hT', 4, cbu)
                cres = 's5c%d' % l
                NSUB = 128 // TS5
                U = NCH * NSUB

                def s5_aq(u, qd):
                    c, s = divmod(u, NSUB)
                    t0 = c * 128 + TS5 * s
                    hh, kq = ((0, 0), (1, 0), (0, 1), (1, 1))[qd]
                    tp = qd % 2
                    b1, b2 = 2 * hh, 2 * hh + 1
                    for i in range(16):
                        kt, e_ = 4 * kq + i // 4, i % 4
                        for o, bb in ((0, b1), (1, b2)):
                            PE(lambda e: e.matmul(PS[bb][:, i * 32:(i + 1) * 32], lhsT=Bw[64 * hh:64 * hh + 64, kt, e_, o, :],
                                                  rhs=ufm[64 * hh:64 * hh + 64, kt, t0:t0 + TS5], start=True, stop=True), ['pkB', 'ufm'], [PSN[bb]])
                    gsel = lambda tb: tb.rearrange("p (k h e) t -> p k h e t", h=2, e=4)[:, 4 * kq:4 * kq + 4, hh, :, :]
                    pv4 = lambda bb: PS[bb][:].rearrange("p (k e t) -> p k e t", e=4, t=32)
                    tz4 = lambda j: tz[:, tp, j, :, :].rearrange("p (k e) t -> p k e t", e=4)
                    V(lambda e: e.tensor_tensor(out=tz4(0), in0=pv4(b1), in1=gsel(WRt), op=OP.mult), [PSN[b1], 'pkB'], ['tz0%d' % tp])
                    V(lambda e: e.tensor_tensor(out=tz4(1), in0=pv4(b2), in1=gsel(WIt), op=OP.mult), [PSN[b2], 'pkB'], ['tz1%d' % tp])
                    V(lambda e: e.tensor_tensor(out=gsel(Zt[:])[:, :, :, 1:33], in0=tz4(0), in1=tz4(1), op=OP.add), ['tz0%d' % tp, 'tz1%d' % tp], ['Zt'])

                def s5_b(u):
                    c, s = divmod(u, NSUB)
                    gc = chunks[c]
                    if s == 0:
                        if fake:
                            b = gc - NPC
                            T.dma('s5io', s5io[:].rearrange("g (r n) -> g r n", r=2), st_s5[l, b].rearrange("r g n -> g r n"), writes=['s5io'])
                            PE(lambda e: e.transpose(PS[6][:, 448:512], s5io[:], identF[0:64, 0:64]), ['s5io', 'cF'], ['ps6t'])
                            A(lambda e: e.copy(out=hcol, in_=PS[6][:, 448:512]), ['ps6t'], ['hcol'])
                            rot(s5c[:, l, :], cres, hcol, 'hcol', 0, ROT)
                        elif sc == 0 and c == 0:
                            V(lambda e: e.memset(s5c[:, l, :], 0.0), [], [cres])
                    V(lambda e: e.tensor_copy(Zt[:, :, 0], s5c[:, l, :]), [cres], ['Zt'])
                    V(lambda e: e.tensor_tensor_scan(out=Gt[:].rearrange("p g t -> p (g t)"), data0=MULT, data1=Zt[:].rearrange("p g t -> p (g t)"),
                                                     initial=0.0, op0=OP.mult, op1=OP.add), ['Zt', 'pkF'], ['Gt'])
                    rot(s5c[:, l, :], cres, Gt[:, :, 32], 'Gt', 2, ROT)
                    sdst = None
                    if fake and s == 0:
                        sdst = o_s5_s[l, gc - NPC]
                        V(lambda e: e.tensor_copy(hcol, Gt[:, :, 1]), ['Gt'], ['hcol'])
                    elif (not fake) and sc == last_prompt_sc and c == NCH - 1 and s == NSUB - 1:
                        sdst = o_s5_p[l]
                        rot(hcol, 'hcol', Gt[:, :, 32], 'Gt', 1, ROT)
                    if sdst is not None:
                        PE(lambda e: e.transpose(PS[6][:, 320:448], hcol2[:], identF), ['hcol', 'cF'], ['ps6s'])
                        A(lambda e: e.copy(out=s5io[:], in_=PS[6][0:64, 320:448]), ['ps6s'], ['s5io'])
                        T.dma('s5io_o', sdst.rearrange("r g n -> g r n"), s5io[:].rearrange("g (r n) -> g r n", r=2), reads=['s5io'], writes=['o_s5'])

                def s5_pq(u):
                    V(lambda e: e.tensor_tensor(out=Pp[:], in0=Gt[:, :, 1:33], in1=CSt_, op=OP.mult), ['Gt', 'pkB'], ['Pp'])
                    V(lambda e: e.tensor_tensor(out=Qp[:], in0=Gt[:, :, 1:33], in1=SNt, op=OP.mult), ['Gt', 'pkB'], ['Qp'])

                def s5_d(u):
                    c, s = divmod(u, NSUB)
                    t0 = c * 128 + TS5 * s
                    for g in range(64):
                        bank, bn = PS[4 + g // 32], PSN[4 + g // 32]
                        col = (g % 32) * 16
                        PE(lambda e: e.matmul(bank[0:32, col:col + 16], lhsT=Pp[:, g, :], rhs=Cw[:, g, 0, :], start=True, stop=False), ['Pp', 'pkB'], [bn])
                        PE(lambda e: e.matmul(bank[0:32, col:col + 16], lhsT=Qp[:, g, :], rhs=Cw[:, g, 1, :], start=False, stop=True), ['Qp', 'pkB'], [bn])
                    for half in range(2):
                        A(lambda e: e.copy(out=ysb[:, half * 512:(half + 1) * 512], in_=PS[4 + half][0:32, :]), [PSN[4 + half]], ['ysb'])
                    for j in range(8):
                        PE(lambda e: e.transpose(PS[6][:, j * 32:(j + 1) * 32], ysb[:, j * 128:(j + 1) * 128], identF[0:32, 0:32]), ['ysb', 'cF'], ['ps6'])

                def s5_d2(u):
                    c, s = divmod(u, NSUB)
                    t0 = c * 128 + TS5 * s
                    p0 = pre[:, 0, :, :]
                    p1 = pre[:, 1, :, :]
                    V(lambda e: e.tensor_tensor(out=p0, in0=ufm[:, :, t0:t0 + TS5], in1=colsT[:, l, 24:32].unsqueeze(2).to_broadcast([128, 8, 32]), op=OP.mult),
                      ['ufm', 'cols'], ['pre0'])
                    V(lambda e: e.tensor_tensor(out=p0, in0=PS[6][:, 0:256].rearrange("p (j t) -> p j t", t=32), in1=p0, op=OP.add), ['ps6', 'pre0'], ['pre0'])
                    V(lambda e: e.tensor_tensor(out=p1, in0=p0, in1=p0, op=OP.mult), ['pre0'], ['pre1'])
                    V(lambda e: e.tensor_scalar(out=p1, in0=p1, scalar1=0.044715, scalar2=1.0, op0=OP.mult, op1=OP.add), ['pre1'], ['pre1'])
                    V(lambda e: e.tensor_tensor(out=p1, in0=p1, in1=p0, op=OP.mult), ['pre0', 'pre1'], ['pre1'])
                    A(lambda e: e.activation(out=p1, in_=p1, func=AF.Sigmoid, scale=1.5957691215), ['pre1'], ['pre1'])
                    V(lambda e: e.tensor_tensor(out=brf[:, :, t0:t0 + TS5], in0=p0, in1=p1, op=OP.mult), ['pre0', 'pre1'], ['brf'])

                ulist = [c * NSUB + s for c in range(NCH) for s in range(1 if fake else NSUB)]
                for qd in range(4):
                    s5_aq(ulist[0], qd)
                s5_b(ulist[0])
                for ui, u in enumerate(ulist):
                    nxt = ulist[ui + 1] if ui + 1 < len(ulist) else None
                    if nxt is not None:
                        s5_aq(nxt, 0)
                        s5_aq(nxt, 1)
                    s5_pq(u)
                    if nxt is not None:
                        s5_aq(nxt, 2)
                        s5_aq(nxt, 3)
                    s5_d(u)
                    if nxt is not None:
                        s5_b(nxt)
                    s5_d2(u)
                if DBG:
                    A(lambda e: e.copy(out=ybr[:], in_=brf[:]), ['brf'], ['ybr'])
                    dbg_dump(sc, l, 3, ybr)
                for b2 in range(2):
                    wv, wvres = w_next('gv%d' % b2)
                    wg, wgres = w_next('gg%d' % b2, ahead=2)
                    for m in range(4):
                        mb = b2 * 4 + m
                        for kt in range(KT):
                            PE(lambda e: e.matmul(PS[0][:, 0:N], lhsT=wv[:, kt, m * 128:(m + 1) * 128], rhs=brf[:, kt, :], start=(kt == 0), stop=(kt == KT - 1)),
                               [wvres, 'brf'], ['ps0'])
                        for kt in range(KT):
                            PE(lambda e: e.matmul(PS[1][:, 0:N], lhsT=wg[:, kt, m * 128:(m + 1) * 128], rhs=brf[:, kt, :], start=(kt == 0), stop=(kt == KT - 1)),
                               [wgres, 'brf'], ['ps1'])
                        g_ = gs[mb % 2]
                        gn = 'gs%d' % (mb % 2)
                        A(lambda e: e.activation(out=g_[:], in_=PS[1][:, 0:N], func=AF.Sigmoid), ['ps1'], [gn])
                        V(lambda e: e.tensor_tensor(out=ybr[:, mb, :], in0=PS[0][:, 0:N], in1=g_[:], op=OP.mult), ['ps0', gn], ['ybr'])
                dbg_dump(sc, l, 1, ybr)
                gate_stage(l, 1)
                T.barrier()

            if STOP <= 3:
                T.finish()
                return nc, T
            with ExitStack() as es:
                qtm = es.enter_context(nc.sbuf_tensor(uq("qtm"), [128, NCH, D], BF16))
                ktm = es.enter_context(nc.sbuf_tensor(uq("ktm"), [128, NCH, 256], F32))
                kdup = es.enter_context(nc.sbuf_tensor(uq("kdup"), [128, 4, 2, 64], BF16))
                vtmf = es.enter_context(nc.sbuf_tensor(uq("vtmf"), [128, NCH, 256], F32))
                qfm = es.enter_context(nc.sbuf_tensor(uq("qfm"), [128, 8, 128], BF16))
                smx = es.enter_context(nc.sbuf_tensor(uq("smx"), [128, 3, 258], F32))
                pbt = es.enter_context(nc.sbuf_tensor(uq("pbt"), [128, 3, 258], BF16))
                pT = es.enter_context(nc.sbuf_tensor(uq("pT"), [128, 3, 256], BF16))
                otm = es.enter_context(nc.sbuf_tensor(uq("otm"), [128, D], BF16))
                ast = es.enter_context(nc.sbuf_tensor(uq("ast"), [128, 3, 8], F32))
                rpt = es.enter_context(nc.sbuf_tensor(uq("rpt"), [128, 2, 8, 8], F32))
                ckt = es.enter_context(nc.sbuf_tensor(uq("ckt"), [128, 256], F32))

                def rope(psv, psn, dstv, dres, ci, nh):
                    cosb = cRt[:, ci, 0:8].unsqueeze(1).to_broadcast([128, nh, 8])
                    sinb = cRt[:, ci, 8:16].unsqueeze(1).to_broadcast([128, nh, 8])
                    ta = rpt[:, 0, 0:nh, :]
                    tb = rpt[:, 1, 0:nh, :]
                    V(lambda e: e.tensor_tensor(out=ta, in0=psv[:, :, 0:8], in1=cosb, op=OP.mult), [psn, 'cR'], ['rpt0'])
                    V(lambda e: e.tensor_tensor(out=tb, in0=psv[:, :, 8:16], in1=sinb, op=OP.mult), [psn, 'cR'], ['rpt1'])
                    V(lambda e: e.tensor_tensor(out=dstv[:, :, 0:8], in0=ta, in1=tb, op=OP.subtract), ['rpt0', 'rpt1'], [dres])
                    V(lambda e: e.tensor_tensor(out=ta, in0=psv[:, :, 8:16], in1=cosb, op=OP.mult), [psn, 'cR'], ['rpt0'])
                    V(lambda e: e.tensor_tensor(out=tb, in0=psv[:, :, 0:8], in1=sinb, op=OP.mult), [psn, 'cR'], ['rpt1'])
                    V(lambda e: e.tensor_tensor(out=dstv[:, :, 8:16], in0=ta, in1=tb, op=OP.add), ['rpt0', 'rpt1'], [dres])

                for half in range(2):
                    def cbq(c, ps, psn, half=half):
                        ci = NPC if fake else chunks[c]
                        psv = ps[:, 0:512].rearrange("p (h d) -> p h d", d=64)
                        dstv = qtm[:, c, half * 512:(half + 1) * 512].rearrange("p (h d) -> p h d", d=64)
                        A(lambda e: e.copy(out=dstv, in_=psv), [psn], ['qtm'])
                        rope(psv, psn, dstv, 'qtm', ci, 8)
                    proj_tm('q%d' % half, hT, 'hT', 512, cbq)

                def cbkv(c, ps, psn):
                    ci = NPC if fake else chunks[c]
                    psv = ps[:, 0:256].rearrange("p (h d) -> p h d", d=64)
                    dstv = ktm[:, c, :].rearrange("p (h d) -> p h d", d=64)
                    A(lambda e: e.copy(out=dstv, in_=psv), [psn], ['ktm'])
                    A(lambda e: e.copy(out=vtmf[:, c, :], in_=ps[:, 256:512]), [psn], ['vtmf'])
                    rope(psv, psn, dstv, 'ktm', ci, 4)
                proj_tm('kv', hT, 'hT', 512, cbkv)

                kres, vres = 'kcat%d' % l, 'vcat%d' % l
                for c, gc in enumerate(chunks):
                    first_chunk = (not fake) and gc == 0
                    k3 = lambda t2d: t2d.rearrange("p (h d) -> p h d", d=64)
                    if fake:
                        b = gc - NPC
                        T.dma('ckt', ckt[:], st_k[l, b], writes=['ckt'])
                        V(lambda e: e.tensor_copy(kdup[:, :, 0, :], k3(ckt[:])), ['ckt'], ['kdup'])
                        A(lambda e: e.copy(out=kdup[:, :, 1, :], in_=k3(ckt[:])), ['ckt'], ['kdup'])
                        for kh in range(4):
                            PE(lambda e: e.transpose(PSB[:, kh * 128:(kh + 1) * 128], kdup[:, kh, :, :].rearrange("p a d -> p (a d)"), identB[:]), ['kdup', 'identB'], ['psb_lo', 'psb_hi'])
                        A(lambda e: e.copy(out=kcat[:, l, :, 0:128], in_=PSB[:, 0:512].rearrange("p (h t) -> p h t", t=128)), ['psb_lo', 'psb_hi'], [kres])
                        T.dma('ckt', ckt[:], st_v[l, b], reads=['kdup'], writes=['ckt'])
                        V(lambda e: e.tensor_copy(vcat[:, l, 0, :], ckt[:]), ['ckt'], [vres])
                        T.dma('kv_d2dk', o_k_s[l, b, 0:127, :], st_k[l, b, 1:128, :], writes=['o_kv'])
                        T.dma('kv_d2dv', o_v_s[l, b, 0:127, :], st_v[l, b, 1:128, :], writes=['o_kv'])
                        T.dma('kv_rowk', o_k_s[l, b, 127:128, :], ktm[0:1, c, :], reads=['ktm'], writes=['o_kv'])
                        T.dma('kv_rowv', o_v_s[l, b, 127:128, :], vtmf[0:1, c, :], reads=['vtmf'], writes=['o_kv'])
                    elif first_chunk:
                        V(lambda e: e.memset(kcat[:, l, :, 0:128], 0.0), [], [kres])
                        GP(lambda e: e.memset(vcat[:, l, 0, :], 0.0), [], [vres])
                    if (not fake) and sc == last_prompt_sc and c == NCH - 1:
                        T.dma('kv_rowk', o_k_p[l], ktm[:, c, :], reads=['ktm'], writes=['o_kv'])
                        T.dma('kv_rowv', o_v_p[l], vtmf[:, c, :], reads=['vtmf'], writes=['o_kv'])
                    A(lambda e: e.copy(out=vcat[:, l, 1, :], in_=vtmf[:, c, :]), ['vtmf'], [vres])
                    V(lambda e: e.tensor_copy(kdup[:, :, 0, :], k3(ktm[:, c, :])), ['ktm'], ['kdup'])
                    A(lambda e: e.copy(out=kdup[:, :, 1, :], in_=k3(ktm[:, c, :])), ['ktm'], ['kdup'])
                    for kh in range(4):
                        PE(lambda e: e.transpose(PSB[:, kh * 128:(kh + 1) * 128], kdup[:, kh, :, :].rearrange("p a d -> p (a d)"), identB[:]), ['kdup', 'identB'], ['psb_lo', 'psb_hi'])
                    A(lambda e: e.copy(out=kcat[:, l, :, 128:256], in_=PSB[:, 0:512].rearrange("p (h t) -> p h t", t=128)), ['psb_lo', 'psb_hi'], [kres])
                    for j in range(8):
                        PE(lambda e: e.transpose(PSB[:, j * 128:(j + 1) * 128], qtm[:, c, j * 128:(j + 1) * 128], identB[:]), ['qtm', 'identB'], ['psb_lo', 'psb_hi'])
                    A(lambda e: e.copy(out=qfm[:], in_=PSB[:].rearrange("p (j t) -> p j t", t=128)), ['psb_lo', 'psb_hi'], ['qfm'])
                    mask = cMt[:, 0 if first_chunk else 1, :]
                    def att_S(h):
                        kh, hh, j, par = h // 4, h % 2, h // 2, h % 2
                        sps, spn = (PS[6], 'ps6') if par == 0 else (PS[3], 'ps3')
                        PE(lambda e: e.matmul(sps[:, 0:256], lhsT=qfm[64 * hh:64 * hh + 64, j, :], rhs=kcat[64 * hh:64 * hh + 64, l, kh, :], start=True, stop=True),
                           ['qfm', kres], [spn])

                    def att_A(h):
                        kh, hh, j, par, b3 = h // 4, h % 2, h // 2, h % 2, h % 3
                        sps, spn = (PS[6], 'ps6') if par == 0 else (PS[3], 'ps3')
                        st = ast[:, b3, :]
                        stn = 'ast%d' % b3
                        V(lambda e: e.scalar_tensor_tensor(out=smx[:, b3, 0:256], in0=sps[:, 0:256], scalar=HD ** -0.5, in1=mask, op0=OP.mult, op1=OP.add),
                          [spn, 'cM'], ['smx%d' % b3])
                        V(lambda e: e.tensor_copy(smx[:, b3, 256:257], rowsT[:, l, 3, h:h + 1]), ['rows', 'smx%d' % b3], ['smx%d' % b3])
                        V(lambda e: e.tensor_reduce(out=st[:, 1:2], in_=smx[:, b3, 0:257], axis=AX.X, op=OP.max, negate=True), ['smx%d' % b3], [stn])

                    def att_B(h):
                        par, b3 = h % 2, h % 3
                        st = ast[:, b3, :]
                        stn = 'ast%d' % b3
                        A(lambda e: e.activation(out=pbt[:, b3, 0:257], in_=smx[:, b3, 0:257], func=AF.Exp, bias=st[:, 1:2], accum_out=st[:, 2:3]),
                          ['smx%d' % b3, stn], ['pbt%d' % b3, stn])
                        V(lambda e: e.reciprocal(out=st[:, 4:5], in_=st[:, 2:3]), [stn], [stn])
                        ptp = PSB if par == 0 else PS2B
                        ptn = 'psb_lo' if par == 0 else 'ps2'
                        for kb in range(2):
                            PE(lambda e: e.transpose(ptp[:, kb * 128:(kb + 1) * 128], pbt[:, b3, kb * 128:(kb + 1) * 128], identB[:]),
                               ['pbt%d' % b3, 'identB'], [ptn])

                    def att_C(h):
                        kh, par, b3 = h // 4, h % 2, h % 3
                        st = ast[:, b3, :]
                        stn = 'ast%d' % b3
                        ptp = PSB if par == 0 else PS2B
                        ptn = 'psb_lo' if par == 0 else 'ps2'
                        V(lambda e: e.tensor_copy(pT[:, b3, :], ptp[:, 0:256]), [ptn], ['pT%d' % b3])
                        ob, obn = PS[4 + par], PSN[4 + par]
                        oc = (h // 2) * 64
                        PE(lambda e: e.matmul(ob[:, oc:oc + 64], lhsT=pT[:, b3, 0:128], rhs=vcat[:, l, 0, kh * 64:(kh + 1) * 64], start=True, stop=False),
                           ['pT%d' % b3, vres], [obn])
                        PE(lambda e: e.matmul(ob[:, oc:oc + 64], lhsT=pT[:, b3, 128:256], rhs=vcat[:, l, 1, kh * 64:(kh + 1) * 64], start=False, stop=True),
                           ['pT%d' % b3, vres], [obn])
                        A(lambda e: e.activation(out=otm[:, h * 64:(h + 1) * 64], in_=ob[:, oc:oc + 64], func=AF.Copy, scale=st[:, 4:5]), [obn, stn], ['otm'])

                    att_S(0)
                    for i in range(16 + 2):
                        if i + 1 < 16:
                            att_S(i + 1)
                        if i < 16:
                            att_A(i)
                        if 0 <= i - 1 < 16:
                            att_B(i - 1)
                        if 0 <= i - 2 < 16:
                            att_C(i - 2)
                    for j in range(8):
                        PE(lambda e: e.transpose(PSB[:, j * 128:(j + 1) * 128], otm[:, j * 128:(j + 1) * 128], identB[:]), ['otm', 'identB'], ['psb_lo', 'psb_hi'])
                    A(lambda e: e.copy(out=brf[:, :, cs_(c)], in_=PSB[:].rearrange("p (j t) -> p j t", t=128)), ['psb_lo', 'psb_hi'], ['brf'])
                    if not fake:
                        V(lambda e: e.tensor_copy(kcat[:, l, :, 0:128], kcat[:, l, :, 128:256]), [kres], [kres])
                        A(lambda e: e.copy(out=vcat[:, l, 0, :], in_=vcat[:, l, 1, :]), [vres], [vres])
                branch_out(['ao0', 'ao1'], brf, 'brf')
                dbg_dump(sc, l, 2, ybr)
                gate_stage(l, 2)
                T.barrier()

            if STOP <= 4:
                T.finish()
                return nc, T
            for half in range(2):
                def cbo(m, ps, psn, half=half):
                    mb = half * 4 + m
                    V(lambda e: e.tensor_tensor(out=xT[:, mb, :], in0=ps[:, 0:N], in1=xT[:, mb, :], op=OP.add), [psn, 'xT'], ['xT'])
                proj_fm('wo%d' % half, mbf, 'mbf', 4, cbo)
            rmsnorm(n2, hT, 'hT')
            with ExitStack() as es:
                actT = es.enter_context(nc.sbuf_tensor(uq("actT"), [128, 32, N], BF16))
                for i in range(8):
                    def cbup(m, ps, psn, i=i):
                        mb = i * 4 + m
                        g_ = gs[mb % 2]
                        gn = 'gs%d' % (mb % 2)
                        A(lambda e: e.activation(out=g_[:], in_=ps[:, 0:N], func=AF.Relu), [psn], [gn])
                        A(lambda e: e.activation(out=actT[:, mb, :], in_=g_[:], func=AF.Square), [gn], ['actT'])
                    proj_fm('up%d' % i, hT, 'hT', 4, cbup)
                for i in range(8):
                    wt, wres = w_next('dn%d' % i)
                    ps, psn = pm_next()
                    for kt in range(32):
                        PE(lambda e: e.matmul(ps[:, 0:N], lhsT=wt[:, kt, :], rhs=actT[:, kt, :], start=(kt == 0), stop=(kt == 31)), [wres, 'actT'], [psn])
                    V(lambda e: e.tensor_tensor(out=xT[:, i, :], in0=ps[:, 0:N], in1=xT[:, i, :], op=OP.add), [psn, 'xT'], ['xT'])
                T.barrier()

        rmsnorm(lambda kt: fnw[:, kt:kt + 1], ybr, 'ybr')
        with ExitStack() as es:
            yout = es.enter_context(nc.sbuf_tensor(uq("yout"), [128, D], F32))
            for c, gc in enumerate(chunks):
                for half in range(2):
                    for j in range(4):
                        PE(lambda e: e.transpose(PS[4 + half][:, j * 128:(j + 1) * 128], ybr[:, half * 4 + j, cs_(c)], identF), ['ybr', 'cF'], [PSN[4 + half]])
                    A(lambda e: e.copy(out=yout[:, half * 512:(half + 1) * 512], in_=PS[4 + half][:]), [PSN[4 + half]], ['yout'])
                if fake:
                    b = gc - NPC
                    T.dma('yout', o_y[cfg.SEQ + b:cfg.SEQ + b + 1, :], yout[0:1, :], reads=['yout'], writes=['o_y'])
                else:
                    T.dma('yout', o_y[gc * 128:(gc + 1) * 128, :], yout[:], reads=['yout'], writes=['o_y'])
            T.barrier()
    T.finish()
    return nc, T


def _consts(cfg):
    f32 = np.float32
    cF = np.zeros((128, 8, 128), f32)
    s = np.arange(128)[:, None]
    t = np.arange(128)[None, :]
    cF[:, 0] = (s == t)
    cF[:, 1] = (s <= t)
    cF[:, 2] = np.where(s > t, NEGBIG, 0.0)
    cF[:, 3] = 1.0
    perm = np.zeros((128, 128), f32)
    for m in range(64):
        perm[m + 64, m] = -1.0
        perm[m, m + 64] = 1.0
    cF[:, 4] = perm
    cM = np.zeros((128, 2, 256), f32)
    i = np.arange(128)[:, None]
    j = np.arange(128)[None, :]
    cM[:, 1, 0:128] = np.where(j >= i, 0.0, NEGBIG)
    cM[:, 1, 128:256] = np.where(j <= i, 0.0, NEGBIG)
    cM[:, 0, 0:128] = NEGBIG
    cM[:, 0, 128:256] = cM[:, 1, 128:256]
    half = 8
    inv_freq = np.exp(-(f32(2.0) * np.arange(half, dtype=f32) / f32(16)) * f32(math.log(500000.0))).astype(f32)
    cR = np.zeros((128, cfg.NPC + 1, 16), f32)
    for ci in range(cfg.NPC + 1):
        base = ci * 128 if ci < cfg.NPC else PAST_LEN
        pos = (base + np.arange(128)).astype(f32)
        ang = (pos[:, None] * inv_freq[None, :]).astype(f32)
        cR[:, ci, 0:8] = np.cos(ang)
        cR[:, ci, 8:16] = np.sin(ang)
    cS = np.zeros((128, 4), f32)
    cS[0, 0] = 1.0
    cS[:64, 1] = -1.0
    cS[64:, 1] = 1.0
    cS[:64, 2] = 1.0
    cS[64:, 2] = -1.0
    return dict(cF=cF, cM=cM, cR=cR, cS=cS)


def _shared_inputs(cfg, inp):
    f32 = np.float32
    L = cfg.DEPTH
    A = lambda k: np.asarray(inp[k], dtype=f32)
    fm = lambda w: np.ascontiguousarray(w.reshape(L, 8, 128).transpose(0, 2, 1))
    cols = np.concatenate([fm(A('norm1_w')), fm(A('norm2_w')), fm(A('m_norm_w')), fm(A('s5_d'))], axis=2)
    cw_ = A('conv_w').reshape(L, 4, 16, 128).transpose(0, 3, 2, 1)
    cb_ = A('conv_b').reshape(L, 16, 128).transpose(0, 2, 1)[..., None]
    convp = np.ascontiguousarray(np.concatenate([cw_, cb_], axis=3))
    rows = np.ascontiguousarray(np.stack([A('dt_bias'), A('a_log'), A('m_d'), A('attn_sinks')], axis=1))
    lr = A('s5_lam_re').transpose(0, 2, 1)
    li = A('s5_lam_im').transpose(0, 2, 1)
    lam = np.stack([np.concatenate([lr, lr], axis=1), np.concatenate([li, li], axis=1)], axis=2)
    bre = A('s5_b_re')
    bim = A('s5_b_im')
    bw = np.zeros((L, 128, 8, 4, 2, 128), f32)
    for g in range(64):
        kt, e = g // 8, g % 4
        r0 = (g % 8) * 16
        br = bre[:, g].transpose(0, 2, 1)
        bi = bim[:, g].transpose(0, 2, 1)
        bw[:, r0:r0 + 16, kt, e, 0, 0:64] = br
        bw[:, r0:r0 + 16, kt, e, 0, 64:128] = bi
        bw[:, r0:r0 + 16, kt, e, 1, 0:64] = bi
        bw[:, r0:r0 + 16, kt, e, 1, 64:128] = br
    cre = A('s5_c_re').transpose(0, 3, 1, 2)
    cim = A('s5_c_im').transpose(0, 3, 1, 2)
    cw = np.zeros((L, 128, 64, 2, 16), f32)
    cw[:, 0:64, :, 0, :] = cre
    cw[:, 64:128, :, 0, :] = cim
    cw[:, 0:64, :, 1, :] = cim
    cw[:, 64:128, :, 1, :] = cre
    d = dict(w_in=A('w_in'), m_proj=A('m_proj'), s5_glu_w=A('s5_glu_w'), attn_o=A('attn_o'), w_out=A('w_out'),
             mlp_up=A('mlp_up'), mlp_down=A('mlp_down'), cols=np.ascontiguousarray(cols), convp=convp, rows=rows,
             lam=np.ascontiguousarray(lam), lstep=A('s5_log_step'), bw=bw.reshape(L, 128, -1), cw=cw.reshape(L, 128, -1),
             fnw=np.ascontiguousarray(A('final_norm_w').reshape(8, 128).T))
    d.update(_consts(cfg))
    return d


def _core_inputs(cfg, inp, shared, core):
    f32 = np.float32
    L, SPC = cfg.DEPTH, cfg.SPC
    seq = core // max(1, cfg.NCORES // 2)
    b0, b1 = core * SPC, (core + 1) * SPC
    A = lambda k: np.asarray(inp[k], dtype=f32)
    d = dict(shared)
    d['xin'] = np.ascontiguousarray(np.concatenate([A('x_prompt')[seq], A('x_sample')[b0:b1, 0, :]], axis=0))
    d['st_ssm'] = np.ascontiguousarray(A('state_ssm')[:, b0:b1].reshape(L, SPC, D, NS))
    d['st_conv'] = np.ascontiguousarray(A('state_conv')[:, b0:b1])
    d['st_s5'] = np.ascontiguousarray(np.stack([A('state_s5_re')[:, b0:b1], A('state_s5_im')[:, b0:b1]], axis=2))
    d['st_k'] = np.ascontiguousarray(A('cache_k')[:, b0:b1].reshape(L, SPC, 128, 256))
    d['st_v'] = np.ascontiguousarray(A('cache_v')[:, b0:b1].reshape(L, SPC, 128, 256))
    return d


def _assemble(cfg, res):
    L, SPC, SEQ = cfg.DEPTH, cfg.SPC, cfg.SEQ
    cps = max(1, cfg.NCORES // 2)
    pc = [0, cps]
    cat_s = lambda k: np.concatenate([r[k] for r in res], axis=1)
    y_p = np.stack([res[c]['o_y'][0:SEQ] for c in pc], axis=0)
    y_s = np.concatenate([r['o_y'][SEQ:SEQ + SPC] for r in res], axis=0)[:, None, :]
    ssm_p = np.stack([res[c]['o_ssm_p'] for c in pc], axis=1).reshape(L, 2, MH, HP, NS)
    ssm_s = cat_s('o_ssm_s').reshape(L, -1, MH, HP, NS)
    conv_p = np.stack([res[c]['o_conv_p'] for c in pc], axis=1)
    conv_s = cat_s('o_conv_s')
    s5_p = np.stack([res[c]['o_s5_p'] for c in pc], axis=1)
    s5_s = cat_s('o_s5_s')
    k_p = np.stack([res[c]['o_k_p'] for c in pc], axis=1).reshape(L, 2, 128, KVH, HD)
    k_s = cat_s('o_k_s').reshape(L, -1, 128, KVH, HD)
    v_p = np.stack([res[c]['o_v_p'] for c in pc], axis=1).reshape(L, 2, 128, KVH, HD)
    v_s = cat_s('o_v_s').reshape(L, -1, 128, KVH, HD)
    outs = (y_p, y_s, ssm_p, ssm_s, conv_p, conv_s,
            s5_p[:, :, 0], s5_s[:, :, 0], s5_p[:, :, 1], s5_s[:, :, 1], k_p, k_s, v_p, v_s)
    return tuple(np.ascontiguousarray(o, dtype=np.float32) for o in outs)


def kernel(**inputs):
    cfg = Cfg()
    nc, _ = build(cfg)
    shared = _shared_inputs(cfg, inputs)
    in_maps = [_core_inputs(cfg, inputs, shared, c) for c in range(cfg.NCORES)]
    res = run_bass_kernel_spmd(nc, in_maps, core_ids=list(range(cfg.NCORES)))
    return _assemble(cfg, res.results)
```

```python
import math
from contextlib import ExitStack
import numpy as np
import concourse.bass as bass
import concourse.mybir as mybir
from concourse.bass_utils import run_bass_kernel_spmd

F32 = mybir.dt.float32
BF16 = mybir.dt.bfloat16
AF = mybir.ActivationFunctionType
OP = mybir.AluOpType
AX = mybir.AxisListType

D = 1024
KT = 8
MH = 16
HP = 64
NG = 4
NS = 128
CONV_DIM = 2048
S5G = 64
S5N = 64
AH = 16
KVH = 4
HD = 64
DFF = 4096
IN_COLS = 8720
OFF_Z, OFF_XBC, OFF_DT, OFF_U, OFF_Q, OFF_K, OFF_G = 0, 1024, 3072, 3088, 4112, 5136, 5648
EPS = 1e-6
PAST_LEN = 8192
NEGBIG = -30000.0
TS5 = 32


class Cfg:
    def __init__(self, SEQ=8192, DEC_BATCH=128, DEPTH=4, NCORES=8, NCH=2):
        self.SEQ, self.DEC_BATCH, self.DEPTH, self.NCORES, self.NCH = SEQ, DEC_BATCH, DEPTH, NCORES, NCH
        self.NPC = SEQ // 128
        self.SPC = DEC_BATCH // NCORES
        assert self.NPC % NCH == 0 and self.SPC % NCH == 0
        self.NTOK = SEQ + self.SPC


class Tracker:
    def __init__(self, nc):
        self.nc = nc
        self.eng = {'pe': nc.tensor, 'dve': nc.vector, 'act': nc.scalar, 'pool': nc.gpsimd, 'sp': nc.sync}
        self.sem = {k: nc.alloc_semaphore('s_' + k) for k in self.eng}
        self.cnt = {k: 0 for k in self.sem}
        self.waited = {k: {} for k in self.eng}
        self.lastw = {}
        self.readers = {}
        self.ninst = 0
        self.bank = {}

    def _semkey(self, key):
        if key not in self.sem:
            self.sem[key] = self.nc.alloc_semaphore('s_' + key)
            self.cnt[key] = 0
        return self.sem[key]

    def _wait(self, e, key, val):
        if self.waited[e].get(key, 0) >= val:
            return
        self.eng[e].wait_ge(self.sem[key], val)
        self.waited[e][key] = val
        self.ninst += 1

    def deps(self, e, reads, writes):
        need = {}
        for r in reads:
            lw = self.lastw.get(r)
            if lw is not None:
                need[lw[0]] = max(need.get(lw[0], 0), lw[1])
        for w in writes:
            lw = self.lastw.get(w)
            if lw is not None:
                need[lw[0]] = max(need.get(lw[0], 0), lw[1])
            for k, v in self.readers.get(w, {}).items():
                need[k] = max(need.get(k, 0), v)
        if e != 'sp':
            for r in list(reads) + list(writes):
                if isinstance(r, str) and r.startswith('ps'):
                    bank = int(r[2]) if r[2].isdigit() else 7
                    for k, v in self.bank.setdefault(bank, {}).items():
                        if k != e:
                            need[k] = max(need.get(k, 0), v)
        for key, val in need.items():
            if key == e and e in ('pe', 'sp'):
                continue
            self._wait(e, key, val)

    def op(self, e, fn, reads=(), writes=()):
        self.deps(e, reads, writes)
        inst = fn(self.eng[e])
        self.cnt[e] += 1
        inst.then_inc(self.sem[e], 1)
        v = self.cnt[e]
        self.ninst += 1
        for r in list(reads) + list(writes):
            if isinstance(r, str) and r.startswith('ps'):
                bank = int(r[2]) if r[2].isdigit() else 7
                self.bank.setdefault(bank, {})[e] = v
        for r in reads:
            self.readers.setdefault(r, {})[e] = v
        for w in writes:
            self.lastw[w] = (e, v)
            self.readers[w] = {}

    def dma(self, tag, out, in_, reads=(), writes=(), q='sp'):
        key = 'd_' + tag
        self._semkey(key)
        self.deps(q, reads, writes)
        inst = self.eng[q].dma_start(out=out, in_=in_)
        self.cnt[key] += 16
        inst.then_inc(self.sem[key], 16)
        v = self.cnt[key]
        self.ninst += 1
        for r in reads:
            self.readers.setdefault(r, {})[key] = v
        for w in writes:
            self.lastw[w] = (key, v)
            self.readers[w] = {}

    def barrier(self):
        engs = ['pe', 'dve', 'act', 'pool']
        for k in list(self.sem):
            if (k.startswith('d_') or k in engs) and self.cnt[k] > 0:
                self._wait('sp', k, self.cnt[k])
        inst = self.eng['sp'].nop()
        self.cnt['sp'] += 1
        inst.then_inc(self.sem['sp'], 1)
        self.ninst += 1
        for e in engs:
            self._wait(e, 'sp', self.cnt['sp'])

    def finish(self):
        for k in self.sem:
            if self.cnt[k] > 0 and k != 'sp':
                self._wait('sp', k, self.cnt[k])


def weight_blocks():
    blks = []

    def add(name, src, r0, c0, w, kind='k8'):
        blks.append(dict(name=name, src=src, r0=r0, c0=c0, w=w, kind=kind))
    add('z0', 'w_in', 0, OFF_Z, 512); add('z1', 'w_in', 0, OFF_Z + 512, 512)
    for i in range(4):
        add('xbc%d' % i, 'w_in', 0, OFF_XBC + 512 * i, 512)
    add('dt', 'w_in', 0, OFF_DT, 16)
    add('mp0', 'm_proj', 0, 0, 512); add('mp1', 'm_proj', 0, 512, 512)
    add('g0_0', 'w_in', 0, OFF_G, 512); add('g0_1', 'w_in', 0, OFF_G + 512, 512)
    add('u0', 'w_in', 0, OFF_U, 512); add('u1', 'w_in', 0, OFF_U + 512, 512)
    add('gv0', 's5_glu_w', 0, 0, 512); add('gg0', 's5_glu_w', 0, 1024, 512)
    add('gv1', 's5_glu_w', 0, 512, 512); add('gg1', 's5_glu_w', 0, 1536, 512)
    add('g1_0', 'w_in', 0, OFF_G + 1024, 512); add('g1_1', 'w_in', 0, OFF_G + 1536, 512)
    add('q0', 'w_in', 0, OFF_Q, 512); add('q1', 'w_in', 0, OFF_Q + 512, 512)
    add('kv', 'w_in', 0, OFF_K, 512)
    add('ao0', 'attn_o', 0, 0, 512); add('ao1', 'attn_o', 0, 512, 512)
    add('g2_0', 'w_in', 0, OFF_G + 2048, 512); add('g2_1', 'w_in', 0, OFF_G + 2560, 512)
    add('wo0', 'w_out', 0, 0, 512); add('wo1', 'w_out', 0, 512, 512)
    for i in range(8):
        add('up%d' % i, 'mlp_up', 0, 512 * i, 512)
    for i in range(8):
        add('dn%d' % i, 'mlp_down', 0, 128 * i, 128, kind='k32')
    return blks


WBLKS = weight_blocks()
NBLK = len(WBLKS)

PB_WR, PB_WI, PB_CS, PB_SN = 0, 2048, 4096, 6144
PB_BW = 8192
PB_CW = PB_BW + 8192
PB_SZ = PB_CW + 2048
PF_MULT = 0
PF_ROT = PF_MULT + 64 * 33
PF_SZ = PF_ROT + 6 * 64


def build(cfg):
    L, NCH, NPC, SPC, NTOK = cfg.DEPTH, cfg.NCH, cfg.NPC, cfg.SPC, cfg.NTOK
    N = NCH * 128
    nc = bass.Bass("TRN2", target_bir_lowering=False)
    T = Tracker(nc)

    def din(name, shape, dt=F32):
        return nc.dram_tensor(name, list(shape), dt, kind="ExternalInput").ap()

    def dout(name, shape):
        return nc.dram_tensor(name, list(shape), F32, kind="ExternalOutput").ap()

    xin = din("xin", [NTOK, D])
    Wd = dict(w_in=din("w_in", [L, D, IN_COLS]), m_proj=din("m_proj", [L, D, D]),
              s5_glu_w=din("s5_glu_w", [L, D, 2 * D]), attn_o=din("attn_o", [L, D, D]),
              w_out=din("w_out", [L, D, D]), mlp_up=din("mlp_up", [L, D, DFF]),
              mlp_down=din("mlp_down", [L, DFF, D]))
    colsd = din("cols", [L, 128, 32])
    convd = din("convp", [L, 128, 16, 5])
    rowsd = din("rows", [L, 4, 16])
    lamd = din("lam", [L, 128, 2, 64])
    lstepd = din("lstep", [L, 64])
    bwd = din("bw", [L, 128, 8 * 4 * 2 * 128])
    cwd = din("cw", [L, 128, 64 * 2 * 16])
    fnwd = din("fnw", [128, 8])
    st_ssm = din("st_ssm", [L, SPC, D, NS])
    st_conv = din("st_conv", [L, SPC, 3, CONV_DIM])
    st_s5 = din("st_s5", [L, SPC, 2, S5G, S5N])
    st_k = din("st_k", [L, SPC, 128, 256])
    st_v = din("st_v", [L, SPC, 128, 256])
    cF = din("cF", [128, 8, 128])
    cM = din("cM", [128, 2, 256])
    cR = din("cR", [128, NPC + 1, 16])
    cS = din("cS", [128, 4])

    o_y = dout("o_y", [NTOK, D])
    o_ssm_p = dout("o_ssm_p", [L, D, NS]); o_ssm_s = dout("o_ssm_s", [L, SPC, D, NS])
    o_conv_p = dout("o_conv_p", [L, 3, CONV_DIM]); o_conv_s = dout("o_conv_s", [L, SPC, 3, CONV_DIM])
    o_s5_p = dout("o_s5_p", [L, 2, S5G, S5N]); o_s5_s = dout("o_s5_s", [L, SPC, 2, S5G, S5N])
    o_k_p = dout("o_k_p", [L, 128, 256]); o_k_s = dout("o_k_s", [L, SPC, 128, 256])
    o_v_p = dout("o_v_p", [L, 128, 256]); o_v_s = dout("o_v_s", [L, SPC, 128, 256])

    wbf = nc.dram_tensor("wbf", [L, NBLK, 128, 4096], BF16, kind="Internal").ap()
    packB = nc.dram_tensor("packB", [L, 128, PB_SZ], BF16, kind="Internal").ap()
    packF = nc.dram_tensor("packF", [L, 128, PF_SZ], F32, kind="Internal").ap()

    _uq = [0]

    def uq(name):
        _uq[0] += 1
        return '%s_%d' % (name, _uq[0])

    def sb(name, shape, dt=F32):
        return nc.alloc_sbuf_tensor(name, list(shape), dt)

    DBG = getattr(cfg, 'DEBUG', False)
    STOP = getattr(cfg, 'STOP', 99)
    if DBG:
        o_dbg = dout('o_dbg', [(cfg.NPC + cfg.SPC) // NCH, L, 4, 128, KT * N])

    def dbg_dump(sc, l, k, tile):
        if DBG:
            T.dma('dbg', o_dbg[sc, l, k], tile[:].rearrange('p a b -> p (a b)'), reads=['ybr', 'xT'], writes=['o_dbg'])

    PS = [nc.alloc_psum_tensor("ps%d" % i, [128, 512], F32) for i in range(7)]
    PSB = nc.alloc_psum_tensor("psb", [128, 1024], BF16)
    PSN = ['ps%d' % i for i in range(7)]
    PS2B = PS[2][:].bitcast(BF16)

    cFt = sb("cFt", [128, 8, 128]); cMt = sb("cMt", [128, 2, 256]); cRt = sb("cRt", [128, NPC + 1, 16])
    cSt = sb("cSt", [128, 4]); fnw = sb("fnw_t", [128, 8])
    identB = sb("identB", [128, 128], BF16)
    onesB = sb("onesB", [128, 128], BF16)
    identF = cFt[:, 0, :]; triF = cFt[:, 1, :]; negF = cFt[:, 2, :]; onesF = cFt[:, 3, :]; permF = cFt[:, 4, :]
    colsT = sb("colsT", [128, L, 32]); convT = sb("convT", [128, L, 16, 5]); rowsT = sb("rowsT", [128, L, 4, 16])
    Abc = sb("Abc", [128, L, 16])
    Sst = sb("Sst", [128, L, D])
    histT = sb("histT", [128, L, 16, 3])
    s5c = sb("s5c", [128, L, 64])
    kcat = sb("kcat", [128, L, 4, 256], BF16)
    vcat = sb("vcat", [128, L, 2, 256], BF16)
    wr = [sb("wr%d" % i, [128, 4096], BF16) for i in range(4)]
    pkB = sb("pkB", [128, PB_SZ], BF16)
    pkF = sb("pkF", [128, PF_SZ])

    T.dma('c_cF', cFt[:], cF, writes=['cF'])
    T.dma('c_cM', cMt[:], cM, writes=['cM'])
    T.dma('c_cR', cRt[:], cR, writes=['cR'])
    T.dma('c_cS', cSt[:], cS, writes=['cS'])
    T.dma('c_fnw', fnw[:], fnwd, writes=['fnw'])
    for l in range(L):
        T.dma('c_cols', colsT[:, l, :], colsd[l], writes=['cols'])
        T.dma('c_conv', convT[:, l, :, :], convd[l], writes=['conv'])
        T.dma('c_rows', rowsT[:, l, :, :].rearrange("p a b -> p (a b)"),
              rowsd[l].rearrange("a b -> (a b)").partition_broadcast(128), writes=['rows'])
    T.op('dve', lambda e: e.tensor_copy(identB[:], identF), reads=['cF'], writes=['identB'])
    T.op('dve', lambda e: e.tensor_copy(onesB[:], onesF), reads=['cF'], writes=['onesB'])
    T.op('act', lambda e: e.activation(out=Abc[:], in_=rowsT[:, :, 1, :], func=AF.Exp), reads=['rows'], writes=['Abc'])
    T.op('dve', lambda e: e.tensor_scalar(out=Abc[:], in0=Abc[:], scalar1=-1.0, scalar2=None, op0=OP.mult),
         reads=['Abc'], writes=['Abc'])

    with ExitStack() as es:
        lamT = es.enter_context(nc.sbuf_tensor(uq("lamT"), [128, 2, 64], F32))
        stp = es.enter_context(nc.sbuf_tensor(uq("stp"), [128, 64], F32))
        CS = es.enter_context(nc.sbuf_tensor(uq("CS"), [128, 64, 33], F32))
        SN = es.enter_context(nc.sbuf_tensor(uq("SN"), [128, 64, 33], F32))
        tA = es.enter_context(nc.sbuf_tensor(uq("tA"), [128, 64, 32], F32))
        tB = es.enter_context(nc.sbuf_tensor(uq("tB"), [128, 64, 32], F32))
        sm = es.enter_context(nc.sbuf_tensor(uq("sm"), [128, 12, 64], F32))
        stg = es.enter_context(nc.sbuf_tensor(uq("stg"), [128, 4096], F32))
        for l in range(L):
            T.dma('pl_lam', lamT[:], lamd[l], writes=['lamT'])
            T.dma('pl_stp', stp[:], lstepd[l].partition_broadcast(128), writes=['stp'])
            lr = lamT[:, 0, :]; li = lamT[:, 1, :]
            th = sm[:, 0, :]; mag = sm[:, 1, :]; r = sm[:, 2, :]; m_ = sm[:, 3, :]
            c1 = sm[:, 4, :]; s1 = sm[:, 5, :]; fre = sm[:, 6, :]; fim = sm[:, 7, :]
            t0 = sm[:, 8, :]; t1 = sm[:, 9, :]; t2 = sm[:, 10, :]; t3 = sm[:, 11, :]
            V = lambda f, rd, wr_: T.op('dve', f, reads=rd, writes=wr_)
            A_ = lambda f, rd, wr_: T.op('act', f, reads=rd, writes=wr_)
            A_(lambda e: e.activation(out=stp[:], in_=stp[:], func=AF.Exp), ['stp'], ['stp'])
            V(lambda e: e.tensor_tensor(out=th, in0=li, in1=stp[:], op=OP.mult), ['lamT', 'stp'], ['sm'])
            V(lambda e: e.tensor_tensor(out=t0, in0=lr, in1=stp[:], op=OP.mult), ['lamT', 'stp'], ['sm'])
            A_(lambda e: e.activation(out=mag, in_=t0, func=AF.Exp), ['sm'], ['sm'])
            V(lambda e: e.tensor_copy(r, th), ['sm'], ['sm'])
            for _ in range(4):
                V(lambda e: e.tensor_scalar(out=m_, in0=r, scalar1=math.pi, scalar2=-2.0 * math.pi, op0=OP.is_gt, op1=OP.mult), ['sm'], ['sm'])
                V(lambda e: e.tensor_tensor(out=r, in0=r, in1=m_, op=OP.add), ['sm'], ['sm'])
            A_(lambda e: e.activation(out=s1, in_=r, func=AF.Sin), ['sm'], ['sm'])
            V(lambda e: e.tensor_scalar(out=t0, in0=r, scalar1=-1.0, scalar2=None, op0=OP.mult), ['sm'], ['sm'])
            V(lambda e: e.tensor_tensor(out=t0, in0=t0, in1=r, op=OP.max), ['sm'], ['sm'])
            V(lambda e: e.tensor_scalar(out=t0, in0=t0, scalar1=-1.0, scalar2=math.pi / 2, op0=OP.mult, op1=OP.add), ['sm'], ['sm'])
            A_(lambda e: e.activation(out=c1, in_=t0, func=AF.Sin), ['sm'], ['sm'])
            V(lambda e: e.memset(CS[:, :, 0:1], 1.0), [], ['CS'])
            V(lambda e: e.memset(SN[:, :, 0:1], 0.0), [], ['SN'])
            V(lambda e: e.tensor_copy(CS[:, :, 1], c1), ['sm'], ['CS'])
            V(lambda e: e.tensor_copy(SN[:, :, 1], s1), ['sm'], ['SN'])
            m = 1
            while m < 32:
                cm = CS[:, :, m:m + 1].to_broadcast([128, 64, m]); smm = SN[:, :, m:m + 1].to_broadcast([128, 64, m])
                a = tA[:, :, 0:m]; b = tB[:, :, 0:m]
                V(lambda e: e.tensor_tensor(out=a, in0=CS[:, :, 1:m + 1], in1=cm, op=OP.mult), ['CS'], ['tA'])
                V(lambda e: e.tensor_tensor(out=b, in0=SN[:, :, 1:m + 1], in1=smm, op=OP.mult), ['SN'], ['tB'])
                V(lambda e: e.tensor_tensor(out=CS[:, :, m + 1:2 * m + 1], in0=a, in1=b, op=OP.subtract), ['tA', 'tB'], ['CS'])
                V(lambda e: e.tensor_tensor(out=a, in0=SN[:, :, 1:m + 1], in1=cm, op=OP.mult), ['SN', 'CS'], ['tA'])
                V(lambda e: e.tensor_tensor(out=b, in0=CS[:, :, 1:m + 1], in1=smm, op=OP.mult), ['SN', 'CS'], ['tB'])
                V(lambda e: e.tensor_tensor(out=SN[:, :, m + 1:2 * m + 1], in0=a, in1=b, op=OP.add), ['tA', 'tB'], ['SN'])
                m *= 2
            V(lambda e: e.tensor_tensor(out=t0, in0=mag, in1=c1, op=OP.mult), ['sm'], ['sm'])
            V(lambda e: e.tensor_tensor(out=t1, in0=mag, in1=s1, op=OP.mult), ['sm'], ['sm'])
            V(lambda e: e.tensor_scalar(out=t0, in0=t0, scalar1=-1.0, scalar2=None, op0=OP.add), ['sm'], ['sm'])
            V(lambda e: e.tensor_tensor(out=t2, in0=lr, in1=lr, op=OP.mult), ['lamT', 'sm'], ['sm'])
            V(lambda e: e.tensor_tensor(out=t3, in0=li, in1=li, op=OP.mult), ['lamT', 'sm'], ['sm'])
            V(lambda e: e.tensor_tensor(out=t2, in0=t2, in1=t3, op=OP.add), ['sm'], ['sm'])
            V(lambda e: e.reciprocal(out=t2, in_=t2), ['sm'], ['sm'])
            V(lambda e: e.tensor_tensor(out=fre, in0=t0, in1=lr, op=OP.mult), ['sm', 'lamT'], ['sm'])
            V(lambda e: e.tensor_tensor(out=t3, in0=t1, in1=li, op=OP.mult), ['sm', 'lamT'], ['sm'])
            V(lambda e: e.tensor_tensor(out=fre, in0=fre, in1=t3, op=OP.add), ['sm'], ['sm'])
            V(lambda e: e.tensor_tensor(out=fre, in0=fre, in1=t2, op=OP.mult), ['sm'], ['sm'])
            V(lambda e: e.tensor_tensor(out=fim, in0=t1, in1=lr, op=OP.mult), ['sm', 'lamT'], ['sm'])
            V(lambda e: e.tensor_tensor(out=t3, in0=t0, in1=li, op=OP.mult), ['sm', 'lamT'], ['sm'])
            V(lambda e: e.tensor_tensor(out=fim, in0=fim, in1=t3, op=OP.subtract), ['sm'], ['sm'])
            V(lambda e: e.tensor_tensor(out=fim, in0=fim, in1=t2, op=OP.mult), ['sm'], ['sm'])
            frb = sm[:, 6, :].unsqueeze(2).to_broadcast([128, 64, 32]); fib = sm[:, 7, :].unsqueeze(2).to_broadcast([128, 64, 32])
            pk3 = lambda off: pkB[:, off:off + 2048].rearrange("p (g t) -> p g t", t=32)
            V(lambda e: e.tensor_tensor(out=tA[:], in0=CS[:, :, 0:32], in1=frb, op=OP.mult), ['CS', 'sm'], ['tA'])
            V(lambda e: e.tensor_tensor(out=tB[:], in0=SN[:, :, 0:32], in1=fib, op=OP.mult), ['SN', 'sm'], ['tB'])
            V(lambda e: e.tensor_tensor(out=pk3(PB_WR), in0=tA[:], in1=tB[:], op=OP.add), ['tA', 'tB'], ['pkB'])
            V(lambda e: e.tensor_tensor(out=tA[:], in0=CS[:, :, 0:32], in1=fib, op=OP.mult), ['CS', 'sm', 'pkB'], ['tA'])
            V(lambda e: e.tensor_tensor(out=tB[:], in0=SN[:, :, 0:32], in1=frb, op=OP.mult), ['SN', 'sm', 'pkB'], ['tB'])
            V(lambda e: e.tensor_tensor(out=tA[:], in0=tA[:], in1=tB[:], op=OP.subtract), ['tA', 'tB'], ['tA'])
            V(lambda e: e.tensor_scalar(out=pk3(PB_WI), in0=tA[:], scalar1=cSt[:, 1:2], scalar2=None, op0=OP.mult), ['tA', 'cS'], ['pkB'])
            V(lambda e: e.tensor_scalar(out=pk3(PB_CS), in0=CS[:, :, 0:32], scalar1=cSt[:, 2:3], scalar2=None, op0=OP.mult), ['CS', 'cS'], ['pkB'])
            V(lambda e: e.tensor_scalar(out=pk3(PB_SN), in0=SN[:, :, 0:32], scalar1=-1.0, scalar2=None, op0=OP.mult), ['SN'], ['pkB'])
            for hb in range(2):
                T.dma('pl_bw', stg[:], bwd[l, :, hb * 4096:(hb + 1) * 4096], writes=['stg'])
                V(lambda e: e.tensor_copy(pkB[:, PB_BW + hb * 4096:PB_BW + (hb + 1) * 4096], stg[:]), ['stg'], ['pkB'])
            T.dma('pl_bw', stg[:, 0:2048], cwd[l], reads=[], writes=['stg'])
            V(lambda e: e.tensor_copy(pkB[:, PB_CW:PB_CW + 2048], stg[:, 0:2048]), ['stg'], ['pkB'])
            mlt = pkF[:, PF_MULT:PF_MULT + 64 * 33].rearrange("p (g t) -> p g t", t=33)
            V(lambda e: e.memset(mlt[:, :, 0:1], 0.0), [], ['pkF'])
            V(lambda e: e.tensor_copy(mlt[:, :, 1:33], sm[:, 1, :].unsqueeze(2).to_broadcast([128, 64, 32])), ['sm'], ['pkF'])
            rot = pkF[:, PF_ROT:PF_ROT + 384].rearrange("p (a c g) -> p a c g", a=3, c=2)
            for ai, tt in enumerate((1, 31, 32)):
                V(lambda e: e.tensor_copy(rot[:, ai, 0, :], CS[:, :, tt]), ['CS'], ['pkF'])
                V(lambda e: e.tensor_copy(rot[:, ai, 1, :], SN[:, :, tt]), ['SN'], ['pkF'])
            T.dma('pl_stB', packB[l], pkB[:], reads=['pkB'], writes=['packB%d' % l])
            T.dma('pl_stF', packF[l], pkF[:], reads=['pkF'], writes=['packF%d' % l])
        T.barrier()

    with ExitStack() as es:
        wst0 = es.enter_context(nc.sbuf_tensor(uq("wst0"), [128, 4096], F32))
        wst1 = es.enter_context(nc.sbuf_tensor(uq("wst1"), [128, 4096], F32))
        wst = [wst0, wst1]
        i = 0
        for l in range(L):
            for bi, blk in enumerate(WBLKS):
                s = i % 2
                src = Wd[blk['src']]
                if blk['kind'] == 'k8':
                    w = blk['w']
                    sap = src[l, 0:D, blk['c0']:blk['c0'] + w].rearrange("(kt p) c -> p kt c", p=128)
                    tap = wst[s][:, 0:8 * w].rearrange("p (kt c) -> p kt c", c=w)
                    n_el = 8 * w
                else:
                    sap = src[l, 0:DFF, blk['c0']:blk['c0'] + 128].rearrange("(kt p) c -> p kt c", p=128)
                    tap = wst[s][:, :].rearrange("p (kt c) -> p kt c", c=128)
                    n_el = 4096
                T.dma('wst%d' % s, tap, sap, writes=['wst%d' % s])
                ce = ('dve', 'act', 'pool')[i % 3]
                if ce == 'act':
                    T.op('act', lambda e: e.copy(out=wr[s][:, 0:n_el], in_=wst[s][:, 0:n_el]), reads=['wst%d' % s], writes=['wr%d' % s])
                else:
                    T.op(ce, lambda e: e.tensor_copy(wr[s][:, 0:n_el], wst[s][:, 0:n_el]), reads=['wst%d' % s], writes=['wr%d' % s])
                T.dma('wcs%d' % s, wbf[l, bi, :, 0:n_el], wr[s][:, 0:n_el], reads=['wr%d' % s], writes=['wbf%d_%d' % (l, bi)])
                i += 1
        T.barrier()

    if STOP <= 0:
        T.finish()
        return nc, T
    xT = sb("xT", [128, KT, N])
    hT = sb("hT", [128, KT, N], BF16)
    mrg = sb("mrg", [128, KT, N])
    mbf = sb("mbf", [128, KT, N], BF16)
    ybr = sb("ybr", [128, KT, N])
    brf = sb("brf", [128, KT, N], BF16)
    rs = sb("rs", [128, N])
    gs = [sb("gs%d" % i, [128, N]) for i in range(2)]
    seq = []
    n_sc = (NPC + SPC) // NCH
    for sc in range(n_sc):
        for l in range(L):
            for bi in range(NBLK):
                seq.append((l, bi))
    wstate = dict(issued=0, cur=-1)

    def w_issue(upto):
        while wstate['issued'] <= min(upto, len(seq) - 1):
            j = wstate['issued']
            l, bi = seq[j]
            s = j % 4
            blk = WBLKS[bi]
            n_el = 8 * blk['w'] if blk['kind'] == 'k8' else 4096
            T.dma('wr%d' % s, wr[s][:, 0:n_el], wbf[l, bi, :, 0:n_el], reads=['wbf%d_%d' % (l, bi)], writes=['wr%d' % s])
            wstate['issued'] += 1

    def w_next(name, ahead=3):
        wstate['cur'] += 1
        j = wstate['cur']
        l, bi = seq[j]
        assert WBLKS[bi]['name'] == name, (WBLKS[bi]['name'], name)
        w_issue(j + ahead)
        s = j % 4
        blk = WBLKS[bi]
        if blk['kind'] == 'k8':
            return wr[s][:, 0:8 * blk['w']].rearrange("p (kt c) -> p kt c", c=blk['w']), 'wr%d' % s
        return wr[s][:, :].rearrange("p (kt c) -> p kt c", c=128), 'wr%d' % s

    pmi = [0]

    def pm_next():
        pmi[0] ^= 1
        return PS[pmi[0]], PSN[pmi[0]]

    pend = []
    bankrot = [0]

    def flush_pend():
        while pend:
            f = pend.pop(0)
            f()

    def proj_fm(wname, act, actres, nm, cb, lag=0, banks=None):
        wt, wres = w_next(wname)
        for m in range(nm):
            if banks is None:
                ps, psn = pm_next()
            else:
                bk = banks[bankrot[0] % len(banks)]
                bankrot[0] += 1
                ps, psn = PS[bk], PSN[bk]
            for kt in range(KT):
                T.op('pe', lambda e: e.matmul(ps[:, 0:N], lhsT=wt[:, kt, m * 128:(m + 1) * 128], rhs=act[:, kt, :],
                                              start=(kt == 0), stop=(kt == KT - 1)), reads=[wres, actres], writes=[psn])
            if lag:
                pend.append(lambda m=m, ps=ps, psn=psn: cb(m, ps, psn))
                while len(pend) > lag:
                    pend.pop(0)()
            else:
                cb(m, ps, psn)

    def proj_tm(wname, act, actres, w, cb):
        wt, wres = w_next(wname)
        for c in range(NCH):
            ps, psn = pm_next()
            for kt in range(KT):
                T.op('pe', lambda e: e.matmul(ps[:, 0:w], lhsT=act[:, kt, c * 128:(c + 1) * 128], rhs=wt[:, kt, 0:w],
                                              start=(kt == 0), stop=(kt == KT - 1)), reads=[wres, actres], writes=[psn])
            cb(c, ps, psn)

    def rmsnorm(wcol, out_t, outres):
        T.op('act', lambda e: e.activation(out=mbf[:], in_=xT[:], func=AF.Square), reads=['xT'], writes=['mbf'])
        ps, psn = pm_next()
        for kt in range(KT):
            T.op('pe', lambda e: e.matmul(ps[:, 0:N], lhsT=onesB[:], rhs=mbf[:, kt, :], start=(kt == 0), stop=(kt == KT - 1)),
                 reads=['onesB', 'mbf'], writes=[psn])
        T.op('act', lambda e: e.activation(out=rs[:], in_=ps[:, 0:N], func=AF.Sqrt, bias=EPS, scale=1.0 / D), reads=[psn], writes=['rs'])
        T.op('dve', lambda e: e.reciprocal(out=rs[:], in_=rs[:]), reads=['rs'], writes=['rs'])
        for kt in range(KT):
            T.op('dve', lambda e: e.scalar_tensor_tensor(out=out_t[:, kt, :], in0=xT[:, kt, :], scalar=wcol(kt), in1=rs[:],
                                                         op0=OP.mult, op1=OP.mult), reads=['xT', 'rs', 'cols', 'fnw'], writes=[outres])

    def gate_stage(l, bidx):
        for half in range(2):
            def cb(m, ps, psn, half=half):
                mb = half * 4 + m
                g = gs[mb % 2]; gn = 'gs%d' % (mb % 2)
                T.op('act', lambda e: e.activation(out=g[:], in_=ps[:, 0:N], func=AF.Sigmoid), reads=[psn], writes=[gn])
                if bidx == 0:
                    T.op('dve', lambda e: e.tensor_tensor(out=mrg[:, mb, :], in0=g[:], in1=ybr[:, mb, :], op=OP.mult),
                         reads=[gn, 'ybr'], writes=['mrg'])
                else:
                    T.op('dve', lambda e: e.tensor_tensor(out=g[:], in0=g[:], in1=ybr[:, mb, :], op=OP.mult), reads=[gn, 'ybr'], writes=[gn])
                    if bidx == 1:
                        T.op('dve', lambda e: e.tensor_tensor(out=mrg[:, mb, :], in0=mrg[:, mb, :], in1=g[:], op=OP.add),
                             reads=[gn, 'mrg'], writes=['mrg'])
                    else:
                        T.op('dve', lambda e: e.tensor_tensor(out=mbf[:, mb, :], in0=mrg[:, mb, :], in1=g[:], op=OP.add),
                             reads=[gn, 'mrg'], writes=['mbf'])
            proj_fm('g%d_%d' % (bidx, half), hT, 'hT', 4, cb)

    def branch_out(names, src, srcres):
        for half, nm in enumerate(names):
            def cb(m, ps, psn, half=half):
                T.op('act', lambda e: e.copy(out=ybr[:, half * 4 + m, :], in_=ps[:, 0:N]), reads=[psn], writes=['ybr'])
            proj_fm(nm, src, srcres, 4, cb)

    def V(f, rd=(), wr_=()):
        T.op('dve', f, reads=rd, writes=wr_)

    def A(f, rd=(), wr_=()):
        T.op('act', f, reads=rd, writes=wr_)

    def GP(f, rd=(), wr_=()):
        T.op('pool', f, reads=rd, writes=wr_)

    def PE(f, rd=(), wr_=()):
        T.op('pe', f, reads=rd, writes=wr_)

    rtmp = sb("rtmp", [128, 2, 64])

    def rot(dst, dstres, w_ap, wres, ai, ROT):
        PE(lambda e: e.matmul(PS[6][:, 256:320], lhsT=permF, rhs=w_ap, start=True, stop=True), [wres, 'cF'], ['ps6r'])
        V(lambda e: e.tensor_tensor(out=rtmp[:, 0, :], in0=w_ap, in1=ROT[:, ai, 0, :], op=OP.mult), [wres, 'pkF'], ['rtmp0'])
        V(lambda e: e.tensor_tensor(out=rtmp[:, 1, :], in0=PS[6][:, 256:320], in1=ROT[:, ai, 1, :], op=OP.mult), ['ps6r', 'pkF'], ['rtmp1'])
        V(lambda e: e.tensor_tensor(out=dst, in0=rtmp[:, 0, :], in1=rtmp[:, 1, :], op=OP.add), ['rtmp0', 'rtmp1'], [dstres])

    last_prompt_sc = NPC // NCH - 1

    for sc in range(n_sc):
        chunks = [sc * NCH + c for c in range(NCH)]
        fake = chunks[0] >= NPC
        cs_ = lambda c: slice(c * 128, (c + 1) * 128)
        with ExitStack() as es:
            xtm = es.enter_context(nc.sbuf_tensor(uq("xtm"), [128, D], F32))
            for c, gc in enumerate(chunks):
                if fake:
                    b = gc - NPC
                    V(lambda e: e.memset(xtm[:], 0.0), [], ['xtm'])
                    T.dma('xtm', xtm[0:1, :], xin[cfg.SEQ + b:cfg.SEQ + b + 1, :], writes=['xtm'])
                else:
                    T.dma('xtm', xtm[:], xin[gc * 128:(gc + 1) * 128, :], writes=['xtm'])
                for half in range(2):
                    ps, psn = PS[2 + half], PSN[2 + half]
                    for j in range(4):
                        jj = half * 4 + j
                        PE(lambda e: e.transpose(ps[:, j * 128:(j + 1) * 128], xtm[:, jj * 128:(jj + 1) * 128], identF), ['xtm', 'cF'], [psn])
                    A(lambda e: e.copy(out=xT[:, half * 4:half * 4 + 4, cs_(c)], in_=ps[:].rearrange("p (j t) -> p j t", t=128)), [psn], ['xT'])
            T.barrier()

        for l in range(L):
            T.dma('pkB', pkB[:], packB[l], reads=['packB%d' % l], writes=['pkB'])
            T.dma('pkF', pkF[:], packF[l], reads=['packF%d' % l], writes=['pkF'])
            tab = lambda off: pkB[:, off:off + 2048].rearrange("p (g t) -> p g t", t=32)
            WRt, WIt, CSt_, SNt = tab(PB_WR), tab(PB_WI), tab(PB_CS), tab(PB_SN)
            Bw = pkB[:, PB_BW:PB_BW + 8192].rearrange("p (kt e o n) -> p kt e o n", kt=8, e=4, o=2)
            Cw = pkB[:, PB_CW:PB_CW + 2048].rearrange("p (g o c) -> p g o c", g=64, o=2)
            MULT = pkF[:, PF_MULT:PF_MULT + 64 * 33]
            ROT = pkF[:, PF_ROT:PF_ROT + 384].rearrange("p (a c g) -> p a c g", a=3, c=2)
            n1 = lambda kt: colsT[:, l, kt:kt + 1]
            n2 = lambda kt: colsT[:, l, 8 + kt:9 + kt]
            S_l = Sst[:, l, :]
            Sres = 'S%d' % l

            rmsnorm(n1, hT, 'hT')
            if STOP <= 1:
                T.finish()
                return nc, T

            with ExitStack() as es:
                zs = es.enter_context(nc.sbuf_tensor(uq("zs"), [128, NCH, D], F32))
                xc = es.enter_context(nc.sbuf_tensor(uq("xc"), [128, 16, N], BF16))
                rawb = es.enter_context(nc.sbuf_tensor(uq("rawb"), [128, 2, NCH, 132], BF16))
                dg = es.enter_context(nc.sbuf_tensor(uq("dg"), [128, 2, 4, 128], BF16))
                dtw = es.enter_context(nc.sbuf_tensor(uq("dtw"), [128, NCH, 8, 16], F32))
                xdt = es.enter_context(nc.sbuf_tensor(uq("xdt"), [128, 16, 64], BF16))
                xw = es.enter_context(nc.sbuf_tensor(uq("xw"), [128, 16, 64], BF16))
                xDd = es.enter_context(nc.sbuf_tensor(uq("xDd"), [128, 16, 64], F32))
                btm = es.enter_context(nc.sbuf_tensor(uq("btm"), [128, 512], BF16))
                LT = es.enter_context(nc.sbuf_tensor(uq("LT"), [128, 2, 128], BF16))
                MT = es.enter_context(nc.sbuf_tensor(uq("MT"), [128, 2, 128], BF16))
                yt = es.enter_context(nc.sbuf_tensor(uq("yt"), [128, D], F32))
                Sbf = es.enter_context(nc.sbuf_tensor(uq("Sbf"), [128, D], BF16))
                ynb = es.enter_context(nc.sbuf_tensor(uq("ynb"), [128, D], BF16))
                ssq = es.enter_context(nc.sbuf_tensor(uq("ssq"), [128, 8], F32))
                ctl = es.enter_context(nc.sbuf_tensor(uq("ctl"), [128, NCH, 8, 16], F32))
                hst = es.enter_context(nc.sbuf_tensor(uq("hst"), [128, NCH, 16, 3], F32))
                cio = es.enter_context(nc.sbuf_tensor(uq("cio"), [48, 128], F32))
                sio = es.enter_context(nc.sbuf_tensor(uq("sio"), [128, KT, 128], F32))
                for half in range(2):
                    def cbz(c, ps, psn, half=half):
                        A(lambda e: e.activation(out=zs[:, c, half * 512:(half + 1) * 512], in_=ps[:, 0:512], func=AF.Silu), [psn], ['zs'])
                    proj_tm('z%d' % half, hT, 'hT', 512, cbz)
                if fake:
                    for c, gc in enumerate(chunks):
                        b = gc - NPC
                        T.dma('cio', cio[:], st_conv[l, b].rearrange("k (mb ch) -> (k mb) ch", ch=128), writes=['cio'])
                        PE(lambda e: e.transpose(PS[6][:, 0:48], cio[:], identF[0:48, 0:48]), ['cio', 'cF'], ['ps6'])
                        V(lambda e: e.tensor_copy(hst[:, c, :, :], PS[6][:, 0:48].rearrange("p (k mb) -> p mb k", k=3)), ['ps6'], ['hst'])
                else:
                    if sc == 0:
                        V(lambda e: e.memset(histT[:, l, :, :], 0.0), [], ['hist%d' % l])
                        GP(lambda e: e.memset(S_l, 0.0), [], [Sres])
                    V(lambda e: e.tensor_copy(hst[:, 0, :, :], histT[:, l, :, :]), ['hist%d' % l], ['hst'])
                V(lambda e: e.memset(ctl[:], 0.0), [], ['ctl'])
                for blk in range(4):
                    def cbx(m, ps, psn, blk=blk):
                        mb = blk * 4 + m
                        rb = mb % 2
                        rw = rawb[:, rb, :, :]
                        rwn = 'raw%d' % rb
                        dgn = 'dg%d' % rb
                        psv = ps[:, 0:N].rearrange("p (c t) -> p c t", t=128)
                        A(lambda e: e.copy(out=rw[:, :, 3:131], in_=psv), [psn], [rwn])
                        for k in range(4):
                            V(lambda e: e.tensor_scalar(out=dg[:, rb, k, :], in0=identB[:], scalar1=convT[:, l, mb, k:k + 1], scalar2=None, op0=OP.mult), ['identB', 'conv'], [dgn])
                        for c in range(NCH):
                            if fake or c == 0:
                                V(lambda e: e.tensor_copy(rw[:, c, 0:3], hst[:, c, mb, :]), ['hst', rwn], [rwn])
                            else:
                                V(lambda e: e.tensor_copy(rw[:, c, 0:3], rw[:, c - 1, 128:131]), [rwn], [rwn])
                            cps, cpn = PS[2 + (c % 2)], PSN[2 + (c % 2)]
                            for k in range(4):
                                PE(lambda e: e.matmul(cps[:, 0:128], lhsT=dg[:, rb, k, :], rhs=rw[:, c, k:k + 128], start=(k == 0), stop=(k == 3)), [dgn, rwn], [cpn])
                            A(lambda e: e.activation(out=xc[:, mb, cs_(c)], in_=cps[:, 0:128], func=AF.Silu, bias=convT[:, l, mb, 4:5]), [cpn, 'conv'], ['xc'])
                            if fake:
                                V(lambda e: e.tensor_copy(ctl[:, c, 0:2, mb], hst[:, c, mb, 1:3]), ['hst'], ['ctl'])
                                V(lambda e: e.tensor_copy(ctl[:, c, 2:3, mb], psv[:, c, 0:1]), [psn], ['ctl'])
                        if not fake:
                            V(lambda e: e.tensor_copy(histT[:, l, mb, :], psv[:, NCH - 1, 125:128]), [psn], ['hist%d' % l])
                            if sc == last_prompt_sc:
                                V(lambda e: e.tensor_copy(ctl[:, 0, 0:3, mb], psv[:, NCH - 1, 125:128]), [psn], ['ctl'])
                    proj_fm('xbc%d' % blk, hT, 'hT', 4, cbx, lag=2, banks=[0, 1, 4, 5])
                flush_pend()
                cout = []
                if fake:
                    cout = [(c, o_conv_s[l, chunks[c] - NPC]) for c in range(NCH)]
                elif sc == last_prompt_sc:
                    cout = [(0, o_conv_p[l])]
                for c, dst in cout:
                    PE(lambda e: e.transpose(PS[6][:, 0:128], ctl[:, c, :, :].rearrange("p k mb -> p (k mb)"), identF), ['ctl', 'cF'], ['ps6'])
                    A(lambda e: e.copy(out=cio[:], in_=PS[6][0:48, 0:128]), ['ps6'], ['cio'])
                    T.dma('cio_o', dst.rearrange("k (mb ch) -> (k mb) ch", ch=128), cio[:], reads=['cio'], writes=['o_conv'])
                def cbdt(c, ps, psn):
                    dq = dtw[:, c, :, :]
                    V(lambda e: e.tensor_tensor(out=dq[:, 0, :], in0=ps[:, 0:16], in1=rowsT[:, l, 0, :], op=OP.add), [psn, 'rows'], ['dtw'])
                    V(lambda e: e.tensor_scalar(out=dq[:, 7, :], in0=dq[:, 0, :], scalar1=-1.0, scalar2=None, op0=OP.mult), ['dtw'], ['dtw'])
                    V(lambda e: e.tensor_tensor(out=dq[:, 7, :], in0=dq[:, 7, :], in1=dq[:, 0, :], op=OP.max), ['dtw'], ['dtw'])
                    A(lambda e: e.activation(out=dq[:, 7, :], in_=dq[:, 7, :], func=AF.Exp, scale=-1.0), ['dtw'], ['dtw'])
                    A(lambda e: e.activation(out=dq[:, 7, :], in_=dq[:, 7, :], func=AF.Ln, bias=1.0), ['dtw'], ['dtw'])
                    V(lambda e: e.scalar_tensor_tensor(out=dq[:, 0, :], in0=dq[:, 0, :], scalar=0.0, in1=dq[:, 7, :], op0=OP.max, op1=OP.add), ['dtw'], ['dtw'])
                    if fake:
                        V(lambda e: e.tensor_scalar(out=dq[:, 0, :], in0=dq[:, 0, :], scalar1=cSt[:, 0:1], scalar2=None, op0=OP.mult), ['dtw', 'cS'], ['dtw'])
                proj_tm('dt', hT, 'hT', 16, cbdt)

                for c, gc in enumerate(chunks):
                    dq = dtw[:, c, :, :]
                    dt_, dA, acum, de, cd, ea, nac, tmp = [dq[:, i, :] for i in range(8)]
                    bc64 = lambda ap: ap.unsqueeze(2).to_broadcast([128, 16, 64])
                    if fake:
                        b = gc - NPC
                        T.dma('sio', sio[:], st_ssm[l, b].rearrange("(j q) n -> q j n", q=128), writes=['sio'])
                        for half in range(2):
                            for j in range(4):
                                PE(lambda e: e.transpose(PS[4 + half][:, j * 128:(j + 1) * 128], sio[:, half * 4 + j, :], identF), ['sio', 'cF'], [PSN[4 + half]])
                            A(lambda e: e.copy(out=Sst[:, l, half * 512:(half + 1) * 512], in_=PS[4 + half][:]), [PSN[4 + half]], [Sres])
                    V(lambda e: e.tensor_tensor(out=dA, in0=dt_, in1=Abc[:, l, :], op=OP.mult), ['dtw', 'Abc'], ['dtw'])
                    PE(lambda e: e.matmul(PS[2][:, 0:16], lhsT=triF, rhs=dA, start=True, stop=True), ['dtw', 'cF'], ['ps2a'])
                    PE(lambda e: e.matmul(PS[2][:, 16:32], lhsT=onesF, rhs=dA, start=True, stop=True), ['dtw', 'cF'], ['ps2a'])
                    A(lambda e: e.copy(out=acum, in_=PS[2][:, 0:16]), ['ps2a'], ['dtw'])
                    V(lambda e: e.tensor_tensor(out=tmp, in0=PS[2][:, 16:32], in1=acum, op=OP.subtract), ['ps2a', 'dtw'], ['dtw'])
                    A(lambda e: e.activation(out=de, in_=tmp, func=AF.Exp), ['dtw'], ['dtw'])
                    A(lambda e: e.activation(out=cd, in_=PS[2][:, 16:32], func=AF.Exp), ['ps2a'], ['dtw'])
                    A(lambda e: e.activation(out=ea, in_=acum, func=AF.Exp), ['dtw'], ['dtw'])
                    V(lambda e: e.tensor_scalar(out=nac, in0=acum, scalar1=-1.0, scalar2=None, op0=OP.mult), ['dtw'], ['dtw'])
                    for j in range(8):
                        PE(lambda e: e.transpose(PSB[:, j * 128:(j + 1) * 128], xc[:, j, cs_(c)], identB[:]), ['xc', 'identB'], ['psb_lo', 'psb_hi'])
                    pv = PSB[:].rearrange("p (h d) -> p h d", d=64)
                    V(lambda e: e.tensor_tensor(out=xdt[:], in0=pv, in1=bc64(dt_), op=OP.mult), ['psb_lo', 'psb_hi', 'dtw'], ['xdt'])
                    V(lambda e: e.tensor_tensor(out=xDd[:], in0=pv, in1=bc64(rowsT[:, l, 2, :]), op=OP.mult), ['psb_lo', 'psb_hi', 'rows'], ['xDd'])
                    V(lambda e: e.tensor_tensor(out=xw[:], in0=xdt[:], in1=bc64(de), op=OP.mult), ['xdt', 'dtw'], ['xw'])
                    for g in range(4):
                        PE(lambda e: e.transpose(PSB[:, g * 128:(g + 1) * 128], xc[:, 8 + g, cs_(c)], identB[:]), ['xc', 'identB'], ['psb_lo', 'psb_hi'])
                    A(lambda e: e.copy(out=btm[:], in_=PSB[:, 0:512]), ['psb_lo', 'psb_hi'], ['btm'])
                    for g in range(4):
                        PE(lambda e: e.matmul(PS[3][:, g * 128:(g + 1) * 128], lhsT=xc[:, 8 + g, cs_(c)], rhs=xc[:, 12 + g, cs_(c)], start=True, stop=True),
                           ['xc'], ['ps3'])
                    A(lambda e: e.copy(out=Sbf[:], in_=S_l), [Sres], ['Sbf'])
                    for g in range(4):
                        PE(lambda e: e.matmul(PS[g // 2][:, (g % 2) * 256:(g % 2) * 256 + 256], lhsT=xc[:, 12 + g, cs_(c)],
                                              rhs=Sbf[:, g * 256:(g + 1) * 256], start=True, stop=True), ['xc', 'Sbf'], [PSN[g // 2]])
                    for half in range(2):
                        yv = yt[:, half * 512:(half + 1) * 512].rearrange("p (h d) -> p h d", d=64)
                        V(lambda e: e.tensor_tensor(out=yv, in0=PS[half][:].rearrange("p (h d) -> p h d", d=64),
                                                    in1=ea[:, half * 8:half * 8 + 8].unsqueeze(2).to_broadcast([128, 8, 64]), op=OP.mult),
                          [PSN[half], 'dtw'], ['yt'])
                    def ssd_A(h):
                        pb = h % 2
                        LTp = PS[pb][:, 0:128]
                        ln = PSN[pb]
                        PE(lambda e: e.matmul(LTp, lhsT=dA[:, h:h + 1].to_broadcast([128, 128]), rhs=triF, start=True, stop=False), ['dtw', 'cF'], [ln])
                        PE(lambda e: e.matmul(LTp, lhsT=identF, rhs=negF, start=False, stop=True), ['cF'], [ln])
                        A(lambda e: e.activation(out=LT[:, pb, :], in_=LTp, func=AF.Exp, bias=nac[:, h:h + 1]), [ln, 'dtw'], ['LT%d' % pb])

                    def ssd_B(h):
                        g = h // 4
                        pb = h % 2
                        V(lambda e: e.tensor_tensor(out=MT[:, pb, :], in0=PS[3][:, g * 128:(g + 1) * 128], in1=LT[:, pb, :], op=OP.mult),
                          ['ps3', 'LT%d' % pb], ['MT%d' % pb])
                        PE(lambda e: e.matmul(PS[4 + h // 8][:, (h % 8) * 64:(h % 8) * 64 + 64], lhsT=MT[:, pb, :], rhs=xdt[:, h, :], start=True, stop=True),
                           ['MT%d' % pb, 'xdt'], [PSN[4 + h // 8]])

                    if fake:
                        for g in range(4):
                            V(lambda e: e.scalar_tensor_tensor(out=yt[0:1, g * 256:(g + 1) * 256], in0=xdt[0:1, 4 * g:4 * g + 4, :].rearrange("p h d -> p (h d)"),
                                                               scalar=PS[3][0:1, g * 128:g * 128 + 1], in1=yt[0:1, g * 256:(g + 1) * 256],
                                                               op0=OP.mult, op1=OP.add), ['xdt', 'ps3', 'yt'], ['yt'])
                    else:
                        for i in range(16 + 1):
                            if i < 16:
                                ssd_A(i)
                            if i >= 1:
                                ssd_B(i - 1)
                    if not fake:
                        for half in range(2):
                            V(lambda e: e.tensor_tensor(out=yt[:, half * 512:(half + 1) * 512], in0=PS[4 + half][:], in1=yt[:, half * 512:(half + 1) * 512], op=OP.add),
                              [PSN[4 + half], 'yt'], ['yt'])
                    V(lambda e: e.tensor_tensor(out=yt[:], in0=yt[:], in1=xDd[:].rearrange("p h d -> p (h d)"), op=OP.add), ['yt', 'xDd'], ['yt'])
                    for g in range(4):
                        PE(lambda e: e.matmul(PS[4 + g // 2][:, (g % 2) * 256:(g % 2) * 256 + 256], lhsT=btm[:, g * 128:(g + 1) * 128],
                                              rhs=xw[:, 4 * g:4 * g + 4, :].rearrange("p h d -> p (h d)"), start=True, stop=True), ['btm', 'xw'], [PSN[4 + g // 2]])
                    S3 = S_l.rearrange("p (h d) -> p h d", d=64)
                    V(lambda e: e.tensor_tensor(out=S3, in0=S3, in1=bc64(cd), op=OP.mult), [Sres, 'dtw'], [Sres])
                    for half in range(2):
                        V(lambda e: e.tensor_tensor(out=Sst[:, l, half * 512:(half + 1) * 512], in0=PS[4 + half][:], in1=Sst[:, l, half * 512:(half + 1) * 512], op=OP.add),
                          [PSN[4 + half], Sres], [Sres])
                    sdst = None
                    if fake:
                        sdst = o_ssm_s[l, gc - NPC]
                    elif sc == last_prompt_sc and c == NCH - 1:
                        sdst = o_ssm_p[l]
                    if sdst is not None:
                        for half in range(2):
                            for j in range(4):
                                jj = half * 4 + j
                                PE(lambda e: e.transpose(PS[half][:, j * 128:(j + 1) * 128], Sst[:, l, jj * 128:(jj + 1) * 128], identF), [Sres, 'cF'], [PSN[half]])
                            A(lambda e: e.copy(out=sio[:, half * 4:half * 4 + 4, :], in_=PS[half][:].rearrange("p (j n) -> p j n", n=128)), [PSN[half]], ['sio'])
                        T.dma('sio_o', sdst.rearrange("(j q) n -> q j n", q=128), sio[:], reads=['sio'], writes=['o_ssm'])
                    V(lambda e: e.tensor_tensor(out=yt[:], in0=yt[:], in1=zs[:, c, :], op=OP.mult), ['yt', 'zs'], ['yt'])
                    for g in range(4):
                        A(lambda e: e.activation(out=ynb[:, g * 256:(g + 1) * 256], in_=yt[:, g * 256:(g + 1) * 256], func=AF.Square, accum_out=ssq[:, g:g + 1]),
                          ['yt'], ['ynb', 'ssq'])
                    A(lambda e: e.activation(out=ssq[:, 4:8], in_=ssq[:, 0:4], func=AF.Sqrt, bias=EPS, scale=1.0 / 256.0), ['ssq'], ['ssq'])
                    V(lambda e: e.reciprocal(out=ssq[:, 4:8], in_=ssq[:, 4:8]), ['ssq'], ['ssq'])
                    V(lambda e: e.tensor_tensor(out=ynb[:].rearrange("p (g d) -> p g d", d=256), in0=yt[:].rearrange("p (g d) -> p g d", d=256),
                                                in1=ssq[:, 4:8].unsqueeze(2).to_broadcast([128, 4, 256]), op=OP.mult), ['yt', 'ssq', 'ynb'], ['ynb'])
                    for j in range(8):
                        PE(lambda e: e.transpose(PSB[:, j * 128:(j + 1) * 128], ynb[:, j * 128:(j + 1) * 128], identB[:]), ['ynb', 'identB'], ['psb_lo', 'psb_hi'])
                    V(lambda e: e.tensor_tensor(out=brf[:, :, cs_(c)], in0=PSB[:].rearrange("p (j t) -> p j t", t=128),
                                                in1=colsT[:, l, 16:24].unsqueeze(2).to_broadcast([128, 8, 128]), op=OP.mult), ['psb_lo', 'psb_hi', 'cols'], ['brf'])
                branch_out(['mp0', 'mp1'], brf, 'brf')
                dbg_dump(sc, l, 0, ybr)
                gate_stage(l, 0)
                T.barrier()

            if STOP <= 2:
                T.finish()
                return nc, T
            with ExitStack() as es:
                ufm = es.enter_context(nc.sbuf_tensor(uq("ufm"), [128, KT, N], BF16))
                Zt = es.enter_context(nc.sbuf_tensor(uq("Zt"), [128, 64, 33], F32))
                Gt = es.enter_context(nc.sbuf_tensor(uq("Gt"), [128, 64, 33], F32))
                tz = es.enter_context(nc.sbuf_tensor(uq("tz"), [128, 2, 2, 16, 32], BF16))
                Pp = es.enter_context(nc.sbuf_tensor(uq("Pp"), [128, 64, 32], BF16))
                Qp = es.enter_context(nc.sbuf_tensor(uq("Qp"), [128, 64, 32], BF16))
                ysb = es.enter_context(nc.sbuf_tensor(uq("ysb"), [32, D], F32))
                pre = es.enter_context(nc.sbuf_tensor(uq("pre"), [128, 2, 8, 32], F32))
                s5io = es.enter_context(nc.sbuf_tensor(uq("s5io"), [64, 128], F32))
                hcol2 = es.enter_context(nc.sbuf_tensor(uq("hcol"), [128, 128], F32))
                hcol = hcol2[:, 0:64]
                V(lambda e: e.memset(hcol2[:], 0.0), [], ['hcol'])
                for half in range(2):
                    def cbu(m, ps, psn, half=half):
                        A(lambda e: e.copy(out=ufm[:, half * 4 + m, :], in_=ps[:, 0:N]), [psn], ['ufm'])
                    proj_fm('u%d' % half, hT, 'hT', 4, cbu)
                cres = 's5c%d' % l
                NSUB = 128 // TS5
                U = NCH * NSUB

                def s5_aq(u, qd):
                    c, s = divmod(u, NSUB)
                    t0 = c * 128 + TS5 * s
                    hh, kq = ((0, 0), (1, 0), (0, 1), (1, 1))[qd]
                    tp = qd % 2
                    b1, b2 = 2 * hh, 2 * hh + 1
                    for i in range(16):
                        kt, e_ = 4 * kq + i // 4, i % 4
                        for o, bb in ((0, b1), (1, b2)):
                            PE(lambda e: e.matmul(PS[bb][:, i * 32:(i + 1) * 32], lhsT=Bw[64 * hh:64 * hh + 64, kt, e_, o, :],
                                                  rhs=ufm[64 * hh:64 * hh + 64, kt, t0:t0 + TS5], start=True, stop=True), ['pkB', 'ufm'], [PSN[bb]])
                    gsel = lambda tb: tb.rearrange("p (k h e) t -> p k h e t", h=2, e=4)[:, 4 * kq:4 * kq + 4, hh, :, :]
                    pv4 = lambda bb: PS[bb][:].rearrange("p (k e t) -> p k e t", e=4, t=32)
                    tz4 = lambda j: tz[:, tp, j, :, :].rearrange("p (k e) t -> p k e t", e=4)
                    V(lambda e: e.tensor_tensor(out=tz4(0), in0=pv4(b1), in1=gsel(WRt), op=OP.mult), [PSN[b1], 'pkB'], ['tz0%d' % tp])
                    V(lambda e: e.tensor_tensor(out=tz4(1), in0=pv4(b2), in1=gsel(WIt), op=OP.mult), [PSN[b2], 'pkB'], ['tz1%d' % tp])
                    V(lambda e: e.tensor_tensor(out=gsel(Zt[:])[:, :, :, 1:33], in0=tz4(0), in1=tz4(1), op=OP.add), ['tz0%d' % tp, 'tz1%d' % tp], ['Zt'])

                def s5_b(u):
                    c, s = divmod(u, NSUB)
                    gc = chunks[c]
                    if s == 0:
                        if fake:
                            b = gc - NPC
                            T.dma('s5io', s5io[:].rearrange("g (r n) -> g r n", r=2), st_s5[l, b].rearrange("r g n -> g r n"), writes=['s5io'])
                            PE(lambda e: e.transpose(PS[6][:, 448:512], s5io[:], identF[0:64, 0:64]), ['s5io', 'cF'], ['ps6t'])
                            A(lambda e: e.copy(out=hcol, in_=PS[6][:, 448:512]), ['ps6t'], ['hcol'])
                            rot(s5c[:, l, :], cres, hcol, 'hcol', 0, ROT)
                        elif sc == 0 and c == 0:
                            V(lambda e: e.memset(s5c[:, l, :], 0.0), [], [cres])
                    V(lambda e: e.tensor_copy(Zt[:, :, 0], s5c[:, l, :]), [cres], ['Zt'])
                    V(lambda e: e.tensor_tensor_scan(out=Gt[:].rearrange("p g t -> p (g t)"), data0=MULT, data1=Zt[:].rearrange("p g t -> p (g t)"),
                                                     initial=0.0, op0=OP.mult, op1=OP.add), ['Zt', 'pkF'], ['Gt'])
                    rot(s5c[:, l, :], cres, Gt[:, :, 32], 'Gt', 2, ROT)
                    sdst = None
                    if fake and s == 0:
                        sdst = o_s5_s[l, gc - NPC]
                        V(lambda e: e.tensor_copy(hcol, Gt[:, :, 1]), ['Gt'], ['hcol'])
                    elif (not fake) and sc == last_prompt_sc and c == NCH - 1 and s == NSUB - 1:
                        sdst = o_s5_p[l]
                        rot(hcol, 'hcol', Gt[:, :, 32], 'Gt', 1, ROT)
                    if sdst is not None:
                        PE(lambda e: e.transpose(PS[6][:, 320:448], hcol2[:], identF), ['hcol', 'cF'], ['ps6s'])
                        A(lambda e: e.copy(out=s5io[:], in_=PS[6][0:64, 320:448]), ['ps6s'], ['s5io'])
                        T.dma('s5io_o', sdst.rearrange("r g n -> g r n"), s5io[:].rearrange("g (r n) -> g r n", r=2), reads=['s5io'], writes=['o_s5'])

                def s5_pq(u):
                    V(lambda e: e.tensor_tensor(out=Pp[:], in0=Gt[:, :, 1:33], in1=CSt_, op=OP.mult), ['Gt', 'pkB'], ['Pp'])
                    V(lambda e: e.tensor_tensor(out=Qp[:], in0=Gt[:, :, 1:33], in1=SNt, op=OP.mult), ['Gt', 'pkB'], ['Qp'])

                def s5_d(u):
                    c, s = divmod(u, NSUB)
                    t0 = c * 128 + TS5 * s
                    for g in range(64):
                        bank, bn = PS[4 + g // 32], PSN[4 + g // 32]
                        col = (g % 32) * 16
                        PE(lambda e: e.matmul(bank[0:32, col:col + 16], lhsT=Pp[:, g, :], rhs=Cw[:, g, 0, :], start=True, stop=False), ['Pp', 'pkB'], [bn])
                        PE(lambda e: e.matmul(bank[0:32, col:col + 16], lhsT=Qp[:, g, :], rhs=Cw[:, g, 1, :], start=False, stop=True), ['Qp', 'pkB'], [bn])
                    for half in range(2):
                        A(lambda e: e.copy(out=ysb[:, half * 512:(half + 1) * 512], in_=PS[4 + half][0:32, :]), [PSN[4 + half]], ['ysb'])
                    for j in range(8):
                        PE(lambda e: e.transpose(PS[6][:, j * 32:(j + 1) * 32], ysb[:, j * 128:(j + 1) * 128], identF[0:32, 0:32]), ['ysb', 'cF'], ['ps6'])

                def s5_d2(u):
                    c, s = divmod(u, NSUB)
                    t0 = c * 128 + TS5 * s
                    p0 = pre[:, 0, :, :]
                    p1 = pre[:, 1, :, :]
                    V(lambda e: e.tensor_tensor(out=p0, in0=ufm[:, :, t0:t0 + TS5], in1=colsT[:, l, 24:32].unsqueeze(2).to_broadcast([128, 8, 32]), op=OP.mult),
                      ['ufm', 'cols'], ['pre0'])
                    V(lambda e: e.tensor_tensor(out=p0, in0=PS[6][:, 0:256].rearrange("p (j t) -> p j t", t=32), in1=p0, op=OP.add), ['ps6', 'pre0'], ['pre0'])
                    V(lambda e: e.tensor_tensor(out=p1, in0=p0, in1=p0, op=OP.mult), ['pre0'], ['pre1'])
                    V(lambda e: e.tensor_scalar(out=p1, in0=p1, scalar1=0.044715, scalar2=1.0, op0=OP.mult, op1=OP.add), ['pre1'], ['pre1'])
                    V(lambda e: e.tensor_tensor(out=p1, in0=p1, in1=p0, op=OP.mult), ['pre0', 'pre1'], ['pre1'])
                    A(lambda e: e.activation(out=p1, in_=p1, func=AF.Sigmoid, scale=1.5957691215), ['pre1'], ['pre1'])
                    V(lambda e: e.tensor_tensor(out=brf[:, :, t0:t0 + TS5], in0=p0, in1=p1, op=OP.mult), ['pre0', 'pre1'], ['brf'])

                ulist = [c * NSUB + s for c in range(NCH) for s in range(1 if fake else NSUB)]
                for qd in range(4):
                    s5_aq(ulist[0], qd)
                s5_b(ulist[0])
                for ui, u in enumerate(ulist):
                    nxt = ulist[ui + 1] if ui + 1 < len(ulist) else None
                    if nxt is not None:
                        s5_aq(nxt, 0)
                        s5_aq(nxt, 1)
                    s5_pq(u)
                    if nxt is not None:
                        s5_aq(nxt, 2)
                        s5_aq(nxt, 3)
                    s5_d(u)
                    if nxt is not None:
                        s5_b(nxt)
                    s5_d2(u)
                if DBG:
                    A(lambda e: e.copy(out=ybr[:], in_=brf[:]), ['brf'], ['ybr'])
                    dbg_dump(sc, l, 3, ybr)
                for b2 in range(2):
                    wv, wvres = w_next('gv%d' % b2)
                    wg, wgres = w_next('gg%d' % b2, ahead=2)
                    for m in range(4):
                        mb = b2 * 4 + m
                        for kt in range(KT):
                            PE(lambda e: e.matmul(PS[0][:, 0:N], lhsT=wv[:, kt, m * 128:(m + 1) * 128], rhs=brf[:, kt, :], start=(kt == 0), stop=(kt == KT - 1)),
                               [wvres, 'brf'], ['ps0'])
                        for kt in range(KT):
                            PE(lambda e: e.matmul(PS[1][:, 0:N], lhsT=wg[:, kt, m * 128:(m + 1) * 128], rhs=brf[:, kt, :], start=(kt == 0), stop=(kt == KT - 1)),
                               [wgres, 'brf'], ['ps1'])
                        g_ = gs[mb % 2]
                        gn = 'gs%d' % (mb % 2)
                        A(lambda e: e.activation(out=g_[:], in_=PS[1][:, 0:N], func=AF.Sigmoid), ['ps1'], [gn])
                        V(lambda e: e.tensor_tensor(out=ybr[:, mb, :], in0=PS[0][:, 0:N], in1=g_[:], op=OP.mult), ['ps0', gn], ['ybr'])
                dbg_dump(sc, l, 1, ybr)
                gate_stage(l, 1)
                T.barrier()

            if STOP <= 3:
                T.finish()
                return nc, T
            with ExitStack() as es:
                qtm = es.enter_context(nc.sbuf_tensor(uq("qtm"), [128, NCH, D], BF16))
                ktm = es.enter_context(nc.sbuf_tensor(uq("ktm"), [128, NCH, 256], F32))
                kdup = es.enter_context(nc.sbuf_tensor(uq("kdup"), [128, 4, 2, 64], BF16))
                vtmf = es.enter_context(nc.sbuf_tensor(uq("vtmf"), [128, NCH, 256], F32))
                qfm = es.enter_context(nc.sbuf_tensor(uq("qfm"), [128, 8, 128], BF16))
                smx = es.enter_context(nc.sbuf_tensor(uq("smx"), [128, 3, 258], F32))
                pbt = es.enter_context(nc.sbuf_tensor(uq("pbt"), [128, 3, 258], BF16))
                pT = es.enter_context(nc.sbuf_tensor(uq("pT"), [128, 3, 256], BF16))
                otm = es.enter_context(nc.sbuf_tensor(uq("otm"), [128, D], BF16))
                ast = es.enter_context(nc.sbuf_tensor(uq("ast"), [128, 3, 8], F32))
                rpt = es.enter_context(nc.sbuf_tensor(uq("rpt"), [128, 2, 8, 8], F32))
                ckt = es.enter_context(nc.sbuf_tensor(uq("ckt"), [128, 256], F32))

                def rope(psv, psn, dstv, dres, ci, nh):
                    cosb = cRt[:, ci, 0:8].unsqueeze(1).to_broadcast([128, nh, 8])
                    sinb = cRt[:, ci, 8:16].unsqueeze(1).to_broadcast([128, nh, 8])
                    ta = rpt[:, 0, 0:nh, :]
                    tb = rpt[:, 1, 0:nh, :]
                    V(lambda e: e.tensor_tensor(out=ta, in0=psv[:, :, 0:8], in1=cosb, op=OP.mult), [psn, 'cR'], ['rpt0'])
                    V(lambda e: e.tensor_tensor(out=tb, in0=psv[:, :, 8:16], in1=sinb, op=OP.mult), [psn, 'cR'], ['rpt1'])
                    V(lambda e: e.tensor_tensor(out=dstv[:, :, 0:8], in0=ta, in1=tb, op=OP.subtract), ['rpt0', 'rpt1'], [dres])
                    V(lambda e: e.tensor_tensor(out=ta, in0=psv[:, :, 8:16], in1=cosb, op=OP.mult), [psn, 'cR'], ['rpt0'])
                    V(lambda e: e.tensor_tensor(out=tb, in0=psv[:, :, 0:8], in1=sinb, op=OP.mult), [psn, 'cR'], ['rpt1'])
                    V(lambda e: e.tensor_tensor(out=dstv[:, :, 8:16], in0=ta, in1=tb, op=OP.add), ['rpt0', 'rpt1'], [dres])

                for half in range(2):
                    def cbq(c, ps, psn, half=half):
                        ci = NPC if fake else chunks[c]
                        psv = ps[:, 0:512].rearrange("p (h d) -> p h d", d=64)
                        dstv = qtm[:, c, half * 512:(half + 1) * 512].rearrange("p (h d) -> p h d", d=64)
                        A(lambda e: e.copy(out=dstv, in_=psv), [psn], ['qtm'])
                        rope(psv, psn, dstv, 'qtm', ci, 8)
                    proj_tm('q%d' % half, hT, 'hT', 512, cbq)

                def cbkv(c, ps, psn):
                    ci = NPC if fake else chunks[c]
                    psv = ps[:, 0:256].rearrange("p (h d) -> p h d", d=64)
                    dstv = ktm[:, c, :].rearrange("p (h d) -> p h d", d=64)
                    A(lambda e: e.copy(out=dstv, in_=psv), [psn], ['ktm'])
                    A(lambda e: e.copy(out=vtmf[:, c, :], in_=ps[:, 256:512]), [psn], ['vtmf'])
                    rope(psv, psn, dstv, 'ktm', ci, 4)
                proj_tm('kv', hT, 'hT', 512, cbkv)

                kres, vres = 'kcat%d' % l, 'vcat%d' % l
                for c, gc in enumerate(chunks):
                    first_chunk = (not fake) and gc == 0
                    k3 = lambda t2d: t2d.rearrange("p (h d) -> p h d", d=64)
                    if fake:
                        b = gc - NPC
                        T.dma('ckt', ckt[:], st_k[l, b], writes=['ckt'])
                        V(lambda e: e.tensor_copy(kdup[:, :, 0, :], k3(ckt[:])), ['ckt'], ['kdup'])
                        A(lambda e: e.copy(out=kdup[:, :, 1, :], in_=k3(ckt[:])), ['ckt'], ['kdup'])
                        for kh in range(4):
                            PE(lambda e: e.transpose(PSB[:, kh * 128:(kh + 1) * 128], kdup[:, kh, :, :].rearrange("p a d -> p (a d)"), identB[:]), ['kdup', 'identB'], ['psb_lo', 'psb_hi'])
                        A(lambda e: e.copy(out=kcat[:, l, :, 0:128], in_=PSB[:, 0:512].rearrange("p (h t) -> p h t", t=128)), ['psb_lo', 'psb_hi'], [kres])
                        T.dma('ckt', ckt[:], st_v[l, b], reads=['kdup'], writes=['ckt'])
                        V(lambda e: e.tensor_copy(vcat[:, l, 0, :], ckt[:]), ['ckt'], [vres])
                        T.dma('kv_d2dk', o_k_s[l, b, 0:127, :], st_k[l, b, 1:128, :], writes=['o_kv'])
                        T.dma('kv_d2dv', o_v_s[l, b, 0:127, :], st_v[l, b, 1:128, :], writes=['o_kv'])
                        T.dma('kv_rowk', o_k_s[l, b, 127:128, :], ktm[0:1, c, :], reads=['ktm'], writes=['o_kv'])
                        T.dma('kv_rowv', o_v_s[l, b, 127:128, :], vtmf[0:1, c, :], reads=['vtmf'], writes=['o_kv'])
                    elif first_chunk:
                        V(lambda e: e.memset(kcat[:, l, :, 0:128], 0.0), [], [kres])
                        GP(lambda e: e.memset(vcat[:, l, 0, :], 0.0), [], [vres])
                    if (not fake) and sc == last_prompt_sc and c == NCH - 1:
                        T.dma('kv_rowk', o_k_p[l], ktm[:, c, :], reads=['ktm'], writes=['o_kv'])
                        T.dma('kv_rowv', o_v_p[l], vtmf[:, c, :], reads=['vtmf'], writes=['o_kv'])
                    A(lambda e: e.copy(out=vcat[:, l, 1, :], in_=vtmf[:, c, :]), ['vtmf'], [vres])
                    V(lambda e: e.tensor_copy(kdup[:, :, 0, :], k3(ktm[:, c, :])), ['ktm'], ['kdup'])
                    A(lambda e: e.copy(out=kdup[:, :, 1, :], in_=k3(ktm[:, c, :])), ['ktm'], ['kdup'])
                    for kh in range(4):
                        PE(lambda e: e.transpose(PSB[:, kh * 128:(kh + 1) * 128], kdup[:, kh, :, :].rearrange("p a d -> p (a d)"), identB[:]), ['kdup', 'identB'], ['psb_lo', 'psb_hi'])
                    A(lambda e: e.copy(out=kcat[:, l, :, 128:256], in_=PSB[:, 0:512].rearrange("p (h t) -> p h t", t=128)), ['psb_lo', 'psb_hi'], [kres])
                    for j in range(8):
                        PE(lambda e: e.transpose(PSB[:, j * 128:(j + 1) * 128], qtm[:, c, j * 128:(j + 1) * 128], identB[:]), ['qtm', 'identB'], ['psb_lo', 'psb_hi'])
                    A(lambda e: e.copy(out=qfm[:], in_=PSB[:].rearrange("p (j t) -> p j t", t=128)), ['psb_lo', 'psb_hi'], ['qfm'])
                    mask = cMt[:, 0 if first_chunk else 1, :]
                    def att_S(h):
                        kh, hh, j, par = h // 4, h % 2, h // 2, h % 2
                        sps, spn = (PS[6], 'ps6') if par == 0 else (PS[3], 'ps3')
                        PE(lambda e: e.matmul(sps[:, 0:256], lhsT=qfm[64 * hh:64 * hh + 64, j, :], rhs=kcat[64 * hh:64 * hh + 64, l, kh, :], start=True, stop=True),
                           ['qfm', kres], [spn])

                    def att_A(h):
                        kh, hh, j, par, b3 = h // 4, h % 2, h // 2, h % 2, h % 3
                        sps, spn = (PS[6], 'ps6') if par == 0 else (PS[3], 'ps3')
                        st = ast[:, b3, :]
                        stn = 'ast%d' % b3
                        V(lambda e: e.scalar_tensor_tensor(out=smx[:, b3, 0:256], in0=sps[:, 0:256], scalar=HD ** -0.5, in1=mask, op0=OP.mult, op1=OP.add),
                          [spn, 'cM'], ['smx%d' % b3])
                        V(lambda e: e.tensor_copy(smx[:, b3, 256:257], rowsT[:, l, 3, h:h + 1]), ['rows', 'smx%d' % b3], ['smx%d' % b3])
                        V(lambda e: e.tensor_reduce(out=st[:, 1:2], in_=smx[:, b3, 0:257], axis=AX.X, op=OP.max, negate=True), ['smx%d' % b3], [stn])

                    def att_B(h):
                        par, b3 = h % 2, h % 3
                        st = ast[:, b3, :]
                        stn = 'ast%d' % b3
                        A(lambda e: e.activation(out=pbt[:, b3, 0:257], in_=smx[:, b3, 0:257], func=AF.Exp, bias=st[:, 1:2], accum_out=st[:, 2:3]),
                          ['smx%d' % b3, stn], ['pbt%d' % b3, stn])
                        V(lambda e: e.reciprocal(out=st[:, 4:5], in_=st[:, 2:3]), [stn], [stn])
                        ptp = PSB if par == 0 else PS2B
                        ptn = 'psb_lo' if par == 0 else 'ps2'
                        for kb in range(2):
                            PE(lambda e: e.transpose(ptp[:, kb * 128:(kb + 1) * 128], pbt[:, b3, kb * 128:(kb + 1) * 128], identB[:]),
                               ['pbt%d' % b3, 'identB'], [ptn])

                    def att_C(h):
                        kh, par, b3 = h // 4, h % 2, h % 3
                        st = ast[:, b3, :]
                        stn = 'ast%d' % b3
                        ptp = PSB if par == 0 else PS2B
                        ptn = 'psb_lo' if par == 0 else 'ps2'
                        V(lambda e: e.tensor_copy(pT[:, b3, :], ptp[:, 0:256]), [ptn], ['pT%d' % b3])
                        ob, obn = PS[4 + par], PSN[4 + par]
                        oc = (h // 2) * 64
                        PE(lambda e: e.matmul(ob[:, oc:oc + 64], lhsT=pT[:, b3, 0:128], rhs=vcat[:, l, 0, kh * 64:(kh + 1) * 64], start=True, stop=False),
                           ['pT%d' % b3, vres], [obn])
                        PE(lambda e: e.matmul(ob[:, oc:oc + 64], lhsT=pT[:, b3, 128:256], rhs=vcat[:, l, 1, kh * 64:(kh + 1) * 64], start=False, stop=True),
                           ['pT%d' % b3, vres], [obn])
                        A(lambda e: e.activation(out=otm[:, h * 64:(h + 1) * 64], in_=ob[:, oc:oc + 64], func=AF.Copy, scale=st[:, 4:5]), [obn, stn], ['otm'])

                    att_S(0)
                    for i in range(16 + 2):
                        if i + 1 < 16:
                            att_S(i + 1)
                        if i < 16:
                            att_A(i)
                        if 0 <= i - 1 < 16:
                            att_B(i - 1)
                        if 0 <= i - 2 < 16:
                            att_C(i - 2)
                    for j in range(8):
                        PE(lambda e: e.transpose(PSB[:, j * 128:(j + 1) * 128], otm[:, j * 128:(j + 1) * 128], identB[:]), ['otm', 'identB'], ['psb_lo', 'psb_hi'])
                    A(lambda e: e.copy(out=brf[:, :, cs_(c)], in_=PSB[:].rearrange("p (j t) -> p j t", t=128)), ['psb_lo', 'psb_hi'], ['brf'])
                    if not fake:
                        V(lambda e: e.tensor_copy(kcat[:, l, :, 0:128], kcat[:, l, :, 128:256]), [kres], [kres])
                        A(lambda e: e.copy(out=vcat[:, l, 0, :], in_=vcat[:, l, 1, :]), [vres], [vres])
                branch_out(['ao0', 'ao1'], brf, 'brf')
                dbg_dump(sc, l, 2, ybr)
                gate_stage(l, 2)
                T.barrier()

            if STOP <= 4:
                T.finish()
                return nc, T
            for half in range(2):
                def cbo(m, ps, psn, half=half):
                    mb = half * 4 + m
                    V(lambda e: e.tensor_tensor(out=xT[:, mb, :], in0=ps[:, 0:N], in1=xT[:, mb, :], op=OP.add), [psn, 'xT'], ['xT'])
                proj_fm('wo%d' % half, mbf, 'mbf', 4, cbo)
            rmsnorm(n2, hT, 'hT')
            with ExitStack() as es:
                actT = es.enter_context(nc.sbuf_tensor(uq("actT"), [128, 32, N], BF16))
                for i in range(8):
                    def cbup(m, ps, psn, i=i):
                        mb = i * 4 + m
                        g_ = gs[mb % 2]
                        gn = 'gs%d' % (mb % 2)
                        A(lambda e: e.activation(out=g_[:], in_=ps[:, 0:N], func=AF.Relu), [psn], [gn])
                        A(lambda e: e.activation(out=actT[:, mb, :], in_=g_[:], func=AF.Square), [gn], ['actT'])
                    proj_fm('up%d' % i, hT, 'hT', 4, cbup)
                for i in range(8):
                    wt, wres = w_next('dn%d' % i)
                    ps, psn = pm_next()
                    for kt in range(32):
                        PE(lambda e: e.matmul(ps[:, 0:N], lhsT=wt[:, kt, :], rhs=actT[:, kt, :], start=(kt == 0), stop=(kt == 31)), [wres, 'actT'], [psn])
                    V(lambda e: e.tensor_tensor(out=xT[:, i, :], in0=ps[:, 0:N], in1=xT[:, i, :], op=OP.add), [psn, 'xT'], ['xT'])
                T.barrier()

        rmsnorm(lambda kt: fnw[:, kt:kt + 1], ybr, 'ybr')
        with ExitStack() as es:
            yout = es.enter_context(nc.sbuf_tensor(uq("yout"), [128, D], F32))
            for c, gc in enumerate(chunks):
                for half in range(2):
                    for j in range(4):
                        PE(lambda e: e.transpose(PS[4 + half][:, j * 128:(j + 1) * 128], ybr[:, half * 4 + j, cs_(c)], identF), ['ybr', 'cF'], [PSN[4 + half]])
                    A(lambda e: e.copy(out=yout[:, half * 512:(half + 1) * 512], in_=PS[4 + half][:]), [PSN[4 + half]], ['yout'])
                if fake:
                    b = gc - NPC
                    T.dma('yout', o_y[cfg.SEQ + b:cfg.SEQ + b + 1, :], yout[0:1, :], reads=['yout'], writes=['o_y'])
                else:
                    T.dma('yout', o_y[gc * 128:(gc + 1) * 128, :], yout[:], reads=['yout'], writes=['o_y'])
            T.barrier()
    T.finish()
    return nc, T


def _consts(cfg):
    f32 = np.float32
    cF = np.zeros((128, 8, 128), f32)
    s = np.arange(128)[:, None]
    t = np.arange(128)[None, :]
    cF[:, 0] = (s == t)
    cF[:, 1] = (s <= t)
    cF[:, 2] = np.where(s > t, NEGBIG, 0.0)
    cF[:, 3] = 1.0
    perm = np.zeros((128, 128), f32)
    for m in range(64):
        perm[m + 64, m] = -1.0
        perm[m, m + 64] = 1.0
    cF[:, 4] = perm
    cM = np.zeros((128, 2, 256), f32)
    i = np.arange(128)[:, None]
    j = np.arange(128)[None, :]
    cM[:, 1, 0:128] = np.where(j >= i, 0.0, NEGBIG)
    cM[:, 1, 128:256] = np.where(j <= i, 0.0, NEGBIG)
    cM[:, 0, 0:128] = NEGBIG
    cM[:, 0, 128:256] = cM[:, 1, 128:256]
    half = 8
    inv_freq = np.exp(-(f32(2.0) * np.arange(half, dtype=f32) / f32(16)) * f32(math.log(500000.0))).astype(f32)
    cR = np.zeros((128, cfg.NPC + 1, 16), f32)
    for ci in range(cfg.NPC + 1):
        base = ci * 128 if ci < cfg.NPC else PAST_LEN
        pos = (base + np.arange(128)).astype(f32)
        ang = (pos[:, None] * inv_freq[None, :]).astype(f32)
        cR[:, ci, 0:8] = np.cos(ang)
        cR[:, ci, 8:16] = np.sin(ang)
    cS = np.zeros((128, 4), f32)
    cS[0, 0] = 1.0
    cS[:64, 1] = -1.0
    cS[64:, 1] = 1.0
    cS[:64, 2] = 1.0
    cS[64:, 2] = -1.0
    return dict(cF=cF, cM=cM, cR=cR, cS=cS)


def _shared_inputs(cfg, inp):
    f32 = np.float32
    L = cfg.DEPTH
    A = lambda k: np.asarray(inp[k], dtype=f32)
    fm = lambda w: np.ascontiguousarray(w.reshape(L, 8, 128).transpose(0, 2, 1))
    cols = np.concatenate([fm(A('norm1_w')), fm(A('norm2_w')), fm(A('m_norm_w')), fm(A('s5_d'))], axis=2)
    cw_ = A('conv_w').reshape(L, 4, 16, 128).transpose(0, 3, 2, 1)
    cb_ = A('conv_b').reshape(L, 16, 128).transpose(0, 2, 1)[..., None]
    convp = np.ascontiguousarray(np.concatenate([cw_, cb_], axis=3))
    rows = np.ascontiguousarray(np.stack([A('dt_bias'), A('a_log'), A('m_d'), A('attn_sinks')], axis=1))
    lr = A('s5_lam_re').transpose(0, 2, 1)
    li = A('s5_lam_im').transpose(0, 2, 1)
    lam = np.stack([np.concatenate([lr, lr], axis=1), np.concatenate([li, li], axis=1)], axis=2)
    bre = A('s5_b_re')
    bim = A('s5_b_im')
    bw = np.zeros((L, 128, 8, 4, 2, 128), f32)
    for g in range(64):
        kt, e = g // 8, g % 4
        r0 = (g % 8) * 16
        br = bre[:, g].transpose(0, 2, 1)
        bi = bim[:, g].transpose(0, 2, 1)
        bw[:, r0:r0 + 16, kt, e, 0, 0:64] = br
        bw[:, r0:r0 + 16, kt, e, 0, 64:128] = bi
        bw[:, r0:r0 + 16, kt, e, 1, 0:64] = bi
        bw[:, r0:r0 + 16, kt, e, 1, 64:128] = br
    cre = A('s5_c_re').transpose(0, 3, 1, 2)
    cim = A('s5_c_im').transpose(0, 3, 1, 2)
    cw = np.zeros((L, 128, 64, 2, 16), f32)
    cw[:, 0:64, :, 0, :] = cre
    cw[:, 64:128, :, 0, :] = cim
    cw[:, 0:64, :, 1, :] = cim
    cw[:, 64:128, :, 1, :] = cre
    d = dict(w_in=A('w_in'), m_proj=A('m_proj'), s5_glu_w=A('s5_glu_w'), attn_o=A('attn_o'), w_out=A('w_out'),
             mlp_up=A('mlp_up'), mlp_down=A('mlp_down'), cols=np.ascontiguousarray(cols), convp=convp, rows=rows,
             lam=np.ascontiguousarray(lam), lstep=A('s5_log_step'), bw=bw.reshape(L, 128, -1), cw=cw.reshape(L, 128, -1),
             fnw=np.ascontiguousarray(A('final_norm_w').reshape(8, 128).T))
    d.update(_consts(cfg))
    return d


def _core_inputs(cfg, inp, shared, core):
    f32 = np.float32
    L, SPC = cfg.DEPTH, cfg.SPC
    seq = core // max(1, cfg.NCORES // 2)
    b0, b1 = core * SPC, (core + 1) * SPC
    A = lambda k: np.asarray(inp[k], dtype=f32)
    d = dict(shared)
    d['xin'] = np.ascontiguousarray(np.concatenate([A('x_prompt')[seq], A('x_sample')[b0:b1, 0, :]], axis=0))
    d['st_ssm'] = np.ascontiguousarray(A('state_ssm')[:, b0:b1].reshape(L, SPC, D, NS))
    d['st_conv'] = np.ascontiguousarray(A('state_conv')[:, b0:b1])
    d['st_s5'] = np.ascontiguousarray(np.stack([A('state_s5_re')[:, b0:b1], A('state_s5_im')[:, b0:b1]], axis=2))
    d['st_k'] = np.ascontiguousarray(A('cache_k')[:, b0:b1].reshape(L, SPC, 128, 256))
    d['st_v'] = np.ascontiguousarray(A('cache_v')[:, b0:b1].reshape(L, SPC, 128, 256))
    return d


def _assemble(cfg, res):
    L, SPC, SEQ = cfg.DEPTH, cfg.SPC, cfg.SEQ
    cps = max(1, cfg.NCORES // 2)
    pc = [0, cps]
    cat_s = lambda k: np.concatenate([r[k] for r in res], axis=1)
    y_p = np.stack([res[c]['o_y'][0:SEQ] for c in pc], axis=0)
    y_s = np.concatenate([r['o_y'][SEQ:SEQ + SPC] for r in res], axis=0)[:, None, :]
    ssm_p = np.stack([res[c]['o_ssm_p'] for c in pc], axis=1).reshape(L, 2, MH, HP, NS)
    ssm_s = cat_s('o_ssm_s').reshape(L, -1, MH, HP, NS)
    conv_p = np.stack([res[c]['o_conv_p'] for c in pc], axis=1)
    conv_s = cat_s('o_conv_s')
    s5_p = np.stack([res[c]['o_s5_p'] for c in pc], axis=1)
    s5_s = cat_s('o_s5_s')
    k_p = np.stack([res[c]['o_k_p'] for c in pc], axis=1).reshape(L, 2, 128, KVH, HD)
    k_s = cat_s('o_k_s').reshape(L, -1, 128, KVH, HD)
    v_p = np.stack([res[c]['o_v_p'] for c in pc], axis=1).reshape(L, 2, 128, KVH, HD)
    v_s = cat_s('o_v_s').reshape(L, -1, 128, KVH, HD)
    outs = (y_p, y_s, ssm_p, ssm_s, conv_p, conv_s,
            s5_p[:, :, 0], s5_s[:, :, 0], s5_p[:, :, 1], s5_s[:, :, 1], k_p, k_s, v_p, v_s)
    return tuple(np.ascontiguousarray(o, dtype=np.float32) for o in outs)


def kernel(**inputs):
    cfg = Cfg()
    nc, _ = build(cfg)
    shared = _shared_inputs(cfg, inputs)
    in_maps = [_core_inputs(cfg, inputs, shared, c) for c in range(cfg.NCORES)]
    res = run_bass_kernel_spmd(nc, in_maps, core_ids=list(range(cfg.NCORES)))
    return _assemble(cfg, res.results)
```

```python
import math
from contextlib import ExitStack
import numpy as np
import concourse.bass as bass
import concourse.mybir as mybir
from concourse.bass_utils import run_bass_kernel_spmd

F32 = mybir.dt.float32
BF16 = mybir.dt.bfloat16
AF = mybir.ActivationFunctionType
OP = mybir.AluOpType
AX = mybir.AxisListType

D = 1024
KT = 8
MH = 16
HP = 64
NG = 4
NS = 128
CONV_DIM = 2048
S5G = 64
S5N = 64
AH = 16
KVH = 4
HD = 64
DFF = 4096
IN_COLS = 8720
OFF_Z, OFF_XBC, OFF_DT, OFF_U, OFF_Q, OFF_K, OFF_G = 0, 1024, 3072, 3088, 4112, 5136, 5648
EPS = 1e-6
PAST_LEN = 8192
NEGBIG = -30000.0
TS5 = 32


class Cfg:
    def __init__(self, SEQ=8192, DEC_BATCH=128, DEPTH=4, NCORES=8, NCH=2):
        self.SEQ, self.DEC_BATCH, self.DEPTH, self.NCORES, self.NCH = SEQ, DEC_BATCH, DEPTH, NCORES, NCH
        self.NPC = SEQ // 128
        self.SPC = DEC_BATCH // NCORES
        assert self.NPC % NCH == 0 and self.SPC % NCH == 0
        self.NTOK = SEQ + self.SPC


class Tracker:
    def __init__(self, nc):
        self.nc = nc
        self.eng = {'pe': nc.tensor, 'dve': nc.vector, 'act': nc.scalar, 'pool': nc.gpsimd, 'sp': nc.sync}
        self.sem = {k: nc.alloc_semaphore('s_' + k) for k in self.eng}
        self.cnt = {k: 0 for k in self.sem}
        self.waited = {k: {} for k in self.eng}
        self.lastw = {}
        self.readers = {}
        self.ninst = 0
        self.bank = {}

    def _semkey(self, key):
        if key not in self.sem:
            self.sem[key] = self.nc.alloc_semaphore('s_' + key)
            self.cnt[key] = 0
        return self.sem[key]

    def _wait(self, e, key, val):
        if self.waited[e].get(key, 0) >= val:
            return
        self.eng[e].wait_ge(self.sem[key], val)
        self.waited[e][key] = val
        self.ninst += 1

    def deps(self, e, reads, writes):
        need = {}
        for r in reads:
            lw = self.lastw.get(r)
            if lw is not None:
                need[lw[0]] = max(need.get(lw[0], 0), lw[1])
        for w in writes:
            lw = self.lastw.get(w)
            if lw is not None:
                need[lw[0]] = max(need.get(lw[0], 0), lw[1])
            for k, v in self.readers.get(w, {}).items():
                need[k] = max(need.get(k, 0), v)
        if e != 'sp':
            for r in list(reads) + list(writes):
                if isinstance(r, str) and r.startswith('ps'):
                    bank = int(r[2]) if r[2].isdigit() else 7
                    for k, v in self.bank.setdefault(bank, {}).items():
                        if k != e:
                            need[k] = max(need.get(k, 0), v)
        for key, val in need.items():
            if key == e and e in ('pe', 'sp'):
                continue
            self._wait(e, key, val)

    def op(self, e, fn, reads=(), writes=()):
        self.deps(e, reads, writes)
        inst = fn(self.eng[e])
        self.cnt[e] += 1
        inst.then_inc(self.sem[e], 1)
        v = self.cnt[e]
        self.ninst += 1
        for r in list(reads) + list(writes):
            if isinstance(r, str) and r.startswith('ps'):
                bank = int(r[2]) if r[2].isdigit() else 7
                self.bank.setdefault(bank, {})[e] = v
        for r in reads:
            self.readers.setdefault(r, {})[e] = v
        for w in writes:
            self.lastw[w] = (e, v)
            self.readers[w] = {}

    def dma(self, tag, out, in_, reads=(), writes=(), q='sp'):
        key = 'd_' + tag
        self._semkey(key)
        self.deps(q, reads, writes)
        inst = self.eng[q].dma_start(out=out, in_=in_)
        self.cnt[key] += 16
        inst.then_inc(self.sem[key], 16)
        v = self.cnt[key]
        self.ninst += 1
        for r in reads:
            self.readers.setdefault(r, {})[key] = v
        for w in writes:
            self.lastw[w] = (key, v)
            self.readers[w] = {}

    def barrier(self):
        engs = ['pe', 'dve', 'act', 'pool']
        for k in list(self.sem):
            if (k.startswith('d_') or k in engs) and self.cnt[k] > 0:
                self._wait('sp', k, self.cnt[k])
        inst = self.eng['sp'].nop()
        self.cnt['sp'] += 1
        inst.then_inc(self.sem['sp'], 1)
        self.ninst += 1
        for e in engs:
            self._wait(e, 'sp', self.cnt['sp'])

    def finish(self):
        for k in self.sem:
            if self.cnt[k] > 0 and k != 'sp':
                self._wait('sp', k, self.cnt[k])


def weight_blocks():
    blks = []

    def add(name, src, r0, c0, w, kind='k8'):
        blks.append(dict(name=name, src=src, r0=r0, c0=c0, w=w, kind=kind))
    add('z0', 'w_in', 0, OFF_Z, 512); add('z1', 'w_in', 0, OFF_Z + 512, 512)
    for i in range(4):
        add('xbc%d' % i, 'w_in', 0, OFF_XBC + 512 * i, 512)
    add('dt', 'w_in', 0, OFF_DT, 16)
    add('mp0', 'm_proj', 0, 0, 512); add('mp1', 'm_proj', 0, 512, 512)
    add('g0_0', 'w_in', 0, OFF_G, 512); add('g0_1', 'w_in', 0, OFF_G + 512, 512)
    add('u0', 'w_in', 0, OFF_U, 512); add('u1', 'w_in', 0, OFF_U + 512, 512)
    add('gv0', 's5_glu_w', 0, 0, 512); add('gg0', 's5_glu_w', 0, 1024, 512)
    add('gv1', 's5_glu_w', 0, 512, 512); add('gg1', 's5_glu_w', 0, 1536, 512)
    add('g1_0', 'w_in', 0, OFF_G + 1024, 512); add('g1_1', 'w_in', 0, OFF_G + 1536, 512)
    add('q0', 'w_in', 0, OFF_Q, 512); add('q1', 'w_in', 0, OFF_Q + 512, 512)
    add('kv', 'w_in', 0, OFF_K, 512)
    add('ao0', 'attn_o', 0, 0, 512); add('ao1', 'attn_o', 0, 512, 512)
    add('g2_0', 'w_in', 0, OFF_G + 2048, 512); add('g2_1', 'w_in', 0, OFF_G + 2560, 512)
    add('wo0', 'w_out', 0, 0, 512); add('wo1', 'w_out', 0, 512, 512)
    for i in range(8):
        add('up%d' % i, 'mlp_up', 0, 512 * i, 512)
    for i in range(8):
        add('dn%d' % i, 'mlp_down', 0, 128 * i, 128, kind='k32')
    return blks


WBLKS = weight_blocks()
NBLK = len(WBLKS)

PB_WR, PB_WI, PB_CS, PB_SN = 0, 2048, 4096, 6144
PB_BW = 8192
PB_CW = PB_BW + 8192
PB_SZ = PB_CW + 2048
PF_MULT = 0
PF_ROT = PF_MULT + 64 * 33
PF_SZ = PF_ROT + 6 * 64


def build(cfg):
    L, NCH, NPC, SPC, NTOK = cfg.DEPTH, cfg.NCH, cfg.NPC, cfg.SPC, cfg.NTOK
    N = NCH * 128
    nc = bass.Bass("TRN2", target_bir_lowering=False)
    T = Tracker(nc)

    def din(name, shape, dt=F32):
        return nc.dram_tensor(name, list(shape), dt, kind="ExternalInput").ap()

    def dout(name, shape):
        return nc.dram_tensor(name, list(shape), F32, kind="ExternalOutput").ap()

    xin = din("xin", [NTOK, D])
    Wd = dict(w_in=din("w_in", [L, D, IN_COLS]), m_proj=din("m_proj", [L, D, D]),
              s5_glu_w=din("s5_glu_w", [L, D, 2 * D]), attn_o=din("attn_o", [L, D, D]),
              w_out=din("w_out", [L, D, D]), mlp_up=din("mlp_up", [L, D, DFF]),
              mlp_down=din("mlp_down", [L, DFF, D]))
    colsd = din("cols", [L, 128, 32])
    convd = din("convp", [L, 128, 16, 5])
    rowsd = din("rows", [L, 4, 16])
    lamd = din("lam", [L, 128, 2, 64])
    lstepd = din("lstep", [L, 64])
    bwd = din("bw", [L, 128, 8 * 4 * 2 * 128])
    cwd = din("cw", [L, 128, 64 * 2 * 16])
    fnwd = din("fnw", [128, 8])
    st_ssm = din("st_ssm", [L, SPC, D, NS])
    st_conv = din("st_conv", [L, SPC, 3, CONV_DIM])
    st_s5 = din("st_s5", [L, SPC, 2, S5G, S5N])
    st_k = din("st_k", [L, SPC, 128, 256])
    st_v = din("st_v", [L, SPC, 128, 256])
    cF = din("cF", [128, 8, 128])
    cM = din("cM", [128, 2, 256])
    cR = din("cR", [128, NPC + 1, 16])
    cS = din("cS", [128, 4])

    o_y = dout("o_y", [NTOK, D])
    o_ssm_p = dout("o_ssm_p", [L, D, NS]); o_ssm_s = dout("o_ssm_s", [L, SPC, D, NS])
    o_conv_p = dout("o_conv_p", [L, 3, CONV_DIM]); o_conv_s = dout("o_conv_s", [L, SPC, 3, CONV_DIM])
    o_s5_p = dout("o_s5_p", [L, 2, S5G, S5N]); o_s5_s = dout("o_s5_s", [L, SPC, 2, S5G, S5N])
    o_k_p = dout("o_k_p", [L, 128, 256]); o_k_s = dout("o_k_s", [L, SPC, 128, 256])
    o_v_p = dout("o_v_p", [L, 128, 256]); o_v_s = dout("o_v_s", [L, SPC, 128, 256])

    wbf = nc.dram_tensor("wbf", [L, NBLK, 128, 4096], BF16, kind="Internal").ap()
    packB = nc.dram_tensor("packB", [L, 128, PB_SZ], BF16, kind="Internal").ap()
    packF = nc.dram_tensor("packF", [L, 128, PF_SZ], F32, kind="Internal").ap()

    _uq = [0]

    def uq(name):
        _uq[0] += 1
        return '%s_%d' % (name, _uq[0])

    def sb(name, shape, dt=F32):
        return nc.alloc_sbuf_tensor(name, list(shape), dt)

    DBG = getattr(cfg, 'DEBUG', False)
    STOP = getattr(cfg, 'STOP', 99)
    if DBG:
        o_dbg = dout('o_dbg', [(cfg.NPC + cfg.SPC) // NCH, L, 4, 128, KT * N])

    def dbg_dump(sc, l, k, tile):
        if DBG:
            T.dma('dbg', o_dbg[sc, l, k], tile[:].rearrange('p a b -> p (a b)'), reads=['ybr', 'xT'], writes=['o_dbg'])

    PS = [nc.alloc_psum_tensor("ps%d" % i, [128, 512], F32) for i in range(7)]
    PSB = nc.alloc_psum_tensor("psb", [128, 1024], BF16)
    PSN = ['ps%d' % i for i in range(7)]
    PS2B = PS[2][:].bitcast(BF16)

    cFt = sb("cFt", [128, 8, 128]); cMt = sb("cMt", [128, 2, 256]); cRt = sb("cRt", [128, NPC + 1, 16])
    cSt = sb("cSt", [128, 4]); fnw = sb("fnw_t", [128, 8])
    identB = sb("identB", [128, 128], BF16)
    onesB = sb("onesB", [128, 128], BF16)
    identF = cFt[:, 0, :]; triF = cFt[:, 1, :]; negF = cFt[:, 2, :]; onesF = cFt[:, 3, :]; permF = cFt[:, 4, :]
    colsT = sb("colsT", [128, L, 32]); convT = sb("convT", [128, L, 16, 5]); rowsT = sb("rowsT", [128, L, 4, 16])
    Abc = sb("Abc", [128, L, 16])
    Sst = sb("Sst", [128, L, D])
    histT = sb("histT", [128, L, 16, 3])
    s5c = sb("s5c", [128, L, 64])
    kcat = sb("kcat", [128, L, 4, 256], BF16)
    vcat = sb("vcat", [128, L, 2, 256], BF16)
    wr = [sb("wr%d" % i, [128, 4096], BF16) for i in range(4)]
    pkB = sb("pkB", [128, PB_SZ], BF16)
    pkF = sb("pkF", [128, PF_SZ])

    T.dma('c_cF', cFt[:], cF, writes=['cF'])
    T.dma('c_cM', cMt[:], cM, writes=['cM'])
    T.dma('c_cR', cRt[:], cR, writes=['cR'])
    T.dma('c_cS', cSt[:], cS, writes=['cS'])
    T.dma('c_fnw', fnw[:], fnwd, writes=['fnw'])
    for l in range(L):
        T.dma('c_cols', colsT[:, l, :], colsd[l], writes=['cols'])
        T.dma('c_conv', convT[:, l, :, :], convd[l], writes=['conv'])
        T.dma('c_rows', rowsT[:, l, :, :].rearrange("p a b -> p (a b)"),
              rowsd[l].rearrange("a b -> (a b)").partition_broadcast(128), writes=['rows'])
    T.op('dve', lambda e: e.tensor_copy(identB[:], identF), reads=['cF'], writes=['identB'])
    T.op('dve', lambda e: e.tensor_copy(onesB[:], onesF), reads=['cF'], writes=['onesB'])
    T.op('act', lambda e: e.activation(out=Abc[:], in_=rowsT[:, :, 1, :], func=AF.Exp), reads=['rows'], writes=['Abc'])
    T.op('dve', lambda e: e.tensor_scalar(out=Abc[:], in0=Abc[:], scalar1=-1.0, scalar2=None, op0=OP.mult),
         reads=['Abc'], writes=['Abc'])

    with ExitStack() as es:
        lamT = es.enter_context(nc.sbuf_tensor(uq("lamT"), [128, 2, 64], F32))
        stp = es.enter_context(nc.sbuf_tensor(uq("stp"), [128, 64], F32))
        CS = es.enter_context(nc.sbuf_tensor(uq("CS"), [128, 64, 33], F32))
        SN = es.enter_context(nc.sbuf_tensor(uq("SN"), [128, 64, 33], F32))
        tA = es.enter_context(nc.sbuf_tensor(uq("tA"), [128, 64, 32], F32))
        tB = es.enter_context(nc.sbuf_tensor(uq("tB"), [128, 64, 32], F32))
        sm = es.enter_context(nc.sbuf_tensor(uq("sm"), [128, 12, 64], F32))
        stg = es.enter_context(nc.sbuf_tensor(uq("stg"), [128, 4096], F32))
        for l in range(L):
            T.dma('pl_lam', lamT[:], lamd[l], writes=['lamT'])
            T.dma('pl_stp', stp[:], lstepd[l].partition_broadcast(128), writes=['stp'])
            lr = lamT[:, 0, :]; li = lamT[:, 1, :]
            th = sm[:, 0, :]; mag = sm[:, 1, :]; r = sm[:, 2, :]; m_ = sm[:, 3, :]
            c1 = sm[:, 4, :]; s1 = sm[:, 5, :]; fre = sm[:, 6, :]; fim = sm[:, 7, :]
            t0 = sm[:, 8, :]; t1 = sm[:, 9, :]; t2 = sm[:, 10, :]; t3 = sm[:, 11, :]
            V = lambda f, rd, wr_: T.op('dve', f, reads=rd, writes=wr_)
            A_ = lambda f, rd, wr_: T.op('act', f, reads=rd, writes=wr_)
            A_(lambda e: e.activation(out=stp[:], in_=stp[:], func=AF.Exp), ['stp'], ['stp'])
            V(lambda e: e.tensor_tensor(out=th, in0=li, in1=stp[:], op=OP.mult), ['lamT', 'stp'], ['sm'])
            V(lambda e: e.tensor_tensor(out=t0, in0=lr, in1=stp[:], op=OP.mult), ['lamT', 'stp'], ['sm'])
            A_(lambda e: e.activation(out=mag, in_=t0, func=AF.Exp), ['sm'], ['sm'])
            V(lambda e: e.tensor_copy(r, th), ['sm'], ['sm'])
            for _ in range(4):
                V(lambda e: e.tensor_scalar(out=m_, in0=r, scalar1=math.pi, scalar2=-2.0 * math.pi, op0=OP.is_gt, op1=OP.mult), ['sm'], ['sm'])
                V(lambda e: e.tensor_tensor(out=r, in0=r, in1=m_, op=OP.add), ['sm'], ['sm'])
            A_(lambda e: e.activation(out=s1, in_=r, func=AF.Sin), ['sm'], ['sm'])
            V(lambda e: e.tensor_scalar(out=t0, in0=r, scalar1=-1.0, scalar2=None, op0=OP.mult), ['sm'], ['sm'])
            V(lambda e: e.tensor_tensor(out=t0, in0=t0, in1=r, op=OP.max), ['sm'], ['sm'])
            V(lambda e: e.tensor_scalar(out=t0, in0=t0, scalar1=-1.0, scalar2=math.pi / 2, op0=OP.mult, op1=OP.add), ['sm'], ['sm'])
            A_(lambda e: e.activation(out=c1, in_=t0, func=AF.Sin), ['sm'], ['sm'])
            V(lambda e: e.memset(CS[:, :, 0:1], 1.0), [], ['CS'])
            V(lambda e: e.memset(SN[:, :, 0:1], 0.0), [], ['SN'])
            V(lambda e: e.tensor_copy(CS[:, :, 1], c1), ['sm'], ['CS'])
            V(lambda e: e.tensor_copy(SN[:, :, 1], s1), ['sm'], ['SN'])
            m = 1
            while m < 32:
                cm = CS[:, :, m:m + 1].to_broadcast([128, 64, m]); smm = SN[:, :, m:m + 1].to_broadcast([128, 64, m])
                a = tA[:, :, 0:m]; b = tB[:, :, 0:m]
                V(lambda e: e.tensor_tensor(out=a, in0=CS[:, :, 1:m + 1], in1=cm, op=OP.mult), ['CS'], ['tA'])
                V(lambda e: e.tensor_tensor(out=b, in0=SN[:, :, 1:m + 1], in1=smm, op=OP.mult), ['SN'], ['tB'])
                V(lambda e: e.tensor_tensor(out=CS[:, :, m + 1:2 * m + 1], in0=a, in1=b, op=OP.subtract), ['tA', 'tB'], ['CS'])
                V(lambda e: e.tensor_tensor(out=a, in0=SN[:, :, 1:m + 1], in1=cm, op=OP.mult), ['SN', 'CS'], ['tA'])
                V(lambda e: e.tensor_tensor(out=b, in0=CS[:, :, 1:m + 1], in1=smm, op=OP.mult), ['SN', 'CS'], ['tB'])
                V(lambda e: e.tensor_tensor(out=SN[:, :, m + 1:2 * m + 1], in0=a, in1=b, op=OP.add), ['tA', 'tB'], ['SN'])
                m *= 2
            V(lambda e: e.tensor_tensor(out=t0, in0=mag, in1=c1, op=OP.mult), ['sm'], ['sm'])
            V(lambda e: e.tensor_tensor(out=t1, in0=mag, in1=s1, op=OP.mult), ['sm'], ['sm'])
            V(lambda e: e.tensor_scalar(out=t0, in0=t0, scalar1=-1.0, scalar2=None, op0=OP.add), ['sm'], ['sm'])
            V(lambda e: e.tensor_tensor(out=t2, in0=lr, in1=lr, op=OP.mult), ['lamT', 'sm'], ['sm'])
            V(lambda e: e.tensor_tensor(out=t3, in0=li, in1=li, op=OP.mult), ['lamT', 'sm'], ['sm'])
            V(lambda e: e.tensor_tensor(out=t2, in0=t2, in1=t3, op=OP.add), ['sm'], ['sm'])
            V(lambda e: e.reciprocal(out=t2, in_=t2), ['sm'], ['sm'])
            V(lambda e: e.tensor_tensor(out=fre, in0=t0, in1=lr, op=OP.mult), ['sm', 'lamT'], ['sm'])
            V(lambda e: e.tensor_tensor(out=t3, in0=t1, in1=li, op=OP.mult), ['sm', 'lamT'], ['sm'])
            V(lambda e: e.tensor_tensor(out=fre, in0=fre, in1=t3, op=OP.add), ['sm'], ['sm'])
            V(lambda e: e.tensor_tensor(out=fre, in0=fre, in1=t2, op=OP.mult), ['sm'], ['sm'])
            V(lambda e: e.tensor_tensor(out=fim, in0=t1, in1=lr, op=OP.mult), ['sm', 'lamT'], ['sm'])
            V(lambda e: e.tensor_tensor(out=t3, in0=t0, in1=li, op=OP.mult), ['sm', 'lamT'], ['sm'])
            V(lambda e: e.tensor_tensor(out=fim, in0=fim, in1=t3, op=OP.subtract), ['sm'], ['sm'])
            V(lambda e: e.tensor_tensor(out=fim, in0=fim, in1=t2, op=OP.mult), ['sm'], ['sm'])
            frb = sm[:, 6, :].unsqueeze(2).to_broadcast([128, 64, 32]); fib = sm[:, 7, :].unsqueeze(2).to_broadcast([128, 64, 32])
            pk3 = lambda off: pkB[:, off:off + 2048].rearrange("p (g t) -> p g t", t=32)
            V(lambda e: e.tensor_tensor(out=tA[:], in0=CS[:, :, 0:32], in1=frb, op=OP.mult), ['CS', 'sm'], ['tA'])
            V(lambda e: e.tensor_tensor(out=tB[:], in0=SN[:, :, 0:32], in1=fib, op=OP.mult), ['SN', 'sm'], ['tB'])
            V(lambda e: e.tensor_tensor(out=pk3(PB_WR), in0=tA[:], in1=tB[:], op=OP.add), ['tA', 'tB'], ['pkB'])
            V(lambda e: e.tensor_tensor(out=tA[:], in0=CS[:, :, 0:32], in1=fib, op=OP.mult), ['CS', 'sm', 'pkB'], ['tA'])
            V(lambda e: e.tensor_tensor(out=tB[:], in0=SN[:, :, 0:32], in1=frb, op=OP.mult), ['SN', 'sm', 'pkB'], ['tB'])
            V(lambda e: e.tensor_tensor(out=tA[:], in0=tA[:], in1=tB[:], op=OP.subtract), ['tA', 'tB'], ['tA'])
            V(lambda e: e.tensor_scalar(out=pk3(PB_WI), in0=tA[:], scalar1=cSt[:, 1:2], scalar2=None, op0=OP.mult), ['tA', 'cS'], ['pkB'])
            V(lambda e: e.tensor_scalar(out=pk3(PB_CS), in0=CS[:, :, 0:32], scalar1=cSt[:, 2:3], scalar2=None, op0=OP.mult), ['CS', 'cS'], ['pkB'])
            V(lambda e: e.tensor_scalar(out=pk3(PB_SN), in0=SN[:, :, 0:32], scalar1=-1.0, scalar2=None, op0=OP.mult), ['SN'], ['pkB'])
            for hb in range(2):
                T.dma('pl_bw', stg[:], bwd[l, :, hb * 4096:(hb + 1) * 4096], writes=['stg'])
                V(lambda e: e.tensor_copy(pkB[:, PB_BW + hb * 4096:PB_BW + (hb + 1) * 4096], stg[:]), ['stg'], ['pkB'])
            T.dma('pl_bw', stg[:, 0:2048], cwd[l], reads=[], writes=['stg'])
            V(lambda e: e.tensor_copy(pkB[:, PB_CW:PB_CW + 2048], stg[:, 0:2048]), ['stg'], ['pkB'])
            mlt = pkF[:, PF_MULT:PF_MULT + 64 * 33].rearrange("p (g t) -> p g t", t=33)
            V(lambda e: e.memset(mlt[:, :, 0:1], 0.0), [], ['pkF'])
            V(lambda e: e.tensor_copy(mlt[:, :, 1:33], sm[:, 1, :].unsqueeze(2).to_broadcast([128, 64, 32])), ['sm'], ['pkF'])
            rot = pkF[:, PF_ROT:PF_ROT + 384].rearrange("p (a c g) -> p a c g", a=3, c=2)
            for ai, tt in enumerate((1, 31, 32)):
                V(lambda e: e.tensor_copy(rot[:, ai, 0, :], CS[:, :, tt]), ['CS'], ['pkF'])
                V(lambda e: e.tensor_copy(rot[:, ai, 1, :], SN[:, :, tt]), ['SN'], ['pkF'])
            T.dma('pl_stB', packB[l], pkB[:], reads=['pkB'], writes=['packB%d' % l])
            T.dma('pl_stF', packF[l], pkF[:], reads=['pkF'], writes=['packF%d' % l])
        T.barrier()

    with ExitStack() as es:
        wst = [es.enter_context(nc.sbuf_tensor(uq("wst"), [128, 4096], F32)) for _ in range(3)]
        i = 0
        for l in range(L):
            for bi, blk in enumerate(WBLKS):
                s = i % 3
                sb_ = i % 4
                src = Wd[blk['src']]
                if blk['kind'] == 'k8':
                    w = blk['w']
                    sap = src[l, 0:D, blk['c0']:blk['c0'] + w].rearrange("(kt p) c -> p kt c", p=128)
                    tap = wst[s][:, 0:8 * w].rearrange("p (kt c) -> p kt c", c=w)
                    n_el = 8 * w
                else:
                    sap = src[l, 0:DFF, blk['c0']:blk['c0'] + 128].rearrange("(kt p) c -> p kt c", p=128)
                    tap = wst[s][:, :].rearrange("p (kt c) -> p kt c", c=128)
                    n_el = 4096
                T.dma('wst%d' % s, tap, sap, writes=['wst%d' % s])
                if i % 2 == 1:
                    T.op('act', lambda e: e.copy(out=wr[sb_][:, 0:n_el], in_=wst[s][:, 0:n_el]), reads=['wst%d' % s], writes=['wr%d' % sb_])
                else:
                    T.op('dve', lambda e: e.tensor_copy(wr[sb_][:, 0:n_el], wst[s][:, 0:n_el]), reads=['wst%d' % s], writes=['wr%d' % sb_])
                T.dma('wcs%d' % sb_, wbf[l, bi, :, 0:n_el], wr[sb_][:, 0:n_el], reads=['wr%d' % sb_], writes=['wbf%d_%d' % (l, bi)])
                i += 1
        T.barrier()

    if STOP <= 0:
        T.finish()
        return nc, T
    xT = sb("xT", [128, KT, N])
    hT = sb("hT", [128, KT, N], BF16)
    mrg = sb("mrg", [128, KT, N])
    mbf = sb("mbf", [128, KT, N], BF16)
    ybr = sb("ybr", [128, KT, N])
    brf = sb("brf", [128, KT, N], BF16)
    rs = sb("rs", [128, N])
    gs = [sb("gs%d" % i, [128, N]) for i in range(2)]
    seq = []
    n_sc = (NPC + SPC) // NCH
    for sc in range(n_sc):
        for l in range(L):
            for bi in range(NBLK):
                seq.append((l, bi))
    wstate = dict(issued=0, cur=-1)

    def w_issue(upto):
        while wstate['issued'] <= min(upto, len(seq) - 1):
            j = wstate['issued']
            l, bi = seq[j]
            s = j % 4
            blk = WBLKS[bi]
            n_el = 8 * blk['w'] if blk['kind'] == 'k8' else 4096
            T.dma('wr%d' % s, wr[s][:, 0:n_el], wbf[l, bi, :, 0:n_el], reads=['wbf%d_%d' % (l, bi)], writes=['wr%d' % s])
            wstate['issued'] += 1

    def w_next(name, ahead=3):
        wstate['cur'] += 1
        j = wstate['cur']
        l, bi = seq[j]
        assert WBLKS[bi]['name'] == name, (WBLKS[bi]['name'], name)
        w_issue(j + ahead)
        s = j % 4
        blk = WBLKS[bi]
        if blk['kind'] == 'k8':
            return wr[s][:, 0:8 * blk['w']].rearrange("p (kt c) -> p kt c", c=blk['w']), 'wr%d' % s
        return wr[s][:, :].rearrange("p (kt c) -> p kt c", c=128), 'wr%d' % s

    pmi = [0]

    def pm_next():
        pmi[0] ^= 1
        return PS[pmi[0]], PSN[pmi[0]]

    pend = []
    bankrot = [0]

    def flush_pend():
        while pend:
            f = pend.pop(0)
            f()

    def proj_fm(wname, act, actres, nm, cb, lag=0, banks=None):
        wt, wres = w_next(wname)
        for m in range(nm):
            if banks is None:
                ps, psn = pm_next()
            else:
                bk = banks[bankrot[0] % len(banks)]
                bankrot[0] += 1
                ps, psn = PS[bk], PSN[bk]
            for kt in range(KT):
                T.op('pe', lambda e: e.matmul(ps[:, 0:N], lhsT=wt[:, kt, m * 128:(m + 1) * 128], rhs=act[:, kt, :],
                                              start=(kt == 0), stop=(kt == KT - 1)), reads=[wres, actres], writes=[psn])
            if lag:
                pend.append(lambda m=m, ps=ps, psn=psn: cb(m, ps, psn))
                while len(pend) > lag:
                    pend.pop(0)()
            else:
                cb(m, ps, psn)

    def proj_tm(wname, act, actres, w, cb):
        wt, wres = w_next(wname)
        for c in range(NCH):
            ps, psn = pm_next()
            for kt in range(KT):
                T.op('pe', lambda e: e.matmul(ps[:, 0:w], lhsT=act[:, kt, c * 128:(c + 1) * 128], rhs=wt[:, kt, 0:w],
                                              start=(kt == 0), stop=(kt == KT - 1)), reads=[wres, actres], writes=[psn])
            cb(c, ps, psn)

    def rmsnorm(wcol, out_t, outres):
        T.op('act', lambda e: e.activation(out=mbf[:], in_=xT[:], func=AF.Square), reads=['xT'], writes=['mbf'])
        ps, psn = pm_next()
        for kt in range(KT):
            T.op('pe', lambda e: e.matmul(ps[:, 0:N], lhsT=onesB[:], rhs=mbf[:, kt, :], start=(kt == 0), stop=(kt == KT - 1)),
                 reads=['onesB', 'mbf'], writes=[psn])
        T.op('act', lambda e: e.activation(out=rs[:], in_=ps[:, 0:N], func=AF.Sqrt, bias=EPS, scale=1.0 / D), reads=[psn], writes=['rs'])
        T.op('dve', lambda e: e.reciprocal(out=rs[:], in_=rs[:]), reads=['rs'], writes=['rs'])
        for kt in range(KT):
            T.op('dve', lambda e: e.scalar_tensor_tensor(out=out_t[:, kt, :], in0=xT[:, kt, :], scalar=wcol(kt), in1=rs[:],
                                                         op0=OP.mult, op1=OP.mult), reads=['xT', 'rs', 'cols', 'fnw'], writes=[outres])

    def gate_stage(l, bidx):
        for half in range(2):
            def cb(m, ps, psn, half=half):
                mb = half * 4 + m
                g = gs[mb % 2]; gn = 'gs%d' % (mb % 2)
                T.op('act', lambda e: e.activation(out=g[:], in_=ps[:, 0:N], func=AF.Sigmoid), reads=[psn], writes=[gn])
                if bidx == 0:
                    T.op('dve', lambda e: e.tensor_tensor(out=mrg[:, mb, :], in0=g[:], in1=ybr[:, mb, :], op=OP.mult),
                         reads=[gn, 'ybr'], writes=['mrg'])
                else:
                    T.op('dve', lambda e: e.tensor_tensor(out=g[:], in0=g[:], in1=ybr[:, mb, :], op=OP.mult), reads=[gn, 'ybr'], writes=[gn])
                    if bidx == 1:
                        T.op('dve', lambda e: e.tensor_tensor(out=mrg[:, mb, :], in0=mrg[:, mb, :], in1=g[:], op=OP.add),
                             reads=[gn, 'mrg'], writes=['mrg'])
                    else:
                        T.op('dve', lambda e: e.tensor_tensor(out=mbf[:, mb, :], in0=mrg[:, mb, :], in1=g[:], op=OP.add),
                             reads=[gn, 'mrg'], writes=['mbf'])
            proj_fm('g%d_%d' % (bidx, half), hT, 'hT', 4, cb)

    def branch_out(names, src, srcres):
        for half, nm in enumerate(names):
            def cb(m, ps, psn, half=half):
                T.op('act', lambda e: e.copy(out=ybr[:, half * 4 + m, :], in_=ps[:, 0:N]), reads=[psn], writes=['ybr'])
            proj_fm(nm, src, srcres, 4, cb)

    def V(f, rd=(), wr_=()):
        T.op('dve', f, reads=rd, writes=wr_)

    def A(f, rd=(), wr_=()):
        T.op('act', f, reads=rd, writes=wr_)

    def GP(f, rd=(), wr_=()):
        T.op('pool', f, reads=rd, writes=wr_)

    def PE(f, rd=(), wr_=()):
        T.op('pe', f, reads=rd, writes=wr_)

    rtmp = sb("rtmp", [128, 2, 64])

    def rot(dst, dstres, w_ap, wres, ai, ROT):
        PE(lambda e: e.matmul(PS[6][:, 256:320], lhsT=permF, rhs=w_ap, start=True, stop=True), [wres, 'cF'], ['ps6r'])
        V(lambda e: e.tensor_tensor(out=rtmp[:, 0, :], in0=w_ap, in1=ROT[:, ai, 0, :], op=OP.mult), [wres, 'pkF'], ['rtmp0'])
        V(lambda e: e.tensor_tensor(out=rtmp[:, 1, :], in0=PS[6][:, 256:320], in1=ROT[:, ai, 1, :], op=OP.mult), ['ps6r', 'pkF'], ['rtmp1'])
        V(lambda e: e.tensor_tensor(out=dst, in0=rtmp[:, 0, :], in1=rtmp[:, 1, :], op=OP.add), ['rtmp0', 'rtmp1'], [dstres])

    last_prompt_sc = NPC // NCH - 1

    for sc in range(n_sc):
        chunks = [sc * NCH + c for c in range(NCH)]
        fake = chunks[0] >= NPC
        cs_ = lambda c: slice(c * 128, (c + 1) * 128)
        with ExitStack() as es:
            xtm = es.enter_context(nc.sbuf_tensor(uq("xtm"), [128, D], F32))
            for c, gc in enumerate(chunks):
                if fake:
                    b = gc - NPC
                    V(lambda e: e.memset(xtm[:], 0.0), [], ['xtm'])
                    T.dma('xtm', xtm[0:1, :], xin[cfg.SEQ + b:cfg.SEQ + b + 1, :], writes=['xtm'])
                else:
                    T.dma('xtm', xtm[:], xin[gc * 128:(gc + 1) * 128, :], writes=['xtm'])
                for half in range(2):
                    ps, psn = PS[2 + half], PSN[2 + half]
                    for j in range(4):
                        jj = half * 4 + j
                        PE(lambda e: e.transpose(ps[:, j * 128:(j + 1) * 128], xtm[:, jj * 128:(jj + 1) * 128], identF), ['xtm', 'cF'], [psn])
                    A(lambda e: e.copy(out=xT[:, half * 4:half * 4 + 4, cs_(c)], in_=ps[:].rearrange("p (j t) -> p j t", t=128)), [psn], ['xT'])
            T.barrier()

        for l in range(L):
            T.dma('pkB', pkB[:], packB[l], reads=['packB%d' % l], writes=['pkB'])
            T.dma('pkF', pkF[:], packF[l], reads=['packF%d' % l], writes=['pkF'])
            tab = lambda off: pkB[:, off:off + 2048].rearrange("p (g t) -> p g t", t=32)
            WRt, WIt, CSt_, SNt = tab(PB_WR), tab(PB_WI), tab(PB_CS), tab(PB_SN)
            Bw = pkB[:, PB_BW:PB_BW + 8192].rearrange("p (kt e o n) -> p kt e o n", kt=8, e=4, o=2)
            Cw = pkB[:, PB_CW:PB_CW + 2048].rearrange("p (g o c) -> p g o c", g=64, o=2)
            MULT = pkF[:, PF_MULT:PF_MULT + 64 * 33]
            ROT = pkF[:, PF_ROT:PF_ROT + 384].rearrange("p (a c g) -> p a c g", a=3, c=2)
            n1 = lambda kt: colsT[:, l, kt:kt + 1]
            n2 = lambda kt: colsT[:, l, 8 + kt:9 + kt]
            S_l = Sst[:, l, :]
            Sres = 'S%d' % l

            rmsnorm(n1, hT, 'hT')
            if STOP <= 1:
                T.finish()
                return nc, T

            with ExitStack() as es:
                zs = es.enter_context(nc.sbuf_tensor(uq("zs"), [128, NCH, D], F32))
                xc = es.enter_context(nc.sbuf_tensor(uq("xc"), [128, 16, N], BF16))
                rawb = es.enter_context(nc.sbuf_tensor(uq("rawb"), [128, 2, NCH, 132], BF16))
                dg = es.enter_context(nc.sbuf_tensor(uq("dg"), [128, 2, 4, 128], BF16))
                dtw = es.enter_context(nc.sbuf_tensor(uq("dtw"), [128, NCH, 8, 16], F32))
                xdt = es.enter_context(nc.sbuf_tensor(uq("xdt"), [128, 16, 64], BF16))
                xw = es.enter_context(nc.sbuf_tensor(uq("xw"), [128, 16, 64], BF16))
                xDd = es.enter_context(nc.sbuf_tensor(uq("xDd"), [128, 16, 64], F32))
                btm = es.enter_context(nc.sbuf_tensor(uq("btm"), [128, 512], BF16))
                LT = es.enter_context(nc.sbuf_tensor(uq("LT"), [128, 2, 128], BF16))
                MT = es.enter_context(nc.sbuf_tensor(uq("MT"), [128, 2, 128], BF16))
                yt = es.enter_context(nc.sbuf_tensor(uq("yt"), [128, D], F32))
                Sbf = es.enter_context(nc.sbuf_tensor(uq("Sbf"), [128, D], BF16))
                ynb = es.enter_context(nc.sbuf_tensor(uq("ynb"), [128, D], BF16))
                ssq = es.enter_context(nc.sbuf_tensor(uq("ssq"), [128, 8], F32))
                ctl = es.enter_context(nc.sbuf_tensor(uq("ctl"), [128, NCH, 8, 16], F32))
                hst = es.enter_context(nc.sbuf_tensor(uq("hst"), [128, NCH, 16, 3], F32))
                cio = es.enter_context(nc.sbuf_tensor(uq("cio"), [48, 128], F32))
                sio = es.enter_context(nc.sbuf_tensor(uq("sio"), [128, KT, 128], F32))
                for half in range(2):
                    def cbz(c, ps, psn, half=half):
                        A(lambda e: e.activation(out=zs[:, c, half * 512:(half + 1) * 512], in_=ps[:, 0:512], func=AF.Silu), [psn], ['zs'])
                    proj_tm('z%d' % half, hT, 'hT', 512, cbz)
                if fake:
                    for c, gc in enumerate(chunks):
                        b = gc - NPC
                        T.dma('cio', cio[:], st_conv[l, b].rearrange("k (mb ch) -> (k mb) ch", ch=128), writes=['cio'])
                        PE(lambda e: e.transpose(PS[6][:, 0:48], cio[:], identF[0:48, 0:48]), ['cio', 'cF'], ['ps6'])
                        V(lambda e: e.tensor_copy(hst[:, c, :, :], PS[6][:, 0:48].rearrange("p (k mb) -> p mb k", k=3)), ['ps6'], ['hst'])
                else:
                    if sc == 0:
                        V(lambda e: e.memset(histT[:, l, :, :], 0.0), [], ['hist%d' % l])
                        GP(lambda e: e.memset(S_l, 0.0), [], [Sres])
                    V(lambda e: e.tensor_copy(hst[:, 0, :, :], histT[:, l, :, :]), ['hist%d' % l], ['hst'])
                V(lambda e: e.memset(ctl[:], 0.0), [], ['ctl'])
                for blk in range(4):
                    def cbx(m, ps, psn, blk=blk):
                        mb = blk * 4 + m
                        rb = mb % 2
                        rw = rawb[:, rb, :, :]
                        rwn = 'raw%d' % rb
                        dgn = 'dg%d' % rb
                        psv = ps[:, 0:N].rearrange("p (c t) -> p c t", t=128)
                        A(lambda e: e.copy(out=rw[:, :, 3:131], in_=psv), [psn], [rwn])
                        for k in range(4):
                            V(lambda e: e.tensor_scalar(out=dg[:, rb, k, :], in0=identB[:], scalar1=convT[:, l, mb, k:k + 1], scalar2=None, op0=OP.mult), ['identB', 'conv'], [dgn])
                        for c in range(NCH):
                            if fake or c == 0:
                                V(lambda e: e.tensor_copy(rw[:, c, 0:3], hst[:, c, mb, :]), ['hst', rwn], [rwn])
                            else:
                                V(lambda e: e.tensor_copy(rw[:, c, 0:3], rw[:, c - 1, 128:131]), [rwn], [rwn])
                            cps, cpn = PS[2 + (c % 2)], PSN[2 + (c % 2)]
                            for k in range(4):
                                PE(lambda e: e.matmul(cps[:, 0:128], lhsT=dg[:, rb, k, :], rhs=rw[:, c, k:k + 128], start=(k == 0), stop=(k == 3)), [dgn, rwn], [cpn])
                            A(lambda e: e.activation(out=xc[:, mb, cs_(c)], in_=cps[:, 0:128], func=AF.Silu, bias=convT[:, l, mb, 4:5]), [cpn, 'conv'], ['xc'])
                            if fake:
                                V(lambda e: e.tensor_copy(ctl[:, c, 0:2, mb], hst[:, c, mb, 1:3]), ['hst'], ['ctl'])
                                V(lambda e: e.tensor_copy(ctl[:, c, 2:3, mb], psv[:, c, 0:1]), [psn], ['ctl'])
                        if not fake:
                            V(lambda e: e.tensor_copy(histT[:, l, mb, :], psv[:, NCH - 1, 125:128]), [psn], ['hist%d' % l])
                            if sc == last_prompt_sc:
                                V(lambda e: e.tensor_copy(ctl[:, 0, 0:3, mb], psv[:, NCH - 1, 125:128]), [psn], ['ctl'])
                    proj_fm('xbc%d' % blk, hT, 'hT', 4, cbx, lag=2, banks=[0, 1, 4, 5])
                flush_pend()
                cout = []
                if fake:
                    cout = [(c, o_conv_s[l, chunks[c] - NPC]) for c in range(NCH)]
                elif sc == last_prompt_sc:
                    cout = [(0, o_conv_p[l])]
                for c, dst in cout:
                    PE(lambda e: e.transpose(PS[6][:, 0:128], ctl[:, c, :, :].rearrange("p k mb -> p (k mb)"), identF), ['ctl', 'cF'], ['ps6'])
                    A(lambda e: e.copy(out=cio[:], in_=PS[6][0:48, 0:128]), ['ps6'], ['cio'])
                    T.dma('cio_o', dst.rearrange("k (mb ch) -> (k mb) ch", ch=128), cio[:], reads=['cio'], writes=['o_conv'])
                def cbdt(c, ps, psn):
                    dq = dtw[:, c, :, :]
                    V(lambda e: e.tensor_tensor(out=dq[:, 0, :], in0=ps[:, 0:16], in1=rowsT[:, l, 0, :], op=OP.add), [psn, 'rows'], ['dtw'])
                    V(lambda e: e.tensor_scalar(out=dq[:, 7, :], in0=dq[:, 0, :], scalar1=-1.0, scalar2=None, op0=OP.mult), ['dtw'], ['dtw'])
                    V(lambda e: e.tensor_tensor(out=dq[:, 7, :], in0=dq[:, 7, :], in1=dq[:, 0, :], op=OP.max), ['dtw'], ['dtw'])
                    A(lambda e: e.activation(out=dq[:, 7, :], in_=dq[:, 7, :], func=AF.Exp, scale=-1.0), ['dtw'], ['dtw'])
                    A(lambda e: e.activation(out=dq[:, 7, :], in_=dq[:, 7, :], func=AF.Ln, bias=1.0), ['dtw'], ['dtw'])
                    V(lambda e: e.scalar_tensor_tensor(out=dq[:, 0, :], in0=dq[:, 0, :], scalar=0.0, in1=dq[:, 7, :], op0=OP.max, op1=OP.add), ['dtw'], ['dtw'])
                    if fake:
                        V(lambda e: e.tensor_scalar(out=dq[:, 0, :], in0=dq[:, 0, :], scalar1=cSt[:, 0:1], scalar2=None, op0=OP.mult), ['dtw', 'cS'], ['dtw'])
                proj_tm('dt', hT, 'hT', 16, cbdt)

                for c, gc in enumerate(chunks):
                    dq = dtw[:, c, :, :]
                    dt_, dA, acum, de, cd, ea, nac, tmp = [dq[:, i, :] for i in range(8)]
                    bc64 = lambda ap: ap.unsqueeze(2).to_broadcast([128, 16, 64])
                    if fake:
                        b = gc - NPC
                        T.dma('sio', sio[:], st_ssm[l, b].rearrange("(j q) n -> q j n", q=128), writes=['sio'])
                        for half in range(2):
                            for j in range(4):
                                PE(lambda e: e.transpose(PS[4 + half][:, j * 128:(j + 1) * 128], sio[:, half * 4 + j, :], identF), ['sio', 'cF'], [PSN[4 + half]])
                            A(lambda e: e.copy(out=Sst[:, l, half * 512:(half + 1) * 512], in_=PS[4 + half][:]), [PSN[4 + half]], [Sres])
                    V(lambda e: e.tensor_tensor(out=dA, in0=dt_, in1=Abc[:, l, :], op=OP.mult), ['dtw', 'Abc'], ['dtw'])
                    PE(lambda e: e.matmul(PS[2][:, 0:16], lhsT=triF, rhs=dA, start=True, stop=True), ['dtw', 'cF'], ['ps2a'])
                    PE(lambda e: e.matmul(PS[2][:, 16:32], lhsT=onesF, rhs=dA, start=True, stop=True), ['dtw', 'cF'], ['ps2a'])
                    A(lambda e: e.copy(out=acum, in_=PS[2][:, 0:16]), ['ps2a'], ['dtw'])
                    V(lambda e: e.tensor_tensor(out=tmp, in0=PS[2][:, 16:32], in1=acum, op=OP.subtract), ['ps2a', 'dtw'], ['dtw'])
                    A(lambda e: e.activation(out=de, in_=tmp, func=AF.Exp), ['dtw'], ['dtw'])
                    A(lambda e: e.activation(out=cd, in_=PS[2][:, 16:32], func=AF.Exp), ['ps2a'], ['dtw'])
                    A(lambda e: e.activation(out=ea, in_=acum, func=AF.Exp), ['dtw'], ['dtw'])
                    V(lambda e: e.tensor_scalar(out=nac, in0=acum, scalar1=-1.0, scalar2=None, op0=OP.mult), ['dtw'], ['dtw'])
                    for j in range(8):
                        PE(lambda e: e.transpose(PSB[:, j * 128:(j + 1) * 128], xc[:, j, cs_(c)], identB[:]), ['xc', 'identB'], ['psb_lo', 'psb_hi'])
                    pv = PSB[:].rearrange("p (h d) -> p h d", d=64)
                    V(lambda e: e.tensor_tensor(out=xdt[:], in0=pv, in1=bc64(dt_), op=OP.mult), ['psb_lo', 'psb_hi', 'dtw'], ['xdt'])
                    V(lambda e: e.tensor_tensor(out=xDd[:], in0=pv, in1=bc64(rowsT[:, l, 2, :]), op=OP.mult), ['psb_lo', 'psb_hi', 'rows'], ['xDd'])
                    V(lambda e: e.tensor_tensor(out=xw[:], in0=xdt[:], in1=bc64(de), op=OP.mult), ['xdt', 'dtw'], ['xw'])
                    for g in range(4):
                        PE(lambda e: e.transpose(PSB[:, g * 128:(g + 1) * 128], xc[:, 8 + g, cs_(c)], identB[:]), ['xc', 'identB'], ['psb_lo', 'psb_hi'])
                    A(lambda e: e.copy(out=btm[:], in_=PSB[:, 0:512]), ['psb_lo', 'psb_hi'], ['btm'])
                    for g in range(4):
                        PE(lambda e: e.matmul(PS[3][:, g * 128:(g + 1) * 128], lhsT=xc[:, 8 + g, cs_(c)], rhs=xc[:, 12 + g, cs_(c)], start=True, stop=True),
                           ['xc'], ['ps3'])
                    A(lambda e: e.copy(out=Sbf[:], in_=S_l), [Sres], ['Sbf'])
                    for g in range(4):
                        PE(lambda e: e.matmul(PS[g // 2][:, (g % 2) * 256:(g % 2) * 256 + 256], lhsT=xc[:, 12 + g, cs_(c)],
                                              rhs=Sbf[:, g * 256:(g + 1) * 256], start=True, stop=True), ['xc', 'Sbf'], [PSN[g // 2]])
                    for half in range(2):
                        yv = yt[:, half * 512:(half + 1) * 512].rearrange("p (h d) -> p h d", d=64)
                        V(lambda e: e.tensor_tensor(out=yv, in0=PS[half][:].rearrange("p (h d) -> p h d", d=64),
                                                    in1=ea[:, half * 8:half * 8 + 8].unsqueeze(2).to_broadcast([128, 8, 64]), op=OP.mult),
                          [PSN[half], 'dtw'], ['yt'])
                    def ssd_A(h):
                        pb = h % 2
                        LTp = PS[pb][:, 0:128]
                        ln = PSN[pb]
                        PE(lambda e: e.matmul(LTp, lhsT=dA[:, h:h + 1].to_broadcast([128, 128]), rhs=triF, start=True, stop=False), ['dtw', 'cF'], [ln])
                        PE(lambda e: e.matmul(LTp, lhsT=identF, rhs=negF, start=False, stop=True), ['cF'], [ln])
                        A(lambda e: e.activation(out=LT[:, pb, :], in_=LTp, func=AF.Exp, bias=nac[:, h:h + 1]), [ln, 'dtw'], ['LT%d' % pb])

                    def ssd_B(h):
                        g = h // 4
                        pb = h % 2
                        V(lambda e: e.tensor_tensor(out=MT[:, pb, :], in0=PS[3][:, g * 128:(g + 1) * 128], in1=LT[:, pb, :], op=OP.mult),
                          ['ps3', 'LT%d' % pb], ['MT%d' % pb])
                        PE(lambda e: e.matmul(PS[4 + h // 8][:, (h % 8) * 64:(h % 8) * 64 + 64], lhsT=MT[:, pb, :], rhs=xdt[:, h, :], start=True, stop=True),
                           ['MT%d' % pb, 'xdt'], [PSN[4 + h // 8]])

                    if fake:
                        for g in range(4):
                            V(lambda e: e.scalar_tensor_tensor(out=yt[0:1, g * 256:(g + 1) * 256], in0=xdt[0:1, 4 * g:4 * g + 4, :].rearrange("p h d -> p (h d)"),
                                                               scalar=PS[3][0:1, g * 128:g * 128 + 1], in1=yt[0:1, g * 256:(g + 1) * 256],
                                                               op0=OP.mult, op1=OP.add), ['xdt', 'ps3', 'yt'], ['yt'])
                    else:
                        for i in range(16 + 1):
                            if i < 16:
                                ssd_A(i)
                            if i >= 1:
                                ssd_B(i - 1)
                    if not fake:
                        for half in range(2):
                            V(lambda e: e.tensor_tensor(out=yt[:, half * 512:(half + 1) * 512], in0=PS[4 + half][:], in1=yt[:, half * 512:(half + 1) * 512], op=OP.add),
                              [PSN[4 + half], 'yt'], ['yt'])
                    V(lambda e: e.tensor_tensor(out=yt[:], in0=yt[:], in1=xDd[:].rearrange("p h d -> p (h d)"), op=OP.add), ['yt', 'xDd'], ['yt'])
                    for g in range(4):
                        PE(lambda e: e.matmul(PS[4 + g // 2][:, (g % 2) * 256:(g % 2) * 256 + 256], lhsT=btm[:, g * 128:(g + 1) * 128],
                                              rhs=xw[:, 4 * g:4 * g + 4, :].rearrange("p h d -> p (h d)"), start=True, stop=True), ['btm', 'xw'], [PSN[4 + g // 2]])
                    S3 = S_l.rearrange("p (h d) -> p h d", d=64)
                    V(lambda e: e.tensor_tensor(out=S3, in0=S3, in1=bc64(cd), op=OP.mult), [Sres, 'dtw'], [Sres])
                    for half in range(2):
                        V(lambda e: e.tensor_tensor(out=Sst[:, l, half * 512:(half + 1) * 512], in0=PS[4 + half][:], in1=Sst[:, l, half * 512:(half + 1) * 512], op=OP.add),
                          [PSN[4 + half], Sres], [Sres])
                    sdst = None
                    if fake:
                        sdst = o_ssm_s[l, gc - NPC]
                    elif sc == last_prompt_sc and c == NCH - 1:
                        sdst = o_ssm_p[l]
                    if sdst is not None:
                        for half in range(2):
                            for j in range(4):
                                jj = half * 4 + j
                                PE(lambda e: e.transpose(PS[half][:, j * 128:(j + 1) * 128], Sst[:, l, jj * 128:(jj + 1) * 128], identF), [Sres, 'cF'], [PSN[half]])
                            A(lambda e: e.copy(out=sio[:, half * 4:half * 4 + 4, :], in_=PS[half][:].rearrange("p (j n) -> p j n", n=128)), [PSN[half]], ['sio'])
                        T.dma('sio_o', sdst.rearrange("(j q) n -> q j n", q=128), sio[:], reads=['sio'], writes=['o_ssm'])
                    V(lambda e: e.tensor_tensor(out=yt[:], in0=yt[:], in1=zs[:, c, :], op=OP.mult), ['yt', 'zs'], ['yt'])
                    for g in range(4):
                        A(lambda e: e.activation(out=ynb[:, g * 256:(g + 1) * 256], in_=yt[:, g * 256:(g + 1) * 256], func=AF.Square, accum_out=ssq[:, g:g + 1]),
                          ['yt'], ['ynb', 'ssq'])
                    A(lambda e: e.activation(out=ssq[:, 4:8], in_=ssq[:, 0:4], func=AF.Sqrt, bias=EPS, scale=1.0 / 256.0), ['ssq'], ['ssq'])
                    V(lambda e: e.reciprocal(out=ssq[:, 4:8], in_=ssq[:, 4:8]), ['ssq'], ['ssq'])
                    V(lambda e: e.tensor_tensor(out=ynb[:].rearrange("p (g d) -> p g d", d=256), in0=yt[:].rearrange("p (g d) -> p g d", d=256),
                                                in1=ssq[:, 4:8].unsqueeze(2).to_broadcast([128, 4, 256]), op=OP.mult), ['yt', 'ssq', 'ynb'], ['ynb'])
                    for j in range(8):
                        PE(lambda e: e.transpose(PSB[:, j * 128:(j + 1) * 128], ynb[:, j * 128:(j + 1) * 128], identB[:]), ['ynb', 'identB'], ['psb_lo', 'psb_hi'])
                    V(lambda e: e.tensor_tensor(out=brf[:, :, cs_(c)], in0=PSB[:].rearrange("p (j t) -> p j t", t=128),
                                                in1=colsT[:, l, 16:24].unsqueeze(2).to_broadcast([128, 8, 128]), op=OP.mult), ['psb_lo', 'psb_hi', 'cols'], ['brf'])
                branch_out(['mp0', 'mp1'], brf, 'brf')
                dbg_dump(sc, l, 0, ybr)
                gate_stage(l, 0)
                T.barrier()

            if STOP <= 2:
                T.finish()
                return nc, T
            with ExitStack() as es:
                ufm = es.enter_context(nc.sbuf_tensor(uq("ufm"), [128, KT, N], BF16))
                Zt = es.enter_context(nc.sbuf_tensor(uq("Zt"), [128, 64, 33], F32))
                Gt = es.enter_context(nc.sbuf_tensor(uq("Gt"), [128, 64, 33], F32))
                tz = es.enter_context(nc.sbuf_tensor(uq("tz"), [128, 2, 2, 16, 32], BF16))
                Pp = es.enter_context(nc.sbuf_tensor(uq("Pp"), [128, 64, 32], BF16))
                Qp = es.enter_context(nc.sbuf_tensor(uq("Qp"), [128, 64, 32], BF16))
                ysb = es.enter_context(nc.sbuf_tensor(uq("ysb"), [32, D], F32))
                pre = es.enter_context(nc.sbuf_tensor(uq("pre"), [128, 2, 8, 32], F32))
                s5io = es.enter_context(nc.sbuf_tensor(uq("s5io"), [64, 128], F32))
                hcol2 = es.enter_context(nc.sbuf_tensor(uq("hcol"), [128, 128], F32))
                hcol = hcol2[:, 0:64]
                V(lambda e: e.memset(hcol2[:], 0.0), [], ['hcol'])
                for half in range(2):
                    def cbu(m, ps, psn, half=half):
                        A(lambda e: e.copy(out=ufm[:, half * 4 + m, :], in_=ps[:, 0:N]), [psn], ['ufm'])
                    proj_fm('u%d' % half, hT, 'hT', 4, cbu)
                cres = 's5c%d' % l
                NSUB = 128 // TS5
                U = NCH * NSUB

                def s5_aq(u, qd):
                    c, s = divmod(u, NSUB)
                    t0 = c * 128 + TS5 * s
                    hh, kq = ((0, 0), (1, 0), (0, 1), (1, 1))[qd]
                    tp = qd % 2
                    b1, b2 = 2 * hh, 2 * hh + 1
                    for i in range(16):
                        kt, e_ = 4 * kq + i // 4, i % 4
                        for o, bb in ((0, b1), (1, b2)):
                            PE(lambda e: e.matmul(PS[bb][:, i * 32:(i + 1) * 32], lhsT=Bw[64 * hh:64 * hh + 64, kt, e_, o, :],
                                                  rhs=ufm[64 * hh:64 * hh + 64, kt, t0:t0 + TS5], start=True, stop=True), ['pkB', 'ufm'], [PSN[bb]])
                    gsel = lambda tb: tb.rearrange("p (k h e) t -> p k h e t", h=2, e=4)[:, 4 * kq:4 * kq + 4, hh, :, :]
                    pv4 = lambda bb: PS[bb][:].rearrange("p (k e t) -> p k e t", e=4, t=32)
                    tz4 = lambda j: tz[:, tp, j, :, :].rearrange("p (k e) t -> p k e t", e=4)
                    V(lambda e: e.tensor_tensor(out=tz4(0), in0=pv4(b1), in1=gsel(WRt), op=OP.mult), [PSN[b1], 'pkB'], ['tz0%d' % tp])
                    V(lambda e: e.tensor_tensor(out=tz4(1), in0=pv4(b2), in1=gsel(WIt), op=OP.mult), [PSN[b2], 'pkB'], ['tz1%d' % tp])
                    V(lambda e: e.tensor_tensor(out=gsel(Zt[:])[:, :, :, 1:33], in0=tz4(0), in1=tz4(1), op=OP.add), ['tz0%d' % tp, 'tz1%d' % tp], ['Zt'])

                def s5_b(u):
                    c, s = divmod(u, NSUB)
                    gc = chunks[c]
                    if s == 0:
                        if fake:
                            b = gc - NPC
                            T.dma('s5io', s5io[:].rearrange("g (r n) -> g r n", r=2), st_s5[l, b].rearrange("r g n -> g r n"), writes=['s5io'])
                            PE(lambda e: e.transpose(PS[6][:, 448:512], s5io[:], identF[0:64, 0:64]), ['s5io', 'cF'], ['ps6t'])
                            A(lambda e: e.copy(out=hcol, in_=PS[6][:, 448:512]), ['ps6t'], ['hcol'])
                            rot(s5c[:, l, :], cres, hcol, 'hcol', 0, ROT)
                        elif sc == 0 and c == 0:
                            V(lambda e: e.memset(s5c[:, l, :], 0.0), [], [cres])
                    V(lambda e: e.tensor_copy(Zt[:, :, 0], s5c[:, l, :]), [cres], ['Zt'])
                    V(lambda e: e.tensor_tensor_scan(out=Gt[:].rearrange("p g t -> p (g t)"), data0=MULT, data1=Zt[:].rearrange("p g t -> p (g t)"),
                                                     initial=0.0, op0=OP.mult, op1=OP.add), ['Zt', 'pkF'], ['Gt'])
                    rot(s5c[:, l, :], cres, Gt[:, :, 32], 'Gt', 2, ROT)
                    sdst = None
                    if fake and s == 0:
                        sdst = o_s5_s[l, gc - NPC]
                        V(lambda e: e.tensor_copy(hcol, Gt[:, :, 1]), ['Gt'], ['hcol'])
                    elif (not fake) and sc == last_prompt_sc and c == NCH - 1 and s == NSUB - 1:
                        sdst = o_s5_p[l]
                        rot(hcol, 'hcol', Gt[:, :, 32], 'Gt', 1, ROT)
                    if sdst is not None:
                        PE(lambda e: e.transpose(PS[6][:, 320:448], hcol2[:], identF), ['hcol', 'cF'], ['ps6s'])
                        A(lambda e: e.copy(out=s5io[:], in_=PS[6][0:64, 320:448]), ['ps6s'], ['s5io'])
                        T.dma('s5io_o', sdst.rearrange("r g n -> g r n"), s5io[:].rearrange("g (r n) -> g r n", r=2), reads=['s5io'], writes=['o_s5'])

                def s5_pq(u):
                    V(lambda e: e.tensor_tensor(out=Pp[:], in0=Gt[:, :, 1:33], in1=CSt_, op=OP.mult), ['Gt', 'pkB'], ['Pp'])
                    V(lambda e: e.tensor_tensor(out=Qp[:], in0=Gt[:, :, 1:33], in1=SNt, op=OP.mult), ['Gt', 'pkB'], ['Qp'])

                def s5_d(u):
                    c, s = divmod(u, NSUB)
                    t0 = c * 128 + TS5 * s
                    for g in range(64):
                        bank, bn = PS[4 + g // 32], PSN[4 + g // 32]
                        col = (g % 32) * 16
                        PE(lambda e: e.matmul(bank[0:32, col:col + 16], lhsT=Pp[:, g, :], rhs=Cw[:, g, 0, :], start=True, stop=False), ['Pp', 'pkB'], [bn])
                        PE(lambda e: e.matmul(bank[0:32, col:col + 16], lhsT=Qp[:, g, :], rhs=Cw[:, g, 1, :], start=False, stop=True), ['Qp', 'pkB'], [bn])
                    for half in range(2):
                        A(lambda e: e.copy(out=ysb[:, half * 512:(half + 1) * 512], in_=PS[4 + half][0:32, :]), [PSN[4 + half]], ['ysb'])
                    for j in range(8):
                        PE(lambda e: e.transpose(PS[6][:, j * 32:(j + 1) * 32], ysb[:, j * 128:(j + 1) * 128], identF[0:32, 0:32]), ['ysb', 'cF'], ['ps6'])

                def s5_d2(u):
                    c, s = divmod(u, NSUB)
                    t0 = c * 128 + TS5 * s
                    p0 = pre[:, 0, :, :]
                    p1 = pre[:, 1, :, :]
                    V(lambda e: e.tensor_tensor(out=p0, in0=ufm[:, :, t0:t0 + TS5], in1=colsT[:, l, 24:32].unsqueeze(2).to_broadcast([128, 8, 32]), op=OP.mult),
                      ['ufm', 'cols'], ['pre0'])
                    V(lambda e: e.tensor_tensor(out=p0, in0=PS[6][:, 0:256].rearrange("p (j t) -> p j t", t=32), in1=p0, op=OP.add), ['ps6', 'pre0'], ['pre0'])
                    V(lambda e: e.tensor_tensor(out=p1, in0=p0, in1=p0, op=OP.mult), ['pre0'], ['pre1'])
                    V(lambda e: e.tensor_scalar(out=p1, in0=p1, scalar1=0.044715, scalar2=1.0, op0=OP.mult, op1=OP.add), ['pre1'], ['pre1'])
                    V(lambda e: e.tensor_tensor(out=p1, in0=p1, in1=p0, op=OP.mult), ['pre0', 'pre1'], ['pre1'])
                    A(lambda e: e.activation(out=p1, in_=p1, func=AF.Sigmoid, scale=1.5957691215), ['pre1'], ['pre1'])
                    V(lambda e: e.tensor_tensor(out=brf[:, :, t0:t0 + TS5], in0=p0, in1=p1, op=OP.mult), ['pre0', 'pre1'], ['brf'])

                ulist = [c * NSUB + s for c in range(NCH) for s in range(1 if fake else NSUB)]
                for qd in range(4):
                    s5_aq(ulist[0], qd)
                s5_b(ulist[0])
                for ui, u in enumerate(ulist):
                    nxt = ulist[ui + 1] if ui + 1 < len(ulist) else None
                    if nxt is not None:
                        s5_aq(nxt, 0)
                        s5_aq(nxt, 1)
                    s5_pq(u)
                    if nxt is not None:
                        s5_aq(nxt, 2)
                        s5_aq(nxt, 3)
                    s5_d(u)
                    if nxt is not None:
                        s5_b(nxt)
                    s5_d2(u)
                if DBG:
                    A(lambda e: e.copy(out=ybr[:], in_=brf[:]), ['brf'], ['ybr'])
                    dbg_dump(sc, l, 3, ybr)
                for b2 in range(2):
                    wv, wvres = w_next('gv%d' % b2)
                    wg, wgres = w_next('gg%d' % b2, ahead=2)
                    for m in range(4):
                        mb = b2 * 4 + m
                        for kt in range(KT):
                            PE(lambda e: e.matmul(PS[0][:, 0:N], lhsT=wv[:, kt, m * 128:(m + 1) * 128], rhs=brf[:, kt, :], start=(kt == 0), stop=(kt == KT - 1)),
                               [wvres, 'brf'], ['ps0'])
                        for kt in range(KT):
                            PE(lambda e: e.matmul(PS[1][:, 0:N], lhsT=wg[:, kt, m * 128:(m + 1) * 128], rhs=brf[:, kt, :], start=(kt == 0), stop=(kt == KT - 1)),
                               [wgres, 'brf'], ['ps1'])
                        g_ = gs[mb % 2]
                        gn = 'gs%d' % (mb % 2)
                        A(lambda e: e.activation(out=g_[:], in_=PS[1][:, 0:N], func=AF.Sigmoid), ['ps1'], [gn])
                        V(lambda e: e.tensor_tensor(out=ybr[:, mb, :], in0=PS[0][:, 0:N], in1=g_[:], op=OP.mult), ['ps0', gn], ['ybr'])
                dbg_dump(sc, l, 1, ybr)
                gate_stage(l, 1)
                T.barrier()

            if STOP <= 3:
                T.finish()
                return nc, T
            with ExitStack() as es:
                qtm = es.enter_context(nc.sbuf_tensor(uq("qtm"), [128, NCH, D], BF16))
                ktm = es.enter_context(nc.sbuf_tensor(uq("ktm"), [128, NCH, 256], F32))
                kdup = es.enter_context(nc.sbuf_tensor(uq("kdup"), [128, 4, 2, 64], BF16))
                vtmf = es.enter_context(nc.sbuf_tensor(uq("vtmf"), [128, NCH, 256], F32))
                qfm = es.enter_context(nc.sbuf_tensor(uq("qfm"), [128, 8, 128], BF16))
                smx = es.enter_context(nc.sbuf_tensor(uq("smx"), [128, 3, 258], F32))
                pbt = es.enter_context(nc.sbuf_tensor(uq("pbt"), [128, 3, 258], BF16))
                pT = es.enter_context(nc.sbuf_tensor(uq("pT"), [128, 3, 256], BF16))
                otm = es.enter_context(nc.sbuf_tensor(uq("otm"), [128, D], BF16))
                ast = es.enter_context(nc.sbuf_tensor(uq("ast"), [128, 3, 8], F32))
                rpt = es.enter_context(nc.sbuf_tensor(uq("rpt"), [128, 2, 8, 8], F32))
                ckt = es.enter_context(nc.sbuf_tensor(uq("ckt"), [128, 256], F32))

                def rope(psv, psn, dstv, dres, ci, nh):
                    cosb = cRt[:, ci, 0:8].unsqueeze(1).to_broadcast([128, nh, 8])
                    sinb = cRt[:, ci, 8:16].unsqueeze(1).to_broadcast([128, nh, 8])
                    ta = rpt[:, 0, 0:nh, :]
                    tb = rpt[:, 1, 0:nh, :]
                    V(lambda e: e.tensor_tensor(out=ta, in0=psv[:, :, 0:8], in1=cosb, op=OP.mult), [psn, 'cR'], ['rpt0'])
                    V(lambda e: e.tensor_tensor(out=tb, in0=psv[:, :, 8:16], in1=sinb, op=OP.mult), [psn, 'cR'], ['rpt1'])
                    V(lambda e: e.tensor_tensor(out=dstv[:, :, 0:8], in0=ta, in1=tb, op=OP.subtract), ['rpt0', 'rpt1'], [dres])
                    V(lambda e: e.tensor_tensor(out=ta, in0=psv[:, :, 8:16], in1=cosb, op=OP.mult), [psn, 'cR'], ['rpt0'])
                    V(lambda e: e.tensor_tensor(out=tb, in0=psv[:, :, 0:8], in1=sinb, op=OP.mult), [psn, 'cR'], ['rpt1'])
                    V(lambda e: e.tensor_tensor(out=dstv[:, :, 8:16], in0=ta, in1=tb, op=OP.add), ['rpt0', 'rpt1'], [dres])

                for half in range(2):
                    def cbq(c, ps, psn, half=half):
                        ci = NPC if fake else chunks[c]
                        psv = ps[:, 0:512].rearrange("p (h d) -> p h d", d=64)
                        dstv = qtm[:, c, half * 512:(half + 1) * 512].rearrange("p (h d) -> p h d", d=64)
                        A(lambda e: e.copy(out=dstv, in_=psv), [psn], ['qtm'])
                        rope(psv, psn, dstv, 'qtm', ci, 8)
                    proj_tm('q%d' % half, hT, 'hT', 512, cbq)

                def cbkv(c, ps, psn):
                    ci = NPC if fake else chunks[c]
                    psv = ps[:, 0:256].rearrange("p (h d) -> p h d", d=64)
                    dstv = ktm[:, c, :].rearrange("p (h d) -> p h d", d=64)
                    A(lambda e: e.copy(out=dstv, in_=psv), [psn], ['ktm'])
                    A(lambda e: e.copy(out=vtmf[:, c, :], in_=ps[:, 256:512]), [psn], ['vtmf'])
                    rope(psv, psn, dstv, 'ktm', ci, 4)
                proj_tm('kv', hT, 'hT', 512, cbkv)

                kres, vres = 'kcat%d' % l, 'vcat%d' % l
                for c, gc in enumerate(chunks):
                    first_chunk = (not fake) and gc == 0
                    k3 = lambda t2d: t2d.rearrange("p (h d) -> p h d", d=64)
                    if fake:
                        b = gc - NPC
                        T.dma('ckt', ckt[:], st_k[l, b], writes=['ckt'])
                        V(lambda e: e.tensor_copy(kdup[:, :, 0, :], k3(ckt[:])), ['ckt'], ['kdup'])
                        A(lambda e: e.copy(out=kdup[:, :, 1, :], in_=k3(ckt[:])), ['ckt'], ['kdup'])
                        for kh in range(4):
                            PE(lambda e: e.transpose(PSB[:, kh * 128:(kh + 1) * 128], kdup[:, kh, :, :].rearrange("p a d -> p (a d)"), identB[:]), ['kdup', 'identB'], ['psb_lo', 'psb_hi'])
                        A(lambda e: e.copy(out=kcat[:, l, :, 0:128], in_=PSB[:, 0:512].rearrange("p (h t) -> p h t", t=128)), ['psb_lo', 'psb_hi'], [kres])
                        T.dma('ckt', ckt[:], st_v[l, b], reads=['kdup'], writes=['ckt'])
                        V(lambda e: e.tensor_copy(vcat[:, l, 0, :], ckt[:]), ['ckt'], [vres])
                        T.dma('kv_d2dk', o_k_s[l, b, 0:127, :], st_k[l, b, 1:128, :], writes=['o_kv'])
                        T.dma('kv_d2dv', o_v_s[l, b, 0:127, :], st_v[l, b, 1:128, :], writes=['o_kv'])
                        T.dma('kv_rowk', o_k_s[l, b, 127:128, :], ktm[0:1, c, :], reads=['ktm'], writes=['o_kv'])
                        T.dma('kv_rowv', o_v_s[l, b, 127:128, :], vtmf[0:1, c, :], reads=['vtmf'], writes=['o_kv'])
                    elif first_chunk:
                        V(lambda e: e.memset(kcat[:, l, :, 0:128], 0.0), [], [kres])
                        GP(lambda e: e.memset(vcat[:, l, 0, :], 0.0), [], [vres])
                    if (not fake) and sc == last_prompt_sc and c == NCH - 1:
                        T.dma('kv_rowk', o_k_p[l], ktm[:, c, :], reads=['ktm'], writes=['o_kv'])
                        T.dma('kv_rowv', o_v_p[l], vtmf[:, c, :], reads=['vtmf'], writes=['o_kv'])
                    A(lambda e: e.copy(out=vcat[:, l, 1, :], in_=vtmf[:, c, :]), ['vtmf'], [vres])
                    V(lambda e: e.tensor_copy(kdup[:, :, 0, :], k3(ktm[:, c, :])), ['ktm'], ['kdup'])
                    A(lambda e: e.copy(out=kdup[:, :, 1, :], in_=k3(ktm[:, c, :])), ['ktm'], ['kdup'])
                    for kh in range(4):
                        PE(lambda e: e.transpose(PSB[:, kh * 128:(kh + 1) * 128], kdup[:, kh, :, :].rearrange("p a d -> p (a d)"), identB[:]), ['kdup', 'identB'], ['psb_lo', 'psb_hi'])
                    A(lambda e: e.copy(out=kcat[:, l, :, 128:256], in_=PSB[:, 0:512].rearrange("p (h t) -> p h t", t=128)), ['psb_lo', 'psb_hi'], [kres])
                    for j in range(8):
                        PE(lambda e: e.transpose(PSB[:, j * 128:(j + 1) * 128], qtm[:, c, j * 128:(j + 1) * 128], identB[:]), ['qtm', 'identB'], ['psb_lo', 'psb_hi'])
                    A(lambda e: e.copy(out=qfm[:], in_=PSB[:].rearrange("p (j t) -> p j t", t=128)), ['psb_lo', 'psb_hi'], ['qfm'])
                    mask = cMt[:, 0 if first_chunk else 1, :]
                    def att_S(h):
                        kh, hh, j, par = h // 4, h % 2, h // 2, h % 2
                        sps, spn = (PS[6], 'ps6') if par == 0 else (PS[3], 'ps3')
                        PE(lambda e: e.matmul(sps[:, 0:256], lhsT=qfm[64 * hh:64 * hh + 64, j, :], rhs=kcat[64 * hh:64 * hh + 64, l, kh, :], start=True, stop=True),
                           ['qfm', kres], [spn])

                    def att_A(h):
                        kh, hh, j, par, b3 = h // 4, h % 2, h // 2, h % 2, h % 3
                        sps, spn = (PS[6], 'ps6') if par == 0 else (PS[3], 'ps3')
                        st = ast[:, b3, :]
                        stn = 'ast%d' % b3
                        V(lambda e: e.scalar_tensor_tensor(out=smx[:, b3, 0:256], in0=sps[:, 0:256], scalar=HD ** -0.5, in1=mask, op0=OP.mult, op1=OP.add),
                          [spn, 'cM'], ['smx%d' % b3])
                        V(lambda e: e.tensor_copy(smx[:, b3, 256:257], rowsT[:, l, 3, h:h + 1]), ['rows', 'smx%d' % b3], ['smx%d' % b3])
                        V(lambda e: e.tensor_reduce(out=st[:, 1:2], in_=smx[:, b3, 0:257], axis=AX.X, op=OP.max, negate=True), ['smx%d' % b3], [stn])

                    def att_B(h):
                        par, b3 = h % 2, h % 3
                        st = ast[:, b3, :]
                        stn = 'ast%d' % b3
                        A(lambda e: e.activation(out=pbt[:, b3, 0:257], in_=smx[:, b3, 0:257], func=AF.Exp, bias=st[:, 1:2], accum_out=st[:, 2:3]),
                          ['smx%d' % b3, stn], ['pbt%d' % b3, stn])
                        V(lambda e: e.reciprocal(out=st[:, 4:5], in_=st[:, 2:3]), [stn], [stn])
                        ptp = PSB if par == 0 else PS2B
                        ptn = 'psb_lo' if par == 0 else 'ps2'
                        for kb in range(2):
                            PE(lambda e: e.transpose(ptp[:, kb * 128:(kb + 1) * 128], pbt[:, b3, kb * 128:(kb + 1) * 128], identB[:]),
                               ['pbt%d' % b3, 'identB'], [ptn])

                    def att_C(h):
                        kh, par, b3 = h // 4, h % 2, h % 3
                        st = ast[:, b3, :]
                        stn = 'ast%d' % b3
                        ptp = PSB if par == 0 else PS2B
                        ptn = 'psb_lo' if par == 0 else 'ps2'
                        V(lambda e: e.tensor_copy(pT[:, b3, :], ptp[:, 0:256]), [ptn], ['pT%d' % b3])
                        ob, obn = PS[4 + par], PSN[4 + par]
                        oc = (h // 2) * 64
                        PE(lambda e: e.matmul(ob[:, oc:oc + 64], lhsT=pT[:, b3, 0:128], rhs=vcat[:, l, 0, kh * 64:(kh + 1) * 64], start=True, stop=False),
                           ['pT%d' % b3, vres], [obn])
                        PE(lambda e: e.matmul(ob[:, oc:oc + 64], lhsT=pT[:, b3, 128:256], rhs=vcat[:, l, 1, kh * 64:(kh + 1) * 64], start=False, stop=True),
                           ['pT%d' % b3, vres], [obn])
                        A(lambda e: e.activation(out=otm[:, h * 64:(h + 1) * 64], in_=ob[:, oc:oc + 64], func=AF.Copy, scale=st[:, 4:5]), [obn, stn], ['otm'])

                    att_S(0)
                    for i in range(16 + 2):
                        if i + 1 < 16:
                            att_S(i + 1)
                        if i < 16:
                            att_A(i)
                        if 0 <= i - 1 < 16:
                            att_B(i - 1)
                        if 0 <= i - 2 < 16:
                            att_C(i - 2)
                    for j in range(8):
                        PE(lambda e: e.transpose(PSB[:, j * 128:(j + 1) * 128], otm[:, j * 128:(j + 1) * 128], identB[:]), ['otm', 'identB'], ['psb_lo', 'psb_hi'])
                    A(lambda e: e.copy(out=brf[:, :, cs_(c)], in_=PSB[:].rearrange("p (j t) -> p j t", t=128)), ['psb_lo', 'psb_hi'], ['brf'])
                    if not fake:
                        V(lambda e: e.tensor_copy(kcat[:, l, :, 0:128], kcat[:, l, :, 128:256]), [kres], [kres])
                        A(lambda e: e.copy(out=vcat[:, l, 0, :], in_=vcat[:, l, 1, :]), [vres], [vres])
                branch_out(['ao0', 'ao1'], brf, 'brf')
                dbg_dump(sc, l, 2, ybr)
                gate_stage(l, 2)
                T.barrier()

            if STOP <= 4:
                T.finish()
                return nc, T
            for half in range(2):
                def cbo(m, ps, psn, half=half):
                    mb = half * 4 + m
                    V(lambda e: e.tensor_tensor(out=xT[:, mb, :], in0=ps[:, 0:N], in1=xT[:, mb, :], op=OP.add), [psn, 'xT'], ['xT'])
                proj_fm('wo%d' % half, mbf, 'mbf', 4, cbo)
            rmsnorm(n2, hT, 'hT')
            with ExitStack() as es:
                actT = es.enter_context(nc.sbuf_tensor(uq("actT"), [128, 32, N], BF16))
                for i in range(8):
                    def cbup(m, ps, psn, i=i):
                        mb = i * 4 + m
                        g_ = gs[mb % 2]
                        gn = 'gs%d' % (mb % 2)
                        A(lambda e: e.activation(out=g_[:], in_=ps[:, 0:N], func=AF.Relu), [psn], [gn])
                        A(lambda e: e.activation(out=actT[:, mb, :], in_=g_[:], func=AF.Square), [gn], ['actT'])
                    proj_fm('up%d' % i, hT, 'hT', 4, cbup)
                for i in range(8):
                    wt, wres = w_next('dn%d' % i)
                    ps, psn = pm_next()
                    for kt in range(32):
                        PE(lambda e: e.matmul(ps[:, 0:N], lhsT=wt[:, kt, :], rhs=actT[:, kt, :], start=(kt == 0), stop=(kt == 31)), [wres, 'actT'], [psn])
                    V(lambda e: e.tensor_tensor(out=xT[:, i, :], in0=ps[:, 0:N], in1=xT[:, i, :], op=OP.add), [psn, 'xT'], ['xT'])
                T.barrier()

        rmsnorm(lambda kt: fnw[:, kt:kt + 1], ybr, 'ybr')
        with ExitStack() as es:
            yout = es.enter_context(nc.sbuf_tensor(uq("yout"), [128, D], F32))
            for c, gc in enumerate(chunks):
                for half in range(2):
                    for j in range(4):
                        PE(lambda e: e.transpose(PS[4 + half][:, j * 128:(j + 1) * 128], ybr[:, half * 4 + j, cs_(c)], identF), ['ybr', 'cF'], [PSN[4 + half]])
                    A(lambda e: e.copy(out=yout[:, half * 512:(half + 1) * 512], in_=PS[4 + half][:]), [PSN[4 + half]], ['yout'])
                if fake:
                    b = gc - NPC
                    T.dma('yout', o_y[cfg.SEQ + b:cfg.SEQ + b + 1, :], yout[0:1, :], reads=['yout'], writes=['o_y'])
                else:
                    T.dma('yout', o_y[gc * 128:(gc + 1) * 128, :], yout[:], reads=['yout'], writes=['o_y'])
            T.barrier()
    T.finish()
    return nc, T


def _consts(cfg):
    f32 = np.float32
    cF = np.zeros((128, 8, 128), f32)
    s = np.arange(128)[:, None]
    t = np.arange(128)[None, :]
    cF[:, 0] = (s == t)
    cF[:, 1] = (s <= t)
    cF[:, 2] = np.where(s > t, NEGBIG, 0.0)
    cF[:, 3] = 1.0
    perm = np.zeros((128, 128), f32)
    for m in range(64):
        perm[m + 64, m] = -1.0
        perm[m, m + 64] = 1.0
    cF[:, 4] = perm
    cM = np.zeros((128, 2, 256), f32)
    i = np.arange(128)[:, None]
    j = np.arange(128)[None, :]
    cM[:, 1, 0:128] = np.where(j >= i, 0.0, NEGBIG)
    cM[:, 1, 128:256] = np.where(j <= i, 0.0, NEGBIG)
    cM[:, 0, 0:128] = NEGBIG
    cM[:, 0, 128:256] = cM[:, 1, 128:256]
    half = 8
    inv_freq = np.exp(-(f32(2.0) * np.arange(half, dtype=f32) / f32(16)) * f32(math.log(500000.0))).astype(f32)
    cR = np.zeros((128, cfg.NPC + 1, 16), f32)
    for ci in range(cfg.NPC + 1):
        base = ci * 128 if ci < cfg.NPC else PAST_LEN
        pos = (base + np.arange(128)).astype(f32)
        ang = (pos[:, None] * inv_freq[None, :]).astype(f32)
        cR[:, ci, 0:8] = np.cos(ang)
        cR[:, ci, 8:16] = np.sin(ang)
    cS = np.zeros((128, 4), f32)
    cS[0, 0] = 1.0
    cS[:64, 1] = -1.0
    cS[64:, 1] = 1.0
    cS[:64, 2] = 1.0
    cS[64:, 2] = -1.0
    return dict(cF=cF, cM=cM, cR=cR, cS=cS)


def _shared_inputs(cfg, inp):
    f32 = np.float32
    L = cfg.DEPTH
    A = lambda k: np.asarray(inp[k], dtype=f32)
    fm = lambda w: np.ascontiguousarray(w.reshape(L, 8, 128).transpose(0, 2, 1))
    cols = np.concatenate([fm(A('norm1_w')), fm(A('norm2_w')), fm(A('m_norm_w')), fm(A('s5_d'))], axis=2)
    cw_ = A('conv_w').reshape(L, 4, 16, 128).transpose(0, 3, 2, 1)
    cb_ = A('conv_b').reshape(L, 16, 128).transpose(0, 2, 1)[..., None]
    convp = np.ascontiguousarray(np.concatenate([cw_, cb_], axis=3))
    rows = np.ascontiguousarray(np.stack([A('dt_bias'), A('a_log'), A('m_d'), A('attn_sinks')], axis=1))
    lr = A('s5_lam_re').transpose(0, 2, 1)
    li = A('s5_lam_im').transpose(0, 2, 1)
    lam = np.stack([np.concatenate([lr, lr], axis=1), np.concatenate([li, li], axis=1)], axis=2)
    bre = A('s5_b_re')
    bim = A('s5_b_im')
    bw = np.zeros((L, 128, 8, 4, 2, 128), f32)
    for g in range(64):
        kt, e = g // 8, g % 4
        r0 = (g % 8) * 16
        br = bre[:, g].transpose(0, 2, 1)
        bi = bim[:, g].transpose(0, 2, 1)
        bw[:, r0:r0 + 16, kt, e, 0, 0:64] = br
        bw[:, r0:r0 + 16, kt, e, 0, 64:128] = bi
        bw[:, r0:r0 + 16, kt, e, 1, 0:64] = bi
        bw[:, r0:r0 + 16, kt, e, 1, 64:128] = br
    cre = A('s5_c_re').transpose(0, 3, 1, 2)
    cim = A('s5_c_im').transpose(0, 3, 1, 2)
    cw = np.zeros((L, 128, 64, 2, 16), f32)
    cw[:, 0:64, :, 0, :] = cre
    cw[:, 64:128, :, 0, :] = cim
    cw[:, 0:64, :, 1, :] = cim
    cw[:, 64:128, :, 1, :] = cre
    d = dict(w_in=A('w_in'), m_proj=A('m_proj'), s5_glu_w=A('s5_glu_w'), attn_o=A('attn_o'), w_out=A('w_out'),
             mlp_up=A('mlp_up'), mlp_down=A('mlp_down'), cols=np.ascontiguousarray(cols), convp=convp, rows=rows,
             lam=np.ascontiguousarray(lam), lstep=A('s5_log_step'), bw=bw.reshape(L, 128, -1), cw=cw.reshape(L, 128, -1),
             fnw=np.ascontiguousarray(A('final_norm_w').reshape(8, 128).T))
    d.update(_consts(cfg))
    return d


def _core_inputs(cfg, inp, shared, core):
    f32 = np.float32
    L, SPC = cfg.DEPTH, cfg.SPC
    seq = core // max(1, cfg.NCORES // 2)
    b0, b1 = core * SPC, (core + 1) * SPC
    A = lambda k: np.asarray(inp[k], dtype=f32)
    d = dict(shared)
    d['xin'] = np.ascontiguousarray(np.concatenate([A('x_prompt')[seq], A('x_sample')[b0:b1, 0, :]], axis=0))
    d['st_ssm'] = np.ascontiguousarray(A('state_ssm')[:, b0:b1].reshape(L, SPC, D, NS))
    d['st_conv'] = np.ascontiguousarray(A('state_conv')[:, b0:b1])
    d['st_s5'] = np.ascontiguousarray(np.stack([A('state_s5_re')[:, b0:b1], A('state_s5_im')[:, b0:b1]], axis=2))
    d['st_k'] = np.ascontiguousarray(A('cache_k')[:, b0:b1].reshape(L, SPC, 128, 256))
    d['st_v'] = np.ascontiguousarray(A('cache_v')[:, b0:b1].reshape(L, SPC, 128, 256))
    return d


def _assemble(cfg, res):
    L, SPC, SEQ = cfg.DEPTH, cfg.SPC, cfg.SEQ
    cps = max(1, cfg.NCORES // 2)
    pc = [0, cps]
    cat_s = lambda k: np.concatenate([r[k] for r in res], axis=1)
    y_p = np.stack([res[c]['o_y'][0:SEQ] for c in pc], axis=0)
    y_s = np.concatenate([r['o_y'][SEQ:SEQ + SPC] for r in res], axis=0)[:, None, :]
    ssm_p = np.stack([res[c]['o_ssm_p'] for c in pc], axis=1).reshape(L, 2, MH, HP, NS)
    ssm_s = cat_s('o_ssm_s').reshape(L, -1, MH, HP, NS)
    conv_p = np.stack([res[c]['o_conv_p'] for c in pc], axis=1)
    conv_s = cat_s('o_conv_s')
    s5_p = np.stack([res[c]['o_s5_p'] for c in pc], axis=1)
    s5_s = cat_s('o_s5_s')
    k_p = np.stack([res[c]['o_k_p'] for c in pc], axis=1).reshape(L, 2, 128, KVH, HD)
    k_s = cat_s('o_k_s').reshape(L, -1, 128, KVH, HD)
    v_p = np.stack([res[c]['o_v_p'] for c in pc], axis=1).reshape(L, 2, 128, KVH, HD)
    v_s = cat_s('o_v_s').reshape(L, -1, 128, KVH, HD)
    outs = (y_p, y_s, ssm_p, ssm_s, conv_p, conv_s,
            s5_p[:, :, 0], s5_s[:, :, 0], s5_p[:, :, 1], s5_s[:, :, 1], k_p, k_s, v_p, v_s)
    return tuple(np.ascontiguousarray(o, dtype=np.float32) for o in outs)


def kernel(**inputs):
    cfg = Cfg()
    nc, _ = build(cfg)
    shared = _shared_inputs(cfg, inputs)
    in_maps = [_core_inputs(cfg, inputs, shared, c) for c in range(cfg.NCORES)]
    res = run_bass_kernel_spmd(nc, in_maps, core_ids=list(range(cfg.NCORES)))
    return _assemble(cfg, res.results)
```
